# Optimizing a Trainium2 kernel written in Bass

```python
import jax, jax.numpy as jnp
from jax import lax
import numpy as np

D_MODEL = 1024
BATCH = 8
SEQ = 2048
DEPTH = 1
DEC_BATCH = 32
DEC_SEQ = 1
PAST_LEN = 16384
PAGE_SIZE = 128

HEAD_DIM = 64
A_HEADS = 8
A_GROUPS = ((128, 1), (512, 4), (2048, 16))
A_WIDTH = A_HEADS * HEAD_DIM
B_HEADS = D_MODEL // HEAD_DIM
B_WIDTH = B_HEADS * HEAD_DIM
DECAY_LORA = 64
AAA_LORA = 64
GATE_LORA = 160
D_FF = -(-8 * D_MODEL // (3 * 256)) * 256
ROPE_THETA = 10000.0
NORM_EPS = 1e-6
GN_EPS = 64e-5
STREAM_BLOCK = 128
A_COLS = len(A_GROUPS) * 3 * A_WIDTH
SHIFT_COLS = 3 * B_WIDTH + DECAY_LORA + AAA_LORA + GATE_LORA
IN_COLS = A_COLS + SHIFT_COLS + 2 * D_MODEL

kernel_name = 'dilated_window_rwkv7_hybrid_step'


def split_cols(t, sizes):
    offs = [int(o) for o in np.cumsum(sizes)[:-1]]
    return jnp.split(t, offs, axis=-1)


def rms_norm(x, g):
    xf = x.astype(jnp.float32)
    y = xf * lax.rsqrt(jnp.mean(xf * xf, axis=-1, keepdims=True) + NORM_EPS)
    return (y * g.astype(jnp.float32)).astype(x.dtype)


def rope(x, pos):
    half = HEAD_DIM // 2
    freqs = ROPE_THETA ** (-jnp.arange(half, dtype=jnp.float32) / half)
    ang = pos.astype(jnp.float32)[:, None] * freqs[None, :]
    cos = jnp.cos(ang)[:, None, :]
    sin = jnp.sin(ang)[:, None, :]
    xf = x.astype(jnp.float32)
    x1, x2 = xf[..., :half], xf[..., half:]
    return jnp.concatenate([x1 * cos - x2 * sin, x1 * sin + x2 * cos], axis=-1).astype(x.dtype)


def masked_softmax(s, mask):
    s = jnp.where(mask, s, -jnp.inf)
    m = jnp.max(s, axis=-1, keepdims=True)
    p = jnp.exp(s - m)
    den = jnp.sum(p, axis=-1, keepdims=True)
    return p / den, (m + jnp.log(den))[..., 0]


def dilated_window_prompt(q, k, v, window, dilation):
    Bn, T = q.shape[0], q.shape[1]
    L = T // dilation
    nb = -(-L // STREAM_BLOCK)
    Lp = nb * STREAM_BLOCK
    span = window // dilation

    def streams(t):
        t = t.reshape(Bn, L, dilation, A_HEADS, HEAD_DIM).transpose(0, 2, 1, 3, 4)
        return jnp.pad(t, ((0, 0), (0, 0), (0, Lp - L), (0, 0), (0, 0)))

    def band(t):
        tb = jnp.pad(t, ((0, 0), (0, 0), (STREAM_BLOCK, 0), (0, 0), (0, 0)))
        tb = tb.reshape(Bn, dilation, nb + 1, STREAM_BLOCK, A_HEADS, HEAD_DIM)
        return jnp.concatenate([tb[:, :, :-1], tb[:, :, 1:]], axis=3)

    qb = streams(q).reshape(Bn, dilation, nb, STREAM_BLOCK, A_HEADS, HEAD_DIM)
    kb, vb = band(streams(k)), band(streams(v))
    qi = jnp.arange(STREAM_BLOCK)[:, None]
    kj = jnp.arange(2 * STREAM_BLOCK)[None, :]
    dist = qi + STREAM_BLOCK - kj
    blk = jnp.arange(nb)[:, None, None]
    valid = (dist >= 0) & (dist <= span) & ((blk > 0) | (kj >= STREAM_BLOCK))
    s = jnp.einsum('bgnqhd,bgnkhd->bgnhqk', qb.astype(jnp.float32), kb.astype(jnp.float32)) * HEAD_DIM ** -0.5
    p, lse = masked_softmax(s, valid[:, None])
    o = jnp.einsum('bgnhqk,bgnkhd->bgnqhd', p, vb.astype(jnp.float32))
    o = o.reshape(Bn, dilation, Lp, A_HEADS, HEAD_DIM)[:, :, :L]
    o = o.transpose(0, 2, 1, 3, 4).reshape(Bn, T, A_HEADS, HEAD_DIM)
    lse = lse.transpose(0, 1, 2, 4, 3).reshape(Bn, dilation, Lp, A_HEADS)[:, :, :L]
    lse = lse.transpose(0, 2, 1, 3).reshape(Bn, T, A_HEADS)
    return o, lse


def dilated_window_sample(q, k_all, v_all, window, dilation, buf_len):
    T = q.shape[1]
    span = window // dilation
    idx = buf_len + jnp.arange(T)[:, None] - dilation * jnp.arange(span + 1)[None, :]
    valid = idx >= 0
    idx_c = jnp.maximum(idx, 0)
    kg = k_all[:, idx_c].astype(jnp.float32)
    vg = v_all[:, idx_c].astype(jnp.float32)
    s = jnp.einsum('bqhd,bqmhd->bhqm', q.astype(jnp.float32), kg) * HEAD_DIM ** -0.5
    p, lse = masked_softmax(s, valid[None, None])
    o = jnp.einsum('bhqm,bqmhd->bqhd', p, vg)
    return o, lse.transpose(0, 2, 1)


def rwkv7_time_mix(p, prev, wkv0, lw):
    Bn, T = p.shape[0], p.shape[1]
    pf = p.astype(jnp.float32)
    shifted = jnp.concatenate([prev.astype(jnp.float32)[:, None], pf[:, :-1]], axis=1)
    xm = pf + (shifted - pf) * lw['mu_shift']
    r, k, v, xw, xa, xg = split_cols(xm, [B_WIDTH] * 3 + [DECAY_LORA, AAA_LORA, GATE_LORA])
    log_w = -jax.nn.softplus(-(lw['w0'] + jnp.tanh(xw) @ lw['w2'])) - 0.5
    decay = jnp.exp(-jnp.exp(log_w))
    a = jax.nn.sigmoid(lw['a0'] + xa @ lw['a2'])
    gate = jax.nn.sigmoid(xg) @ lw['g2']

    def heads(t):
        return t.reshape(Bn, T, B_HEADS, HEAD_DIM)

    kk = heads(k * lw['k_k'])
    kk = kk / jnp.maximum(jnp.sqrt(jnp.sum(kk * kk, axis=-1, keepdims=True)), 1e-12)
    k = k * (1.0 + (a - 1.0) * lw['k_a'])
    rh, wh, kh, vh, ah = heads(r), heads(decay), heads(k), heads(v), heads(a)

    def step(S, inp):
        r_t, w_t, k_t, v_t, kk_t, a_t = inp
        sa = jnp.einsum('bhij,bhj->bhi', S, -kk_t)
        S = (S * w_t[:, :, None, :] + sa[..., None] * (kk_t * a_t)[:, :, None, :]
             + v_t[..., None] * k_t[:, :, None, :])
        return S, jnp.einsum('bhij,bhj->bhi', S, r_t)

    xs = tuple(jnp.moveaxis(t, 1, 0) for t in (rh, wh, kh, vh, kk, ah))
    S_last, ys = lax.scan(step, wkv0.astype(jnp.float32), xs)
    y = jnp.moveaxis(ys, 0, 1)
    mean = jnp.mean(y, axis=-1, keepdims=True)
    var = jnp.mean(jnp.square(y - mean), axis=-1, keepdims=True)
    yn = ((y - mean) * lax.rsqrt(var + GN_EPS)).reshape(Bn, T, B_WIDTH) * lw['gn_w'] + lw['gn_b']
    bonus = jnp.sum(rh * kh * lw['r_k'], axis=-1, keepdims=True) * vh
    out = (yn + bonus.reshape(Bn, T, B_WIDTH)) * gate
    return out, p[:, -1], S_last


def hybrid_mixer(h, pos, kv_bufs, shift_prev, wkv_prev, lw):
    Bn, T = h.shape[0], h.shape[1]
    proj = jnp.einsum('btd,dc->btc', h, lw['w_in'])
    a_cols, s_cols, gate_a, gate_b = split_cols(proj, [A_COLS, SHIFT_COLS, D_MODEL, D_MODEL])
    a_parts = split_cols(a_cols, [A_WIDTH] * (3 * len(A_GROUPS)))
    outs, lses, new_kv = [], [], []
    for g, (window, dilation) in enumerate(A_GROUPS):
        q, k, v = (t.reshape(Bn, T, A_HEADS, HEAD_DIM) for t in a_parts[3 * g:3 * g + 3])
        q, k = rope(q, pos), rope(k, pos)
        if kv_bufs is None:
            o, lse = dilated_window_prompt(q, k, v, window, dilation)
            keep = min(window, T)
            new_kv.append(jnp.stack([k[:, T - keep:], v[:, T - keep:]], axis=2))
        else:
            buf = kv_bufs[g]
            k_all = jnp.concatenate([buf[:, :, 0], k], axis=1)
            v_all = jnp.concatenate([buf[:, :, 1], v], axis=1)
            o, lse = dilated_window_sample(q, k_all, v_all, window, dilation, buf.shape[1])
            new_kv.append(jnp.stack([k, v], axis=2))
        outs.append(o)
        lses.append(lse)
    alpha = jax.nn.softmax(jnp.stack(lses, axis=0), axis=0)[..., None]
    o_a = jnp.sum(alpha * jnp.stack(outs, axis=0), axis=0).reshape(Bn, T, A_WIDTH)
    o_b, shift_last, wkv_last = rwkv7_time_mix(s_cols, shift_prev, wkv_prev, lw)
    y_a = o_a @ lw['w_proj_a']
    y_b = o_b @ lw['w_proj_b']
    merged = jax.nn.sigmoid(gate_a.astype(jnp.float32)) * y_a + jax.nn.sigmoid(gate_b.astype(jnp.float32)) * y_b
    out = (merged @ lw['w_out']).astype(h.dtype)
    return out, new_kv, shift_last, wkv_last


def trunk_layer(x, pos, kv_bufs, shift_prev, wkv_prev, lw):
    mix, new_kv, shift_last, wkv_last = hybrid_mixer(rms_norm(x, lw['norm_mix']), pos, kv_bufs, shift_prev, wkv_prev, lw)
    x = x + mix
    hn = rms_norm(x, lw['norm_ffn'])
    ffn = (jax.nn.silu(hn @ lw['w_gate']) * (hn @ lw['w_up'])) @ lw['w_down']
    return x + ffn, new_kv, shift_last, wkv_last


def setup_inputs(seed: int = 0) -> dict:
    key = jax.random.key(seed)
    ks = jax.random.split(key, 32)
    nrm = jax.random.normal
    buf_lens = [min(w, PAST_LEN) for (w, _) in A_GROUPS]
    return {
        'x_prompt': nrm(ks[0], (BATCH, SEQ, D_MODEL), jnp.float32),
        'x_sample': nrm(ks[1], (DEC_BATCH, DEC_SEQ, D_MODEL), jnp.float32),
        'cache_kv_g1': nrm(ks[2], (DEPTH, DEC_BATCH, buf_lens[0], 2, A_HEADS, HEAD_DIM), jnp.float32),
        'cache_kv_g2': nrm(ks[3], (DEPTH, DEC_BATCH, buf_lens[1], 2, A_HEADS, HEAD_DIM), jnp.float32),
        'cache_kv_g3': nrm(ks[4], (DEPTH, DEC_BATCH, buf_lens[2], 2, A_HEADS, HEAD_DIM), jnp.float32),
        'state_shift': nrm(ks[5], (DEPTH, DEC_BATCH, SHIFT_COLS), jnp.float32),
        'state_wkv': 0.3 * nrm(ks[6], (DEPTH, DEC_BATCH, B_HEADS, HEAD_DIM, HEAD_DIM), jnp.float32),
        'norm_mix': 1.0 + 0.05 * nrm(ks[7], (DEPTH, D_MODEL), jnp.float32),
        'w_in': nrm(ks[8], (DEPTH, D_MODEL, IN_COLS), jnp.float32) * D_MODEL ** -0.5,
        'mu_shift': jax.random.uniform(ks[9], (DEPTH, SHIFT_COLS), jnp.float32),
        'w0': -1.0 + 0.5 * nrm(ks[10], (DEPTH, B_WIDTH), jnp.float32),
        'w2': 0.5 * nrm(ks[11], (DEPTH, DECAY_LORA, B_WIDTH), jnp.float32) * DECAY_LORA ** -0.5,
        'a0': 0.1 * nrm(ks[12], (DEPTH, B_WIDTH), jnp.float32),
        'a2': 0.5 * nrm(ks[13], (DEPTH, AAA_LORA, B_WIDTH), jnp.float32) * AAA_LORA ** -0.5,
        'g2': nrm(ks[14], (DEPTH, GATE_LORA, B_WIDTH), jnp.float32) * GATE_LORA ** -0.5,
        'k_k': 0.85 + 0.05 * nrm(ks[15], (DEPTH, B_WIDTH), jnp.float32),
        'k_a': 1.0 + 0.05 * nrm(ks[16], (DEPTH, B_WIDTH), jnp.float32),
        'r_k': 0.1 * nrm(ks[17], (DEPTH, B_HEADS, HEAD_DIM), jnp.float32),
        'gn_w': 1.0 + 0.05 * nrm(ks[18], (DEPTH, B_WIDTH), jnp.float32),
        'gn_b': 0.02 * nrm(ks[19], (DEPTH, B_WIDTH), jnp.float32),
        'w_proj_a': nrm(ks[20], (DEPTH, A_WIDTH, D_MODEL), jnp.float32) * A_WIDTH ** -0.5,
        'w_proj_b': nrm(ks[21], (DEPTH, B_WIDTH, D_MODEL), jnp.float32) * B_WIDTH ** -0.5,
        'w_out': nrm(ks[22], (DEPTH, D_MODEL, D_MODEL), jnp.float32) * D_MODEL ** -0.5,
        'norm_ffn': 1.0 + 0.05 * nrm(ks[23], (DEPTH, D_MODEL), jnp.float32),
        'w_gate': nrm(ks[24], (DEPTH, D_MODEL, D_FF), jnp.float32) * D_MODEL ** -0.5,
        'w_up': nrm(ks[25], (DEPTH, D_MODEL, D_FF), jnp.float32) * D_MODEL ** -0.5,
        'w_down': nrm(ks[26], (DEPTH, D_FF, D_MODEL), jnp.float32) * D_FF ** -0.5,
        'norm_final': 1.0 + 0.05 * nrm(ks[27], (D_MODEL,), jnp.float32),
    }


def reference(x_prompt, x_sample, cache_kv_g1, cache_kv_g2, cache_kv_g3, state_shift, state_wkv,
              norm_mix, w_in, mu_shift, w0, w2, a0, a2, g2, k_k, k_a, r_k, gn_w, gn_b,
              w_proj_a, w_proj_b, w_out, norm_ffn, w_gate, w_up, w_down, norm_final):
    Bp, Tp = x_prompt.shape[0], x_prompt.shape[1]
    pos_p = jnp.arange(Tp, dtype=jnp.int32)
    pos_s = PAST_LEN + jnp.arange(x_sample.shape[1], dtype=jnp.int32)
    hp, hs = x_prompt, x_sample
    kv_p, kv_s = ([], [], []), ([], [], [])
    sh_p, sh_s, wkv_p, wkv_s = [], [], [], []
    for l in range(DEPTH):
        lw = {'norm_mix': norm_mix[l], 'w_in': w_in[l], 'mu_shift': mu_shift[l], 'w0': w0[l], 'w2': w2[l],
              'a0': a0[l], 'a2': a2[l], 'g2': g2[l], 'k_k': k_k[l], 'k_a': k_a[l], 'r_k': r_k[l],
              'gn_w': gn_w[l], 'gn_b': gn_b[l], 'w_proj_a': w_proj_a[l], 'w_proj_b': w_proj_b[l],
              'w_out': w_out[l], 'norm_ffn': norm_ffn[l], 'w_gate': w_gate[l], 'w_up': w_up[l],
              'w_down': w_down[l]}
        zero_shift = jnp.zeros((Bp, SHIFT_COLS), x_prompt.dtype)
        zero_wkv = jnp.zeros((Bp, B_HEADS, HEAD_DIM, HEAD_DIM), jnp.float32)
        hp, nkv, nsh, nwkv = trunk_layer(hp, pos_p, None, zero_shift, zero_wkv, lw)
        for g in range(3):
            kv_p[g].append(nkv[g])
        sh_p.append(nsh)
        wkv_p.append(nwkv)
        bufs = (cache_kv_g1[l], cache_kv_g2[l], cache_kv_g3[l])
        hs, nkv, nsh, nwkv = trunk_layer(hs, pos_s, bufs, state_shift[l], state_wkv[l], lw)
        for g in range(3):
            kv_s[g].append(nkv[g])
        sh_s.append(nsh)
        wkv_s.append(nwkv)
    y_prompt = rms_norm(hp, norm_final)
    y_sample = rms_norm(hs, norm_final)
    new_kv_g1_prompt = jnp.stack(kv_p[0], axis=0)
    new_kv_g2_prompt = jnp.stack(kv_p[1], axis=0)
    new_kv_g3_prompt = jnp.stack(kv_p[2], axis=0)
    new_shift_prompt = jnp.stack(sh_p, axis=0)
    new_wkv_prompt = jnp.stack(wkv_p, axis=0)
    new_kv_g1_sample = jnp.stack(kv_s[0], axis=0)
    new_kv_g2_sample = jnp.stack(kv_s[1], axis=0)
    new_kv_g3_sample = jnp.stack(kv_s[2], axis=0)
    new_shift_sample = jnp.stack(sh_s, axis=0)
    new_wkv_sample = jnp.stack(wkv_s, axis=0)
    return (y_prompt, y_sample, new_kv_g1_prompt, new_kv_g2_prompt, new_kv_g3_prompt, new_shift_prompt, new_wkv_prompt,
            new_kv_g1_sample, new_kv_g2_sample, new_kv_g3_sample, new_shift_sample, new_wkv_sample)
```

```python
import numpy as np
from contextlib import ExitStack
import concourse.bass as bass
import concourse.mybir as mybir
from concourse.bass_utils import run_bass_kernel_spmd

F32 = mybir.dt.float32
BF16 = mybir.dt.bfloat16
ALU = mybir.AluOpType
AF = mybir.ActivationFunctionType
AX = mybir.AxisListType

import os
ENGS = ("tensor", "vector", "scalar", "gpsimd", "sync")
DLIM = int(os.environ.get("DLIM", "9"))
DNOX = int(os.environ.get("DNOX", "0"))
DNB = int(os.environ.get("DNB", "16"))
ELIM = int(os.environ.get("ELIM", "9"))
PH_STOP = os.environ.get("PH_STOP", "")
E1LIM = int(os.environ.get("E1LIM", "9"))
EPOCH = 12000

D = 1024
T = 2048
NB = 16
NS = 4
A_COLS = 4608
SH = 3360
DFF = 2816
GROUPS = ((128, 1), (512, 4), (2048, 16))
C0 = 2
HTW = C0 + T + 128
SC0 = C0 + T


class Buf:
    __slots__ = ("w", "r", "name")

    def __init__(self, name=""):
        self.w = None
        self.r = {}
        self.name = name


def _hkey(h):
    if h[0] == "dma":
        return ("dma", h[3]), h[2]
    return (h[0], h[1]), h[2]


class Sched:
    def __init__(self, nc, stack):
        self.nc = nc
        self.stack = stack
        self.q = {e: [] for e in ENGS}
        self.cnt = {e: 0 for e in ENGS}
        self.sems = {e: [] for e in ENGS}
        self.waited = {e: {} for e in ENGS}
        self.pools = {}
        self.outstanding_dma = []
        self.ninst = {e: 0 for e in ENGS}

    def _sem(self, eng, epoch):
        while len(self.sems[eng]) <= epoch:
            s = self.stack.enter_context(self.nc.semaphore(f"s_{eng}_{len(self.sems[eng])}"))
            self.sems[eng].append(s)
        return self.sems[eng][epoch]

    def _emit_waits(self, eng, deps):
        w = self.waited[eng]
        for d in deps:
            if d is None:
                continue
            key, n = _hkey(d)
            if w.get(key, 0) >= n:
                continue
            w[key] = n
            if d[0] == "dma":
                sem = d[1]
            else:
                sem = self._sem(d[0], d[1])
            self.q[eng].append(lambda e, sem=sem, n=n: e.wait_ge(sem, n))

    def op(self, eng, fn, deps=()):
        self._emit_waits(eng, deps)
        c = self.cnt[eng]
        epoch = c // EPOCH
        n = c % EPOCH + 1
        self.cnt[eng] = c + 1
        sem = self._sem(eng, epoch)
        self.q[eng].append(lambda e, fn=fn, sem=sem: fn(e).then_inc(sem, 1))
        self.ninst[eng] += 1
        return (eng, epoch, n)

    def op_noinc(self, eng, fn):
        self.q[eng].append(lambda e, fn=fn: fn(e))
        self.ninst[eng] += 1

    def dma_raw(self, eng, out, in_, deps, **kw):
        if eng not in self.pools:
            k = 4 if eng == "gpsimd" else 10
            self.pools[eng] = {"sems": [self.stack.enter_context(self.nc.semaphore(f"d_{eng}_{i}")) for i in range(k)],
                               "vals": [0] * k, "last": [None] * k, "i": 0}
        p = self.pools[eng]
        i = p["i"]
        p["i"] = (i + 1) % len(p["sems"])
        deps = list(deps) + [p["last"][i]]
        self._emit_waits(eng, deps)
        p["vals"][i] += 16
        sem, val = p["sems"][i], p["vals"][i]
        self.q[eng].append(lambda e, out=out, in_=in_, sem=sem, kw=kw: e.dma_start(out=out, in_=in_, **kw).then_inc(sem, 16))
        h = ("dma", sem, val, (eng, i))
        p["last"][i] = h
        self.outstanding_dma.append(h)
        self.ninst[eng] += 1
        return h

    @staticmethod
    def _deps(reads, writes):
        deps = []
        for b in reads:
            deps.append(b.w)
        for b in writes:
            deps.append(b.w)
            deps.extend(b.r.values())
        return deps

    @staticmethod
    def _update(h, reads, writes):
        key, n = _hkey(h)
        for b in reads:
            old = b.r.get(key)
            if old is None or _hkey(old)[1] < n:
                b.r[key] = h
        for b in writes:
            b.w = h
            b.r = {}

    def do(self, eng, fn, reads=(), writes=()):
        h = self.op(eng, fn, self._deps(reads, writes))
        self._update(h, reads, writes)
        return h

    def mm(self, specs, reads=(), writes=()):
        self._emit_waits("tensor", [d_ for d_ in self._deps(reads, writes) if d_ is not None and d_[0] != "tensor"])

        def mk(sp):
            if isinstance(sp, tuple):
                _, o, i, idn = sp
                return lambda e: e.transpose(o, i, idn)
            return lambda e: e.matmul(sp["out"], lhsT=sp["lhsT"], rhs=sp["rhs"], start=sp.get("start", True),
                                      stop=sp.get("stop", True), skip_group_check=True)
        for sp in specs[:-1]:
            self.op_noinc("tensor", mk(sp))
        h = self.op("tensor", mk(specs[-1]))
        self._update(h, reads, writes)
        return h

    def dma(self, eng, out, in_, reads=(), writes=(), **kw):
        h = self.dma_raw(eng, out, in_, self._deps(reads, writes), **kw)
        self._update(h, reads, writes)
        return h

    def barrier(self):
        hs = []
        for e in ENGS:
            c = self.cnt[e]
            if c:
                hs.append((e, (c - 1) // EPOCH, (c - 1) % EPOCH + 1))
        hs += self.outstanding_dma
        self.outstanding_dma = []
        for e in ENGS:
            self._emit_waits(e, hs)

    def final_wait(self):
        self._emit_waits("sync", self.outstanding_dma)

    def emit(self):
        with self.nc.Block() as block:
            for name in ENGS:
                lst = self.q[name]
                if not lst:
                    continue

                def body(e, lst=lst):
                    for f in lst:
                        f(e)
                getattr(block, name)(body)


def make_consts():
    c = {}
    c["ident"] = np.eye(128, dtype=np.float32)
    k = np.arange(128)[:, None]
    q = np.arange(128)[None, :]
    mP = (k >= q).astype(np.float32)
    mC = (k <= q).astype(np.float32)
    c["amask"] = np.stack([mP, mC, mP, mC], 1).copy()
    half = 32
    freqs = (10000.0 ** (-np.arange(half, dtype=np.float32) / half)).astype(np.float32)

    def tab(pos):
        ang = pos.astype(np.float32)[..., None] * freqs
        return np.cos(ang).astype(np.float32), np.sin(ang).astype(np.float32)
    rc = np.zeros((3, 128, NB, 32), np.float32)
    rs = np.zeros((3, 128, NB, 2, 32), np.float32)
    for g, (_, d) in enumerate(GROUPS):
        nb = NB // d
        for r in range(d):
            for n in range(nb):
                sb = r * nb + n
                pos = r + d * (128 * n + np.arange(128))
                co, si = tab(pos)
                rc[g, :, sb] = co
                rs[g, :, sb, 0] = -si
                rs[g, :, sb, 1] = si
    c["rope_c"] = rc
    c["rope_s"] = rs
    co, si = tab(np.array([16384]))
    c["rope_cs"] = np.concatenate([co[0], -si[0], si[0]])[None, :].copy()
    ep = np.zeros((128, 4, 4), np.float32)
    for p in range(4):
        ep[:, p, p] = 1.0
    c["epair"] = ep
    selb = np.zeros((NS, NS, 128), np.float32)
    selp = np.zeros((4, 4, 64), np.float32)
    for b in range(4):
        selb[b, b, :] = 1.0
        selp[b, b, :] = 1.0
    c["selb"] = selb
    c["selp"] = selp
    ss_, tt_ = np.arange(128)[:, None], np.arange(128)[None, :]
    su = (ss_ < tt_).astype(np.float32)
    iu = (ss_ <= tt_).astype(np.float32)
    c["rmask"] = np.stack([su, iu, su, iu], 1).copy()
    c["bones"] = ((ss_ // 64) == (tt_ // 64)).astype(np.float32)
    c["bo2"] = ((np.arange(128)[:, None] // 64) == np.arange(2)[None, :]).astype(np.float32)
    c["i64x2"] = np.concatenate([np.eye(64, dtype=np.float32)] * 2, 0)
    return c


def build(debug=()):
    nc = bass.Bass("TRN2", target_bir_lowering=False)

    def din(name, shape, dt=F32):
        return nc.dram_tensor(name, list(shape), dt, kind="ExternalInput").ap()

    def dout(name, shape, dt=F32):
        return nc.dram_tensor(name, list(shape), dt, kind="ExternalOutput").ap()

    x_d = din("x", [T, D])
    xs_d = din("xs", [NS, D])
    w_in_d = din("w_in", [D, 10016])
    norm_mix_d = din("norm_mix", [1, D])
    ident_d = din("ident", [128, 128])
    amask_d = din("amask", [128, 4, 128])
    rope_c_d = din("rope_c", [3, 128, NB, 32])
    rope_s_d = din("rope_s", [3, 128, NB, 2, 32])
    rope_cs_d = din("rope_cs", [1, 96])
    epair_d = din("epair", [128, 4, 4])
    selb_d = din("selb", [NS, NS, 128])
    selp_d = din("selp", [4, 4, 64])
    ck_d = [din(f"ck{g + 1}", [NS, min(w_, 16384), 1024]) for g, (w_, _) in enumerate(GROUPS)]

    w_pa_d = din("w_proj_a", [512, D])
    w_pb_d = din("w_proj_b", [D, D])
    w_out_d = din("w_out", [D, D])
    norm_ffn_d = din("norm_ffn", [1, D])
    w_gate_d = din("w_gate", [D, DFF])
    w_up_d = din("w_up", [D, DFF])
    w_down_d = din("w_down", [DFF, D])
    norm_final_d = din("norm_final", [1, D])
    y_d = dout("y", [T, D])
    ys_d = dout("ys", [NS, D])
    x1_d = nc.dram_tensor("x1_scr", [T + 128, D], F32, kind="Internal").ap()
    sshift_d = din("sshift", [NS, SH])
    swkv_d = din("swkv", [NS, 16, 64, 64])
    mu_c_d = din("mu_c", [128, 27])
    cols8_d = din("cols8", [128, 5, 8])
    w2a2_d = din("w2a2", [128, 1024])
    g2_d = din("g2", [160, 1024])
    gn_w_d = din("gn_w", [1, 1024])
    gn_b_d = din("gn_b", [1, 1024])
    rmask_d = din("rmask", [128, 4, 128])
    bones_d = din("bones", [128, 128])
    bo2_d = din("bo2", [128, 2])
    i64x2_d = din("i64x2", [128, 64])
    shift_p_d = dout("shift_p", [1, SH])
    wkv_p_d = dout("wkv_p", [16, 64, 64])
    shift_s_d = dout("shift_s", [NS, SH])
    wkv_s_d = dout("wkv_s", [NS, 16, 64, 64])
    kvp_d = [dout("kv1_p", [128, 1024]), dout("kv2_p", [512, 1024]), dout("kv3_p", [2048, 1024])]
    kvs_d = [dout(f"kv{g + 1}_s", [NS, 1024]) for g in range(3)]
    dbg = {}

    def dbg_out(name, shape):
        dbg[name] = dout("dbg_" + name, shape)
        return dbg[name]

    w_in_v = w_in_d.rearrange("(kc p) c -> p kc c", p=128)

    with ExitStack() as st:
        S = Sched(nc, st)

        _uid = [0]

        def sb(stack, name, shape, dt):
            _uid[0] += 1
            return stack.enter_context(nc.sbuf_tensor(f"t{_uid[0]}_{name}", list(shape), dt))

        pb = [st.enter_context(nc.psum_tensor(f"pb{i}", [128, 512], F32)) for i in range(8)]
        pbB = [Buf(f"pb{i}") for i in range(8)]

        ident = sb(st, "ident", [128, 128], F32)
        identb = sb(st, "identb", [128, 128], BF16)
        hT = sb(st, "hT", [128, 8, HTW], BF16)
        B_ident, B_identb, B_hT = Buf("ident"), Buf("identb"), Buf("hT")
        oaT_d = dout("scr_oaT", [128, 4, T + 128], BF16)
        S.dma("sync", ident[:], ident_d, writes=[B_ident])
        S.do("vector", lambda e: e.tensor_copy(out=identb[:], in_=ident[:]), reads=[B_ident], writes=[B_identb])
        S.do("vector", lambda e: e.memset(hT[:], 0.0), writes=[B_hT])

        def rmsnorm_to_T(ph, src_blocks, gain_d, dstT, B_dstT, tag, producer=None, out_rows=None):
            g_bc = sb(ph, tag + "g_bc", [128, D], F32)
            B_g = Buf()
            S.dma("sync", g_bc[:], gain_d.partition_broadcast(128), writes=[B_g])
            NX = 4 if producer is None else 2
            xst = [sb(ph, f"{tag}xst{i}", [128, D], F32) for i in range(NX)]
            B_xst = [Buf() for _ in range(NX)]
            junk = sb(ph, tag + "junk", [128, D], F32)
            B_junk = Buf()
            ss = sb(ph, tag + "ss", [128, 4 * len(src_blocks)], F32)
            B_ss = [Buf() for _ in src_blocks]
            hb = [sb(ph, f"{tag}hb{i}", [128, D], BF16 if out_rows is None else F32) for i in range(2)]
            B_hb = [Buf(), Buf()]
            for i, (src, rows, col0) in enumerate(src_blocks):
                s = i % 2
                sx = i % NX
                if rows < 128:
                    S.do("vector", lambda e, sx=sx: e.memset(xst[sx][:], 0.0), writes=[B_xst[sx]])
                S.dma("sync", xst[sx][0:rows, :], src, writes=[B_xst[sx]])
                if producer is not None:
                    producer(i, xst[sx], B_xst[sx])
                S.do("scalar", lambda e, sx=sx, i=i: e.activation(out=junk[:], in_=xst[sx][:], func=AF.Square,
                                                                 accum_out=ss[:, 4 * i:4 * i + 1]),
                     reads=[B_xst[sx]], writes=[B_junk, B_ss[i]])
                S.do("vector", lambda e, i=i: e.tensor_scalar(out=ss[:, 4 * i + 1:4 * i + 2], in0=ss[:, 4 * i:4 * i + 1],
                                                              scalar1=1.0 / D, scalar2=1e-6, op0=ALU.mult, op1=ALU.add),
                     reads=[B_ss[i]], writes=[B_ss[i]])
                S.do("scalar", lambda e, i=i: e.activation(out=ss[:, 4 * i + 2:4 * i + 3], in_=ss[:, 4 * i + 1:4 * i + 2], func=AF.Sqrt),
                     reads=[B_ss[i]], writes=[B_ss[i]])
                S.do("vector", lambda e, i=i: e.reciprocal(out=ss[:, 4 * i + 3:4 * i + 4], in_=ss[:, 4 * i + 2:4 * i + 3]),
                     reads=[B_ss[i]], writes=[B_ss[i]])
                S.do("vector", lambda e, s=s, sx=sx, i=i: e.scalar_tensor_tensor(out=hb[s][:], in0=xst[sx][:], scalar=ss[:, 4 * i + 3:4 * i + 4],
                                                                                in1=g_bc[:], op0=ALU.mult, op1=ALU.mult),
                     reads=[B_xst[sx], B_ss[i], B_g], writes=[B_hb[s]])
                if out_rows is not None:
                    dst, nrow = out_rows[i]
                    S.dma("sync", dst, hb[s][0:nrow, :], reads=[B_hb[s]])
                    continue
                bk = 6 + (i % 2)
                pbb = pb[bk][:].bitcast(BF16)
                S.mm([("T", pbb[:, kc * 128:(kc + 1) * 128], hb[s][:, kc * 128:(kc + 1) * 128], identb[:]) for kc in range(8)],
                     reads=[B_hb[s], B_identb], writes=[pbB[bk]])
                S.do("scalar", lambda e, pbb=pbb, col0=col0: e.copy(out=dstT[:, :, col0:col0 + 128],
                                                                   in_=pbb.rearrange("p (k t) -> p k t", k=8)),
                     reads=[pbB[bk]], writes=[B_dstT])

        phB = st.enter_context(ExitStack())
        W3 = [sb(phB, f"W3_{j}", [128, 8, 512], BF16) for j in range(3)]
        B_W3 = [Buf() for _ in range(3)]
        for j in range(3):
            S.dma("gpsimd", W3[j][:], w_in_v[:, :, j * 512:(j + 1) * 512], writes=[B_W3[j]])
        with ExitStack() as ph:
            blocks = [(x_d[i * 128:(i + 1) * 128, :], 128, C0 + i * 128) for i in range(NB)]
            if "nosample" not in debug:
                blocks.append((xs_d, NS, SC0))
            rmsnorm_to_T(ph, blocks, norm_mix_d, hT, B_hT, "A")
            S.barrier()

        if "hT" in debug:
            o = dbg_out("hT", [128, 8, HTW])
            with ExitStack() as ph:
                tmp = sb(ph, "dbgtmp", [128, 8, HTW], F32)
                Bt = Buf()
                S.do("vector", lambda e: e.tensor_copy(out=tmp[:], in_=hT[:]), reads=[B_hT], writes=[Bt])
                S.dma("sync", o, tmp[:], reads=[Bt])
                S.barrier()

        with phB as ph:
          if "stopA" not in debug:
              QT = sb(ph, "QTo", [128, 4, T + 128], BF16)
              B_oaT = Buf("oaT")
              S.do("vector", lambda e: e.memset(QT[:, :, T:T + 128], 0.0), writes=[B_oaT])
              amask = sb(ph, "amask", [128, 4, 128], BF16)
              epair = sb(ph, "epair", [128, 4, 4], BF16)
              B_const = Buf()
              S.dma("gpsimd", amask[:], amask_d, writes=[B_const])
              S.dma("gpsimd", epair[:], epair_d, writes=[B_const])
              ropes = sb(ph, "ropes", [128, 96], F32)
              S.dma("sync", ropes[:], rope_cs_d.partition_broadcast(128), writes=[B_const])
              acc_num = sb(ph, "acc_num", [128, 4, T], F32)
              acc_den = sb(ph, "acc_den", [4, 2, T], F32)
              B_acc = [Buf() for _ in range(NB)]
              qkv_s = sb(ph, "qkv_s", [NS, 3, 512], F32)
              B_qkvs = Buf()
              selb = sb(ph, "selb", [NS, NS, 128], F32)
              S.dma("sync", selb[:], selb_d, writes=[B_const])
              KVc = [sb(ph, f"KVc{i}", [128, 2, 8, 65], F32) for i in range(2)]
              B_KVc = [Buf(), Buf()]
              for i in range(2):
                  S.do("vector", lambda e, i=i: e.memset(KVc[i][:], 1.0), writes=[B_KVc[i]])
              prod = sb(ph, "prod", [128, 512], F32)
              B_prod = Buf()
              sc = sb(ph, "sc", [128, 16], F32)
              B_sc = Buf()
              Pz = [sb(ph, f"Pz{b}", [128, 8, NS], F32) for b in range(NS)]
              B_Pz = [Buf() for _ in range(NS)]
              for b in range(NS):
                  S.do("vector", lambda e, b=b: e.memset(Pz[b][:], 0.0), writes=[B_Pz[b]])
              acc_s = sb(ph, "acc_s", [NS, 8, 65], F32)
              cn = sb(ph, "cn", [NS, 8, 65], F32)
              sn = sb(ph, "sn", [NS, 16], F32)
              B_accs, B_cn, B_sn = Buf(), Buf(), Buf()
              KT = sb(ph, "KT", [128, 4, T], BF16)
              Vt = sb(ph, "Vt", [128, NB, 512], BF16)
              B_Q = [Buf() for _ in range(NB)]
              B_K = [Buf() for _ in range(NB)]
              B_V = [Buf() for _ in range(NB)]
              rc = sb(ph, "rc", [128, NB, 32], F32)
              rs = sb(ph, "rs", [128, NB, 2, 32], F32)
              B_rope = Buf()
              t1_ = sb(ph, "t1", [128, 512], F32)
              t2_ = sb(ph, "t2", [128, 512], F32)
              t1, t2 = [t1_, t1_], [t2_, t2_]
              B_t1_, B_t2_ = Buf(), Buf()
              B_t1, B_t2 = [B_t1_, B_t1_], [B_t2_, B_t2_]
              qb = [sb(ph, f"qb{i}", [128, 512], BF16) for i in range(2)]
              kb = [sb(ph, f"kb{i}", [128, 512], BF16) for i in range(2)]
              B_qb = [Buf(), Buf()]
              B_kb = [Buf(), Buf()]
              kv32 = [sb(ph, f"kv32_{i}", [128, 2, 512], F32) for i in range(2)]
              B_kv32 = [Buf(), Buf()]
              PT = [sb(ph, f"PT{i}", [128, 512], BF16) for i in range(2)]
              PM = [sb(ph, f"PM{i}", [128, 512], BF16) for i in range(2)]
              B_PT = [Buf(), Buf()]
              B_PM = [Buf(), Buf()]

              def v4(ap):
                  return ap.rearrange("p (h t d) -> p h t d", h=8, t=2, d=32)

              def rope_ops(np_, src_ps, B_src, cos_ap, sin_ap, B_tab, out_ap, B_out, s):
                  S.do("vector", lambda e: e.tensor_tensor(out=v4(t1[s][0:np_, :]), in0=v4(src_ps), in1=cos_ap, op=ALU.mult),
                       reads=[B_src, B_tab], writes=[B_t1[s]])
                  S.do("vector", lambda e: e.tensor_tensor(out=v4(t2[s][0:np_, :]), in0=v4(src_ps)[:, :, ::-1, :], in1=sin_ap, op=ALU.mult),
                       reads=[B_src, B_tab], writes=[B_t2[s]])
                  S.do("vector", lambda e: e.tensor_tensor(out=out_ap, in0=t1[s][0:np_, :], in1=t2[s][0:np_, :], op=ALU.add),
                       reads=[B_t1[s], B_t2[s]], writes=[B_out])

              for g, (window, d) in enumerate(GROUPS):
                  nbs = NB // d
                  if "g1" in debug and g > 0:
                      continue
                  c0 = g * 1536
                  for j in range(3):
                      if g > 0:
                          S.dma("gpsimd", W3[j][:], w_in_v[:, :, c0 + j * 512:c0 + (j + 1) * 512], writes=[B_W3[j]])
                  S.dma("sync", rc[:], rope_c_d[g], writes=[B_rope])
                  S.dma("sync", rs[:], rope_s_d[g], writes=[B_rope])
                  keep0 = T - min(window, T)
                  for sbk in range(NB + 1):
                      s = sbk % 2
                      if sbk == NB and "nosampleB" in debug:
                          continue
                      if sbk < NB:
                          r, n = divmod(sbk, nbs)
                          tok0 = r + d * 128 * n
                          np_ = 128
                          lcol = lambda kc: hT[:, kc, C0 + tok0:C0 + tok0 + d * 127 + 1:d]
                          cos_ap = rc[:, sbk, :].unsqueeze(1).unsqueeze(1).broadcast_to([128, 8, 2, 32])
                          sin_ap = rs[:, sbk, :, :].unsqueeze(1).broadcast_to([128, 8, 2, 32])
                          B_tab = B_rope
                      else:
                          np_ = NS
                          lcol = lambda kc: hT[:, kc, SC0:SC0 + NS]
                          cos_ap = ropes[0:NS, 0:32].unsqueeze(1).unsqueeze(1).broadcast_to([NS, 8, 2, 32])
                          sin_ap = ropes[0:NS, 32:96].rearrange("p (t d) -> p t d", t=2).unsqueeze(1).broadcast_to([NS, 8, 2, 32])
                          B_tab = B_const
                      banks = [3 * s + j for j in range(3)]
                      for j in range(3):
                          bk = banks[j]
                          S.mm([dict(out=pb[bk][0:np_, :], lhsT=lcol(kc), rhs=W3[j][:, kc, :], start=(kc == 0), stop=(kc == 7)) for kc in range(8)],
                               reads=[B_hT, B_W3[j]], writes=[pbB[bk]])
                      if sbk < NB:
                          rope_ops(128, pb[banks[0]][:, :], pbB[banks[0]], cos_ap, sin_ap, B_tab, qb[s][:], B_qb[s], s)
                          tb = 6
                          pbb = pb[tb][:].bitcast(BF16)
                          S.mm([("T", pbb[:, p * 128:(p + 1) * 128], qb[s][:, p * 128:(p + 1) * 128], identb[:]) for p in range(4)],
                               reads=[B_qb[s], B_identb], writes=[pbB[tb]])
                          S.do("scalar", lambda e, pbb=pbb, sbk=sbk: e.copy(out=QT[:, :, sbk * 128:(sbk + 1) * 128],
                                                                         in_=pbb[:, 0:512].rearrange("p (k t) -> p k t", k=4)),
                               reads=[pbB[tb]], writes=[B_Q[sbk]])
                          rope_ops(128, pb[banks[1]][:, :], pbB[banks[1]], cos_ap, sin_ap, B_tab, kv32[s][:, 0, :], B_kv32[s], s)
                          S.do("scalar", lambda e, s=s: e.copy(out=kb[s][:], in_=kv32[s][:, 0, :]), reads=[B_kv32[s]], writes=[B_kb[s]])
                          tb = 7
                          pbb = pb[tb][:].bitcast(BF16)
                          S.mm([("T", pbb[:, p * 128:(p + 1) * 128], kb[s][:, p * 128:(p + 1) * 128], identb[:]) for p in range(4)],
                               reads=[B_kb[s], B_identb], writes=[pbB[tb]])
                          S.do("scalar", lambda e, pbb=pbb, sbk=sbk: e.copy(out=KT[:, :, sbk * 128:(sbk + 1) * 128],
                                                                         in_=pbb[:, 0:512].rearrange("p (k t) -> p k t", k=4)),
                               reads=[pbB[tb]], writes=[B_K[sbk]])
                          S.do("scalar", lambda e, s=s, bk=banks[2]: e.copy(out=kv32[s][:, 1, :], in_=pb[bk][:, :]), reads=[pbB[banks[2]]], writes=[B_kv32[s]])
                          S.do("scalar", lambda e, s=s, sbk=sbk: e.copy(out=Vt[:, sbk, :], in_=kv32[s][:, 1, :]), reads=[B_kv32[s]], writes=[B_V[sbk]])
                          if tok0 + d * 127 >= keep0 and tok0 >= keep0:
                              dst = kvp_d[g][tok0 - keep0:tok0 - keep0 + d * 127 + 1:d, :].rearrange("t (s c) -> t s c", s=2)
                              S.dma("sync", dst, kv32[s][:], reads=[B_kv32[s]])
                      else:
                          rope_ops(NS, pb[banks[0]][0:NS, :], pbB[banks[0]], cos_ap, sin_ap, B_tab, qkv_s[:, 0, :], B_qkvs, s)
                          rope_ops(NS, pb[banks[1]][0:NS, :], pbB[banks[1]], cos_ap, sin_ap, B_tab, qkv_s[:, 1, :], B_qkvs, s)
                          S.do("scalar", lambda e, bk=banks[2], g=g: e.copy(out=qkv_s[:, 2, :], in_=pb[bk][0:NS, :]), reads=[pbB[banks[2]]], writes=[B_qkvs])
                          S.dma("sync", kvs_d[g].rearrange("t (s c) -> t s c", s=2), qkv_s[:, 1:3, :], reads=[B_qkvs])
                  for sbk in range(NB):
                      if "noattn" in debug:
                          continue
                      r, n = divmod(sbk, nbs)
                      tok0 = r + d * 128 * n
                      kbs = [sbk - 1, sbk] if n > 0 else [sbk]
                      nk = len(kbs)
                      ncol = nk * 256
                      moff = 0 if nk == 2 else 256
                      bO = 4 + (sbk % 2)
                      bD = 6 + (sbk % 2)
                      for p in range(4):
                          u = (sbk * 4 + p) % 2
                          bS = [2 * u, 2 * u + 1]
                          specs = []
                          for ki, kbk in enumerate(kbs):
                              for hp in range(2):
                                  specs.append(dict(out=pb[bS[hp]][:, ki * 128:(ki + 1) * 128],
                                                    lhsT=KT[hp * 64:(hp + 1) * 64, p, kbk * 128:(kbk + 1) * 128],
                                                    rhs=QT[hp * 64:(hp + 1) * 64, p, sbk * 128:(sbk + 1) * 128], start=True, stop=True))
                          S.mm(specs, reads=[B_K[k_] for k_ in kbs] + [B_Q[sbk]], writes=[pbB[bS[0]], pbB[bS[1]]])
                          for hp in range(2):
                              S.do("scalar", lambda e, u=u, b_=bS[hp], hp=hp, nk=nk: e.activation(out=PT[u][:, hp * 256:hp * 256 + nk * 128], in_=pb[b_][:, 0:nk * 128],
                                                                                                  func=AF.Exp, scale=0.125),
                                   reads=[pbB[bS[hp]]], writes=[B_PT[u]])
                          pt3 = PT[u][:].rearrange("p (a c) -> p a c", a=2)[:, :, 0:nk * 128]
                          pm3 = PM[u][:].rearrange("p (a c) -> p a c", a=2)[:, :, 0:nk * 128]
                          mk3 = amask[:].rearrange("p (a b) c -> p a (b c)", a=2)[:, :, (2 - nk) * 128:256]
                          S.do("vector", lambda e, pt3=pt3, pm3=pm3, mk3=mk3: e.tensor_tensor(out=pm3, in0=pt3, in1=mk3, op=ALU.mult),
                               reads=[B_PT[u], B_const], writes=[B_PM[u]])
                          specs = []
                          for hp in range(2):
                              for ki, kbk in enumerate(kbs):
                                  col0 = hp * 256 + ki * 128
                                  specs.append(dict(out=pb[bO][hp * 64:(hp + 1) * 64, p * 128:(p + 1) * 128],
                                                    lhsT=Vt[:, kbk, p * 128 + hp * 64:p * 128 + (hp + 1) * 64],
                                                    rhs=PM[u][:, col0:col0 + 128], start=(ki == 0), stop=(ki == nk - 1)))
                          for ki in range(nk):
                              rhs = PM[u][:].rearrange("p (a c) -> p a c", a=2)[:, :, ki * 128:(ki + 1) * 128]
                              specs.append(dict(out=pb[bD][0:4, 0:256], lhsT=epair[:, p, :], rhs=rhs,
                                                start=(p == 0 and ki == 0), stop=(p == 3 and ki == nk - 1)))
                          S.mm(specs, reads=[B_PM[u]] + [B_V[k_] for k_ in kbs] + [B_const], writes=[pbB[bO], pbB[bD]])
                      num_dst = acc_num[:, :, tok0:tok0 + d * 127 + 1:d]
                      den_dst = acc_den[:, :, tok0:tok0 + d * 127 + 1:d]
                      num_src = pb[bO][:, :].rearrange("p (a q) -> p a q", a=4)
                      den_src = pb[bD][0:4, 0:256].rearrange("p (a q) -> p a q", a=2)
                      Bacc = B_acc[0]
                      if g == 0:
                          S.do("scalar", lambda e, num_dst=num_dst, num_src=num_src: e.copy(out=num_dst, in_=num_src), reads=[pbB[bO]], writes=[Bacc])
                          S.do("scalar", lambda e, den_dst=den_dst, den_src=den_src: e.copy(out=den_dst, in_=den_src), reads=[pbB[bD]], writes=[Bacc])
                      else:
                          S.do("vector", lambda e, num_dst=num_dst, num_src=num_src: e.tensor_tensor(out=num_dst, in0=num_src, in1=num_dst, op=ALU.add),
                               reads=[pbB[bO]], writes=[Bacc])
                          S.do("vector", lambda e, den_dst=den_dst, den_src=den_src: e.tensor_tensor(out=den_dst, in0=den_src, in1=den_dst, op=ALU.add),
                               reads=[pbB[bD]], writes=[Bacc])
                  if "nosampleB" not in debug:
                      buf_len = min(window, 16384)
                      for b in range(NS):
                          sl = b % 2
                          bq = b % 2
                          S.dma("sync", KVc[sl][:, :, :, 0:64], ck_d[g][b, 0:buf_len:d, :].rearrange("m (s h e) -> m s h e", s=2, h=8), writes=[B_KVc[sl]])
                          S.mm([dict(out=pb[bq][:, :], lhsT=selb[0:NS, b, :], rhs=qkv_s[0:NS, 0, :])], reads=[B_qkvs, B_const], writes=[pbB[bq]])
                          S.do("vector", lambda e, sl=sl, bq=bq: e.tensor_tensor(out=prod[:].rearrange("p (h e) -> p h e", h=8), in0=KVc[sl][:, 0, :, 0:64],
                                                                              in1=pb[bq][:, :].rearrange("p (h e) -> p h e", h=8), op=ALU.mult),
                               reads=[B_KVc[sl], pbB[bq]], writes=[B_prod])
                          S.do("vector", lambda e: e.tensor_reduce(out=sc[:, 0:8], in_=prod[:].rearrange("p (h e) -> p h e", h=8), axis=AX.X, op=ALU.add),
                               reads=[B_prod], writes=[B_sc])
                          S.do("scalar", lambda e, b=b: e.activation(out=Pz[b][:, :, b], in_=sc[:, 0:8], func=AF.Exp, scale=0.125),
                               reads=[B_sc], writes=[B_Pz[b]])
                          specs = []
                          for h in range(8):
                              bn = 2 + h // 4
                              specs.append(dict(out=pb[bn][0:NS, (h % 4) * 65:(h % 4 + 1) * 65], lhsT=Pz[b][:, h, :], rhs=KVc[sl][:, 1, h, :],
                                                start=(b == 0 and h % 4 == 0), stop=(b == NS - 1 and h % 4 == 3)))
                          S.mm(specs, reads=[B_Pz[b], B_KVc[sl]], writes=[pbB[2], pbB[3]])
                      S.do("vector", lambda e: e.tensor_tensor(out=prod[0:NS, :], in0=qkv_s[:, 0, :], in1=qkv_s[:, 1, :], op=ALU.mult),
                           reads=[B_qkvs], writes=[B_prod])
                      S.do("vector", lambda e: e.tensor_reduce(out=sn[:, 0:8], in_=prod[0:NS, :].rearrange("p (h e) -> p h e", h=8), axis=AX.X, op=ALU.add),
                           reads=[B_prod], writes=[B_sn])
                      S.do("scalar", lambda e: e.activation(out=sn[:, 8:16], in_=sn[:, 0:8], func=AF.Exp, scale=0.125), reads=[B_sn], writes=[B_sn])
                      S.do("vector", lambda e: e.tensor_tensor(out=cn[:, :, 0:64], in0=qkv_s[:, 2, :].rearrange("p (h e) -> p h e", h=8),
                                                               in1=sn[:, 8:16].unsqueeze(2).broadcast_to([NS, 8, 64]), op=ALU.mult),
                           reads=[B_qkvs, B_sn], writes=[B_cn])
                      S.do("vector", lambda e: e.tensor_copy(out=cn[:, :, 64:65], in_=sn[:, 8:16].unsqueeze(2)), reads=[B_sn], writes=[B_cn])
                      for hb in range(2):
                          src = pb[2 + hb][0:NS, 0:260].rearrange("p (h e) -> p h e", h=4)
                          dst = acc_s[:, hb * 4:(hb + 1) * 4, :]
                          cns = cn[:, hb * 4:(hb + 1) * 4, :]
                          S.do("vector", lambda e, src=src, cns=cns: e.tensor_tensor(out=cns, in0=src, in1=cns, op=ALU.add),
                               reads=[pbB[2 + hb]], writes=[B_cn])
                          if g == 0:
                              S.do("vector", lambda e, dst=dst, cns=cns: e.tensor_copy(out=dst, in_=cns), reads=[B_cn], writes=[B_accs])
                          else:
                              S.do("vector", lambda e, dst=dst, cns=cns: e.tensor_tensor(out=dst, in0=dst, in1=cns, op=ALU.add), reads=[B_cn], writes=[B_accs])
              S.barrier()
              selp = sb(ph, "selp", [4, 4, 64], F32)
              S.dma("sync", selp[:], selp_d, writes=[B_const])
              if "noattn" not in debug:
                  S.do("vector", lambda e: e.reciprocal(out=acc_den[:], in_=acc_den[:]), reads=[B_acc[0]], writes=[B_acc[0]])
                  for p in range(4):
                      for ch in range(4):
                          bk = (p * 4 + ch) % 2
                          S.mm([dict(out=pb[bk][hp * 64:(hp + 1) * 64, :], lhsT=selp[0:4, p, :], rhs=acc_den[0:4, hp, ch * 512:(ch + 1) * 512]) for hp in range(2)],
                               reads=[B_acc[0], B_const], writes=[pbB[bk]])
                          S.do("vector", lambda e, p=p, ch=ch, bk=bk: e.tensor_tensor(out=QT[:, p, ch * 512:(ch + 1) * 512], in0=acc_num[:, p, ch * 512:(ch + 1) * 512],
                                                                                  in1=pb[bk][:, :], op=ALU.mult),
                               reads=[B_acc[0], pbB[bk]], writes=[B_oaT])
              if "nosampleB" not in debug:
                  S.do("vector", lambda e: e.reciprocal(out=sn[:, 0:8], in_=acc_s[:, :, 64]), reads=[B_accs], writes=[B_sn])
                  S.do("vector", lambda e: e.tensor_tensor(out=prod[0:NS, :].rearrange("p (h e) -> p h e", h=8), in0=acc_s[:, :, 0:64],
                                                           in1=sn[:, 0:8].unsqueeze(2).broadcast_to([NS, 8, 64]), op=ALU.mult),
                       reads=[B_accs, B_sn], writes=[B_prod])
                  S.mm([("T", pb[4][:, p * 4:p * 4 + NS], prod[0:NS, p * 128:(p + 1) * 128], ident[0:NS, 0:NS]) for p in range(4)],
                       reads=[B_prod, B_ident], writes=[pbB[4]])
                  S.do("vector", lambda e: e.tensor_copy(out=QT[:, :, T:T + NS], in_=pb[4][:, 0:16].rearrange("p (a b) -> p a b", a=4)),
                       reads=[pbB[4]], writes=[B_oaT])
                  if "acc" in debug:
                      o3 = dbg_out("oas", [NS, 512])
                      S.dma("sync", o3, prod[0:NS, :], reads=[B_prod])
              S.dma("sync", oaT_d, QT[:], reads=[B_oaT])
              if "acc" in debug:
                  o1 = dbg_out("num", [128, 4, T])
                  o2 = dbg_out("den", [4, 2, T])
                  S.dma("sync", o1, acc_num[:], reads=[B_acc[0]])
                  S.dma("sync", o2, acc_den[:], reads=[B_acc[0]])
              S.barrier()

        o_bT = sb(st, "o_bT", [128, 8, T + 128], BF16)
        B_obT = Buf("obT")
        S.do("vector", lambda e: e.memset(o_bT[:, :, T:T + 128], 0.0), writes=[B_obT])
        S.do("vector", lambda e: e.tensor_copy(out=hT[:, :, SC0 + NS:SC0 + NS + 1], in_=hT[:, :, C0 + T - 1:C0 + T]), reads=[B_hT], writes=[B_hT])
        C0D = 0.6065306597126334
        def rwkv_pass(hh, ph):
            a0g = 4 * hh
            Bc = Buf("dconst")
            Ws, B_Ws = Ws_g, B_Ws_g
            if hh == 0:
                load_Ws(0)

            def gct(lt):
                return (lt // 4) * 8 + a0g + lt % 4 if lt < 12 else 24 + (lt - 12)
            mu_c = sb(ph, "mu_c", [128, 27], F32)
            omm_c = sb(ph, "omm_c", [128, 27], F32)
            cols8 = sb(ph, "cols8", [128, 5, 8], F32)
            omka = sb(ph, "omka", [128, 8], F32)
            S.dma("sync", mu_c[:], mu_c_d, writes=[Bc])
            S.dma("sync", cols8[:], cols8_d, writes=[Bc])
            S.do("vector", lambda e: e.tensor_scalar(out=omm_c[:], in0=mu_c[:], scalar1=-1.0, scalar2=1.0, op0=ALU.mult, op1=ALU.add), reads=[Bc], writes=[Bc])
            S.do("vector", lambda e: e.tensor_scalar(out=omka[:], in0=cols8[:, 3, :], scalar1=-1.0, scalar2=1.0, op0=ALU.mult, op1=ALU.add), reads=[Bc], writes=[Bc])
            w2a2 = sb(ph, "w2a2", [128, 512], F32)
            g2a = sb(ph, "g2a", [128, 512], BF16)
            g2b = sb(ph, "g2b", [128, 512], BF16)
            xgb = sb(ph, "xgb", [128, 2, 128], BF16)
            sqb = sb(ph, "sqb", [128, 512], BF16)
            bonesb = sb(ph, "bonesb", [128, 128], BF16)
            B_xgb, B_sqb = Buf(), Buf()
            S.do("vector", lambda e: e.memset(xgb[:], 0.0), writes=[B_xgb])
            S.dma("gpsimd", bonesb[:], bones_d, writes=[Bc])
            gnw = sb(ph, "gnw", [128, 512], F32)
            gnb = sb(ph, "gnb", [128, 512], F32)
            hs = slice(hh * 512, (hh + 1) * 512)
            S.dma("sync", w2a2[:], w2a2_d[:, hs], writes=[Bc])
            S.dma("gpsimd", g2a[:], g2_d[0:128, hs], writes=[Bc])
            S.do("vector", lambda e: e.memset(g2b[:], 0.0), writes=[Bc])
            S.dma("gpsimd", g2b[0:32, :], g2_d[128:160, hs], writes=[Bc])
            S.dma("sync", gnw[:], gn_w_d[:, hs].partition_broadcast(128), writes=[Bc])
            S.dma("sync", gnb[:], gn_b_d[:, hs].partition_broadcast(128), writes=[Bc])
            rmask = sb(ph, "rmask", [128, 4, 128], BF16)
            bones = sb(ph, "bones", [128, 128], F32)
            bo2 = sb(ph, "bo2", [128, 2], BF16)
            i64x2 = sb(ph, "i64x2", [128, 64], F32)
            ones128 = sb(ph, "ones128", [128, 128], F32)
            S.dma("gpsimd", rmask[:], rmask_d, writes=[Bc])
            S.dma("sync", bones[:], bones_d, writes=[Bc])
            S.dma("gpsimd", bo2[:], bo2_d, writes=[Bc])
            S.dma("sync", i64x2[:], i64x2_d, writes=[Bc])
            S.do("vector", lambda e: e.memset(ones128[:], 1.0), writes=[Bc])

            def f4(nm):
                return sb(ph, nm, [128, 4, 128], F32)
            sgT, aT, ginv, S1, S2, S3 = [f4(n) for n in ("sgT", "aT", "ginv", "S1", "S2", "S3")]
            rT3, kT3, vT3 = [sb(ph, n, [128, 4, 384], F32) for n in ("rT3", "kT3", "vT3")]
            B_r, B_k, B_v, B_sg, B_a, B_gi, B_S1, B_S2, B_S3 = [Buf() for _ in range(9)]
            cs, B_cs = S3, B_S3
            lw3 = sb(ph, "lw3", [128, 384], F32)
            xg3 = sb(ph, "xg3", [128, 2, 384], F32)
            B_lw, B_xg = Buf(), Buf()
            S.do("vector", lambda e: e.memset(xg3[:], 0.0), writes=[B_xg])
            tmix = sb(ph, "tmix", [128, 384], F32)
            B_tmix = Buf()
            gamx2 = [sb(ph, "gamx", [128, 4, 129], F32)] * 2
            B_gam2 = [Buf()] * 2
            S.do("vector", lambda e: e.memset(gamx2[0][:], 1.0), writes=[B_gam2[0]])
            AR2 = [sb(ph, "AR", [128, 4, 2, 128], BF16)] * 2
            ARbd2 = [sb(ph, "ARbd", [128, 4, 2, 2, 128], BF16)] * 2
            B_AR2, B_ARbd2 = [Buf()] * 2, [Buf()] * 2
            S.do("vector", lambda e: e.memset(ARbd2[0][:], 0.0), writes=[B_ARbd2[0]])
            BT2 = [sb(ph, "BT", [128, 4, 128], BF16)] * 2
            KhT2 = [sb(ph, "KhT", [128, 4, 128], BF16)] * 2
            B_BT2, B_KhT2 = [Buf()] * 2, [Buf()] * 2
            rks = sb(ph, "rks", [128, 512], F32)
            B_rks = Buf()
            VT = sb(ph, "VT", [128, 4, 128], BF16)
            rkT = sb(ph, "rkT", [128, 4, 128], BF16)
            B_VT, B_rk = Buf(), Buf()
            Btok2 = [sb(ph, "Btok", [128, 512], BF16)] * 2
            Ktok2 = [sb(ph, "Ktok", [128, 512], BF16)] * 2
            Vtok2 = [sb(ph, "Vtok", [128, 512], BF16)] * 2
            B_Btok2, B_Ktok2, B_Vtok2 = [Buf()] * 2, [Buf()] * 2, [Buf()] * 2
            bon2 = [sb(ph, "bon", [128, 8], F32)] * 2
            B_bon2 = [Buf()] * 2
            gate_sb2 = [sb(ph, "gate_sb", [128, 512], F32)] * 2
            B_gate2 = [Buf()] * 2
            XS1 = [sb(ph, f"XS1_{a}", [128, 512], BF16) for a in range(4)]
            XS2 = [sb(ph, f"XS2_{a}", [128, 512], BF16) for a in range(4)]
            B_XS1 = [Buf() for _ in range(4)]
            B_XS2 = [Buf() for _ in range(4)]
            L0 = [sb(ph, f"L0_{a}", [128, 2, 128], BF16) for a in range(4)]
            B_L0 = [Buf() for _ in range(4)]
            LN = [[sb(ph, f"LN_{a}_{q}", [128, 512], BF16) for q in range(2)] for a in range(4)]
            B_LN = [[Buf(), Buf()] for _ in range(4)]
            Ut = [[sb(ph, f"U_{a}_{q}", [128, 128], BF16) for q in range(2)] for a in range(4)]
            B_U = [[Buf(), Buf()] for _ in range(4)]
            Mst = sb(ph, "Mst", [128, 4, 64], F32)
            Mg = sb(ph, "Mg", [128, 4, 64], F32)
            M0bd = sb(ph, "M0bd", [128, 4, 2, 64], BF16)
            B_M, B_Mg, B_M0bd = Buf(), Buf(), Buf()
            S.do("vector", lambda e: e.memset(Mst[:], 0.0), writes=[B_M])
            S.do("vector", lambda e: e.memset(M0bd[:], 0.0), writes=[B_M0bd])
            ob = sb(ph, "ob", [128, 512], BF16)
            stt = sb(ph, "stt", [128, 48], F32)
            B_ob, B_st = Buf(), Buf()
            ssT = sb(ph, "ssT", [128, 15, NS], F32)
            B_ssT = Buf()
            praw = sb(ph, "praw", [128, 15, 8], F32)
            B_praw = Buf()
            ysq = sb(ph, "ysq", [128, 512], F32)
            tt = sb(ph, "tt", [128, 512], F32)
            bv = sb(ph, "bv", [128, 512], F32)
            B_ysq, B_tt, B_bv = Buf(), Buf(), Buf()
            for q in range(4):
                w_ = 512 if q < 3 else 288
                src_c0 = (q * 1024 + a0g * 128) if q < 3 else 3072
                S.dma("sync", tt[0:NS, 0:w_], sshift_d[:, src_c0:src_c0 + w_], writes=[B_tt])
                for j in range((w_ + 127) // 128):
                    lt = q * 4 + j
                    cw = 32 if lt == 14 else 128
                    bk = lt % 2
                    S.mm([("T", pb[bk][0:cw, 0:NS], tt[0:NS, j * 128:j * 128 + cw], ident[0:NS, 0:NS])], reads=[B_tt, B_ident], writes=[pbB[bk]])
                    S.do("vector", lambda e, lt=lt, cw=cw, bk=bk: e.tensor_copy(out=ssT[0:cw, lt, :], in_=pb[bk][0:cw, 0:NS]), reads=[pbB[bk]], writes=[B_ssT])

            def bc3(ap2, n=128):
                return ap2.unsqueeze(2).broadcast_to([128, 4, n])

            def dest_of(lt):
                if lt < 4:
                    return rT[:, lt, :], B_r
                if lt < 8:
                    return kT[:, lt - 4, :], B_k
                if lt < 12:
                    return vT[:, lt - 8, :], B_v
                if lt == 12:
                    return lw[:, :], B_lw
                if lt == 13:
                    return xg[:, 0, :], B_xg
                return xg[0:32, 1, :], B_xg

            def rwkv_block(c, stage):
                isX = (c == NB)
                np_ = NS if isX else 128
                if (isX and DNOX) or (not isX and c >= DNB):
                    return
                pc = c % 2
                gate_sb, B_gate = gate_sb2[pc], B_gate2[pc]
                gamx, B_gam = gamx2[pc], B_gam2[pc]
                Vtok, B_Vtok = Vtok2[pc], B_Vtok2[pc]
                Btok, B_Btok = Btok2[pc], B_Btok2[pc]
                Ktok, B_Ktok = Ktok2[pc], B_Ktok2[pc]
                bon, B_bon = bon2[pc], B_bon2[pc]
                AR, B_AR = AR2[pc], B_AR2[pc]
                ARbd, B_ARbd = ARbd2[pc], B_ARbd2[pc]
                BT, B_BT = BT2[pc], B_BT2[pc]
                KhT, B_KhT = KhT2[pc], B_KhT2[pc]
                jb = 0 if isX else c % 3
                rT = rT3[:, :, jb * 128:(jb + 1) * 128]
                kT = kT3[:, :, jb * 128:(jb + 1) * 128]
                vT = vT3[:, :, jb * 128:(jb + 1) * 128]
                lw = lw3[:, jb * 128:(jb + 1) * 128]
                xg = xg3[:, :, jb * 128:(jb + 1) * 128]
                if stage == "B":
                    yield from rwkv_stage_b(c, isX, np_, gate_sb, B_gate, gamx, B_gam, Vtok, B_Vtok, Btok, B_Btok, Ktok, B_Ktok, bon, B_bon, AR, B_AR, ARbd, B_ARbd, BT, B_BT, KhT, B_KhT)
                    return
                def dest3(lt):
                    if lt < 4:
                        return rT3[:, lt, :], B_r
                    if lt < 8:
                        return kT3[:, lt - 4, :], B_k
                    if lt < 12:
                        return vT3[:, lt - 8, :], B_v
                    if lt == 12:
                        return lw3[:, :], B_lw
                    if lt == 13:
                        return xg3[:, 0, :], B_xg
                    return xg3[0:32, 1, :], B_xg
                if isX or c % 3 == 0:
                    nb_ = 1 if isX else min(3, NB - c)
                    for lt in range(15):
                        bk = lt % 4
                        cw = 32 if lt == 14 else 128
                        ct = gct(lt)
                        dst, B_dst = dest3(lt)
                        if isX:
                            S.mm([dict(out=pb[bk][0:cw, 0:128], lhsT=Ws[:, kc, lt * 128:lt * 128 + cw], rhs=hT[:, kc, SC0:SC0 + 128], start=(kc == 0), stop=(kc == 7)) for kc in range(8)],
                                 reads=[B_hT, B_Ws], writes=[pbB[bk]])
                            S.do("vector", lambda e, lt=lt, cw=cw, bk=bk: e.tensor_copy(out=praw[0:cw, lt, :], in_=pb[bk][0:cw, 0:8]),
                                 reads=[pbB[bk]], writes=[B_praw])
                            S.do("vector", lambda e, lt=lt, cw=cw, ct=ct: e.tensor_scalar(out=tmix[0:cw, 0:NS], in0=ssT[0:cw, lt, :], scalar1=mu_c[0:cw, ct:ct + 1],
                                                                                  scalar2=None, op0=ALU.mult),
                                 reads=[B_ssT, Bc], writes=[B_tmix])
                            S.do("vector", lambda e, cw=cw, bk=bk, ct=ct, dst=dst: e.scalar_tensor_tensor(out=dst[0:cw, 0:NS], in0=pb[bk][0:cw, 0:NS],
                                                                                                 scalar=omm_c[0:cw, ct:ct + 1], in1=tmix[0:cw, 0:NS],
                                                                                                 op0=ALU.mult, op1=ALU.add),
                                 reads=[pbB[bk], B_tmix, Bc], writes=[B_dst])
                        else:
                            n_ = nb_ * 128
                            S.mm([dict(out=pb[bk][0:cw, 0:n_ + 1], lhsT=Ws[:, kc, lt * 128:lt * 128 + cw], rhs=hT[:, kc, C0 + c * 128 - 1:C0 + c * 128 + n_],
                                       start=(kc == 0), stop=(kc == 7)) for kc in range(8)], reads=[B_hT, B_Ws], writes=[pbB[bk]])
                            S.do("vector", lambda e, cw=cw, bk=bk, ct=ct, n_=n_: e.tensor_scalar(out=tmix[0:cw, 0:n_], in0=pb[bk][0:cw, 0:n_],
                                                                                         scalar1=mu_c[0:cw, ct:ct + 1], scalar2=None, op0=ALU.mult),
                                 reads=[pbB[bk], Bc], writes=[B_tmix])
                            S.do("vector", lambda e, cw=cw, bk=bk, ct=ct, dst=dst, n_=n_: e.scalar_tensor_tensor(out=dst[0:cw, 0:n_], in0=pb[bk][0:cw, 1:n_ + 1],
                                                                                                        scalar=omm_c[0:cw, ct:ct + 1], in1=tmix[0:cw, 0:n_],
                                                                                                        op0=ALU.mult, op1=ALU.add),
                                 reads=[pbB[bk], B_tmix, Bc], writes=[B_dst])
                        if lt % 3 == 2:
                            yield
                    if isX and hh == 0:
                        load_Ws(1)
                    if isX and hh == 1:
                        load_Wp()
                if isX:
                    for q in range(4):
                        w_ = 512 if q < 3 else 288
                        bk = q % 2
                        for lt in range(4 * q, min(4 * q + 4, 15)):
                            cw = 32 if lt == 14 else 128
                            S.mm([("T", pb[bk][0:8, (lt % 4) * 128:(lt % 4) * 128 + cw], praw[0:cw, lt, :], ident[0:cw, 0:cw])], reads=[B_praw, B_ident], writes=[pbB[bk]])
                        if q == 3 and hh == 1:
                            continue
                        dc0 = (q * 1024 + a0g * 128) if q < 3 else 3072
                        s1f = S1[0:8, :, :].rearrange("p a t -> p (a t)")
                        S.do("vector", lambda e, bk=bk, w_=w_, s1f=s1f: e.tensor_copy(out=s1f[:, 0:w_], in_=pb[bk][0:8, 0:w_]), reads=[pbB[bk]], writes=[B_S1])
                        S.dma("sync", shift_s_d[:, dc0:dc0 + w_], s1f[0:NS, 0:w_], reads=[B_S1])
                        S.dma("sync", shift_p_d[:, dc0:dc0 + w_], s1f[NS:NS + 1, 0:w_], reads=[B_S1])
                    yield
                if DLIM < 2:
                    return
                yield
                S.do("scalar", lambda e: e.activation(out=lw[0:64, :], in_=lw[0:64, :], func=AF.Tanh), reads=[B_lw], writes=[B_lw])
                S.do("scalar", lambda e: e.activation(out=xgb[:, 0, :], in_=xg[:, 0, :], func=AF.Sigmoid), reads=[B_xg], writes=[B_xgb])
                S.do("scalar", lambda e: e.activation(out=xgb[0:32, 1, :], in_=xg[0:32, 1, :], func=AF.Sigmoid), reads=[B_xg], writes=[B_xgb])
                S.mm([dict(out=pb[4][:, a * 128:(a + 1) * 128], lhsT=w2a2[0:64, a * 128:(a + 1) * 128], rhs=lw[0:64, :]) for a in range(4)],
                     reads=[B_lw, Bc], writes=[pbB[4]])
                S.mm([dict(out=pb[5][:, a * 128:(a + 1) * 128], lhsT=w2a2[64:128, a * 128:(a + 1) * 128], rhs=lw[64:128, :]) for a in range(4)],
                     reads=[B_lw, Bc], writes=[pbB[5]])
                for a in range(4):
                    S.do("scalar", lambda e, a=a: e.activation(out=sgT[:, a, :], in_=pb[4][:, a * 128:(a + 1) * 128], func=AF.Sigmoid, bias=cols8[:, 0, a0g + a:a0g + a + 1]),
                         reads=[pbB[4], Bc], writes=[B_sg])
                    S.do("scalar", lambda e, a=a: e.activation(out=aT[:, a, :], in_=pb[5][:, a * 128:(a + 1) * 128], func=AF.Sigmoid, bias=cols8[:, 1, a0g + a:a0g + a + 1]),
                         reads=[pbB[5], Bc], writes=[B_a])
                S.mm([dict(out=pb[6][:, :], lhsT=xgb[:, 0, :], rhs=g2a[:, :], start=True, stop=False),
                      dict(out=pb[6][:, :], lhsT=xgb[:, 1, :], rhs=g2b[:, :], start=False, stop=True)], reads=[B_xgb, Bc], writes=[pbB[6]])
                S.do("scalar", lambda e: e.copy(out=gate_sb[:], in_=pb[6][:, :]), reads=[pbB[6]], writes=[B_gate])
                if DLIM < 3:
                    return
                yield
                if isX:
                    src_cs, B_src = sgT, B_sg
                else:
                    for a in range(4):
                        S.do("vector", lambda e, a=a: e.tensor_tensor_scan(out=cs[:, a, :], data0=ones128[:], data1=sgT[:, a, :], initial=0.0, op0=ALU.mult, op1=ALU.add),
                             reads=[B_sg, Bc], writes=[B_cs])
                    src_cs, B_src = cs, B_cs
                S.do("scalar", lambda e, src_cs=src_cs: e.activation(out=gamx[:, :, 1:129], in_=src_cs[:], func=AF.Exp, scale=-C0D), reads=[B_src], writes=[B_gam])
                S.do("scalar", lambda e, src_cs=src_cs: e.activation(out=ginv[:], in_=src_cs[:], func=AF.Exp, scale=C0D), reads=[B_src], writes=[B_gi])
                if DLIM < 4:
                    return
                yield
                ag = slice(a0g, a0g + 4)
                S.do("vector", lambda e: e.tensor_tensor(out=S1[:], in0=kT[:], in1=bc3(cols8[:, 2, ag]), op=ALU.mult), reads=[B_k, Bc], writes=[B_S1])
                S.do("vector", lambda e: e.tensor_tensor(out=sqb[:].rearrange("p (a t) -> p a t", a=4), in0=S1[:], in1=S1[:], op=ALU.mult), reads=[B_S1], writes=[B_sqb])
                S.mm([dict(out=pb[7][:, :], lhsT=bonesb[:], rhs=sqb[:])], reads=[B_sqb, Bc], writes=[pbB[7]])
                S.do("vector", lambda e: e.tensor_scalar(out=S2[:].rearrange("p a t -> p (a t)"), in0=pb[7][:, :], scalar1=1e-24, scalar2=None, op0=ALU.max),
                     reads=[pbB[7]], writes=[B_S2])
                S.do("scalar", lambda e: e.activation(out=S2[:], in_=S2[:], func=AF.Ln), reads=[B_S2], writes=[B_S2])
                S.do("scalar", lambda e: e.activation(out=S2[:], in_=S2[:], func=AF.Exp, scale=-0.5), reads=[B_S2], writes=[B_S2])
                S.do("vector", lambda e: e.tensor_tensor(out=S1[:], in0=S1[:], in1=S2[:], op=ALU.mult), reads=[B_S1, B_S2], writes=[B_S1])
                S.do("vector", lambda e: e.tensor_tensor(out=S2[:], in0=aT[:], in1=bc3(cols8[:, 3, ag]), op=ALU.mult), reads=[B_a, Bc], writes=[B_S2])
                S.do("vector", lambda e: e.tensor_tensor(out=S2[:], in0=S2[:], in1=bc3(omka[:, ag]), op=ALU.add), reads=[B_S2, Bc], writes=[B_S2])
                S.do("vector", lambda e: e.tensor_tensor(out=S2[:], in0=kT[:], in1=S2[:], op=ALU.mult), reads=[B_k, B_S2], writes=[B_S2])
                S.do("vector", lambda e: e.tensor_tensor(out=S3[:], in0=S1[:], in1=aT[:], op=ALU.mult), reads=[B_S1, B_a], writes=[B_S3])
                if DLIM < 5:
                    return
                yield
                S.do("scalar", lambda e: e.copy(out=VT[:], in_=vT[:]), reads=[B_v], writes=[B_VT])
                pbb = pb[2][:].bitcast(BF16)
                S.mm([("T", pbb[:, a * 128:(a + 1) * 128], VT[:, a, :], identb[:]) for a in range(4)], reads=[B_VT, B_identb], writes=[pbB[2]])
                S.do("scalar", lambda e, pbb=pbb: e.copy(out=Vtok[:], in_=pbb[:, 0:512]), reads=[pbB[2]], writes=[B_Vtok])
                S.do("vector", lambda e: e.tensor_tensor(out=rks[:].rearrange("p (a t) -> p a t", a=4), in0=rT[:], in1=S2[:], op=ALU.mult), reads=[B_r, B_S2], writes=[B_rks])
                S.do("vector", lambda e: e.tensor_tensor(out=rkT[:], in0=rks[:].rearrange("p (a t) -> p a t", a=4), in1=bc3(cols8[:, 4, ag]), op=ALU.mult),
                     reads=[B_rks, Bc], writes=[B_rk])
                S.mm([dict(out=pb[3][:, 2 * a:2 * a + 2], lhsT=rkT[:, a, :], rhs=bo2[:, :]) for a in range(4)], reads=[B_rk, Bc], writes=[pbB[3]])
                S.do("scalar", lambda e: e.copy(out=bon[:], in_=pb[3][:, 0:8]), reads=[pbB[3]], writes=[B_bon])
                if not isX:
                    S.do("vector", lambda e: e.scalar_tensor_tensor(out=AR[:, :, 0, :], in0=S1[:], scalar=-1.0, in1=gamx[:, :, 0:128], op0=ALU.mult, op1=ALU.mult),
                         reads=[B_S1, B_gam], writes=[B_AR])
                    S.do("vector", lambda e: e.tensor_tensor(out=AR[:, :, 1, :], in0=rT[:], in1=gamx[:, :, 1:129], op=ALU.mult), reads=[B_r, B_gam], writes=[B_AR])
                    S.do("scalar", lambda e: e.copy(out=ARbd[0:64, :, 0, :, :], in_=AR[0:64, :, :, :]), reads=[B_AR], writes=[B_ARbd])
                    S.do("scalar", lambda e: e.copy(out=ARbd[64:128, :, 1, :, :], in_=AR[64:128, :, :, :]), reads=[B_AR], writes=[B_ARbd])
                    S.do("vector", lambda e: e.tensor_tensor(out=BT[:], in0=S3[:], in1=ginv[:], op=ALU.mult), reads=[B_S3, B_gi], writes=[B_BT])
                    S.do("vector", lambda e: e.tensor_tensor(out=KhT[:], in0=S2[:], in1=ginv[:], op=ALU.mult), reads=[B_S2, B_gi], writes=[B_KhT])
                    for src, B_src2, dstk, B_dk, bk in ((BT, B_BT, Btok, B_Btok, 0), (KhT, B_KhT, Ktok, B_Ktok, 1)):
                        pbb = pb[bk][:].bitcast(BF16)
                        S.mm([("T", pbb[:, a * 128:(a + 1) * 128], src[:, a, :], identb[:]) for a in range(4)], reads=[B_src2, B_identb], writes=[pbB[bk]])
                        S.do("scalar", lambda e, pbb=pbb, dstk=dstk: e.copy(out=dstk[:], in_=pbb[:, 0:512]), reads=[pbB[bk]], writes=[B_dk])

            def rwkv_stage_b(c, isX, np_, gate_sb, B_gate, gamx, B_gam, Vtok, B_Vtok, Btok, B_Btok, Ktok, B_Ktok, bon, B_bon, AR, B_AR, ARbd, B_ARbd, BT, B_BT, KhT, B_KhT):
                rT = rT3[:, :, 0:128]
                vT = vT3[:, :, 0:128]
                if not isX:
                    if DLIM < 6:
                        return
                    S.do("vector", lambda e: e.tensor_tensor(out=Mg[:], in0=Mst[:], in1=gamx[:, :, 128:129].broadcast_to([128, 4, 64]), op=ALU.mult),
                         reads=[B_M, B_gam], writes=[B_Mg])
                    for a in range(4):
                        arbd = ARbd[:, a, :, :, :].rearrange("p h q t -> p (h q t)")
                        S.mm([dict(out=pb[2][:, :], lhsT=BT[:, a, :], rhs=arbd)], reads=[B_BT, B_ARbd], writes=[pbB[2]])
                        S.mm([dict(out=pb[3][:, :], lhsT=KhT[:, a, :], rhs=arbd)], reads=[B_KhT, B_ARbd], writes=[pbB[3]])
                        rm = rmask[:].rearrange("p a t -> p (a t)")
                        S.do("vector", lambda e, a=a, rm=rm: e.tensor_tensor(out=XS1[a][:], in0=pb[2][:, :], in1=rm, op=ALU.mult), reads=[pbB[2], Bc], writes=[B_XS1[a]])
                        S.do("vector", lambda e, a=a, rm=rm: e.tensor_tensor(out=XS2[a][:], in0=pb[3][:, :], in1=rm, op=ALU.mult), reads=[pbB[3], Bc], writes=[B_XS2[a]])
                        pbb = pb[6][:].bitcast(BF16)
                        S.mm([("T", pbb[:, hp * 128:(hp + 1) * 128], XS1[a][:, hp * 256:hp * 256 + 128], identb[:]) for hp in range(2)],
                             reads=[B_XS1[a], B_identb], writes=[pbB[6]])
                        S.do("scalar", lambda e, a=a, pbb=pbb: e.copy(out=L0[a][:].rearrange("p h t -> p (h t)"), in_=pbb[:, 0:256]), reads=[pbB[6]], writes=[B_L0[a]])
                        bw = (4, 5, 0, 1)[a]
                        specs = [dict(out=pb[bw][:, 0:128], lhsT=AR[:, a, 0, :], rhs=M0bd[:, a, :, :].rearrange("p h i -> p (h i)"), start=True, stop=False)]
                        for hp in range(2):
                            specs.append(dict(out=pb[bw][:, hp * 64:(hp + 1) * 64], lhsT=XS2[a][:, hp * 256:hp * 256 + 128],
                                              rhs=Vtok[:, a * 128 + hp * 64:a * 128 + (hp + 1) * 64], start=False, stop=(hp == 1)))
                        S.mm(specs, reads=[B_AR, B_M0bd, B_XS2[a], B_Vtok], writes=[pbB[bw]])
                        S.do("scalar", lambda e, a=a, bw=bw: e.copy(out=Ut[a][0][:], in_=pb[bw][:, 0:128]), reads=[pbB[bw]], writes=[B_U[a][0]])
                        yield
                    for k in range(7):
                        for a in range(4):
                            q0, q1 = k % 2, (k + 1) % 2
                            if k == 0:
                                Nk = [XS1[a][:, hp * 256:hp * 256 + 128] for hp in range(2)]
                                Lk = [L0[a][:, hp, :] for hp in range(2)]
                                rdN, rdL = [B_XS1[a]], [B_L0[a]]
                            else:
                                Nk = [LN[a][q0][:, hp * 256 + 128:hp * 256 + 256] for hp in range(2)]
                                Lk = [LN[a][q0][:, hp * 256:hp * 256 + 128] for hp in range(2)]
                                rdN, rdL = [B_LN[a][q0]], [B_LN[a][q0]]
                            bu = (4, 5, 0, 1)[a]
                            specs = [dict(out=pb[bu][:, 0:128], lhsT=identb[:], rhs=Ut[a][q0][:], start=True, stop=False)]
                            for hp in range(2):
                                specs.append(dict(out=pb[bu][:, hp * 64:(hp + 1) * 64], lhsT=Nk[hp], rhs=Ut[a][q0][:, hp * 64:(hp + 1) * 64], start=False, stop=(hp == 1)))
                            S.mm(specs, reads=[B_identb, B_U[a][q0]] + rdN, writes=[pbB[bu]])
                            S.do("scalar", lambda e, a=a, q1=q1, bu=bu: e.copy(out=Ut[a][q1][:], in_=pb[bu][:, 0:128]), reads=[pbB[bu]], writes=[B_U[a][q1]])
                            if k < 6:
                                bq = (2, 3, 6)[(k * 4 + a) % 3]
                                specs = []
                                for hp in range(2):
                                    specs.append(dict(out=pb[bq][:, hp * 256:hp * 256 + 128], lhsT=Nk[hp], rhs=Lk[hp]))
                                    specs.append(dict(out=pb[bq][:, hp * 256 + 128:hp * 256 + 256], lhsT=Lk[hp], rhs=Nk[hp]))
                                S.mm(specs, reads=rdN + rdL, writes=[pbB[bq]])
                                if a % 2 == 0:
                                    S.do("vector", lambda e, a=a, q1=q1, bq=bq: e.tensor_copy(out=LN[a][q1][:], in_=pb[bq][:, :]), reads=[pbB[bq]], writes=[B_LN[a][q1]])
                                else:
                                    S.do("scalar", lambda e, a=a, q1=q1, bq=bq: e.copy(out=LN[a][q1][:], in_=pb[bq][:, :]), reads=[pbB[bq]], writes=[B_LN[a][q1]])
                            if a % 2 == 1:
                                yield
                    for a in range(4):
                        Uf, B_Uf = Ut[a][1], B_U[a][1]
                        specs = [dict(out=pb[7][:, a * 128:(a + 1) * 128], lhsT=AR[:, a, 1, :], rhs=M0bd[:, a, :, :].rearrange("p h i -> p (h i)"), start=True, stop=False)]
                        for hp in range(2):
                            oc = pb[7][:, a * 128 + hp * 64:a * 128 + (hp + 1) * 64]
                            specs.append(dict(out=oc, lhsT=XS1[a][:, hp * 256 + 128:hp * 256 + 256], rhs=Uf[:, hp * 64:(hp + 1) * 64], start=False, stop=False))
                            specs.append(dict(out=oc, lhsT=XS2[a][:, hp * 256 + 128:hp * 256 + 256], rhs=Vtok[:, a * 128 + hp * 64:a * 128 + (hp + 1) * 64], start=False, stop=(hp == 1)))
                        S.mm(specs, reads=[B_AR, B_M0bd, B_XS1[a], B_XS2[a], B_Uf, B_Vtok], writes=[pbB[7]])
                        bs = (4, 5, 0, 1)[a]
                        S.mm([dict(out=pb[bs][:, 0:128], lhsT=Btok[:, a * 128:(a + 1) * 128], rhs=Uf[:], start=True, stop=False),
                              dict(out=pb[bs][:, 0:128], lhsT=Ktok[:, a * 128:(a + 1) * 128], rhs=Vtok[:, a * 128:(a + 1) * 128], start=False, stop=True)],
                             reads=[B_Btok, B_Ktok, B_Vtok, B_Uf], writes=[pbB[bs]])
                        for hp in range(2):
                            rows = slice(hp * 64, (hp + 1) * 64)
                            S.do("vector", lambda e, a=a, hp=hp, rows=rows, bs=bs: e.scalar_tensor_tensor(out=Mst[rows, a, :], in0=pb[bs][rows, hp * 64:(hp + 1) * 64],
                                                                                                      scalar=gamx[rows, a, 128:129], in1=Mg[rows, a, :],
                                                                                                      op0=ALU.mult, op1=ALU.add),
                                 reads=[pbB[bs], B_gam, B_Mg], writes=[B_M])
                    S.do("scalar", lambda e: e.copy(out=M0bd[0:64, :, 0, :], in_=Mst[0:64, :, :]), reads=[B_M], writes=[B_M0bd])
                    S.do("scalar", lambda e: e.copy(out=M0bd[64:128, :, 1, :], in_=Mst[64:128, :, :]), reads=[B_M], writes=[B_M0bd])
                    if c == NB - 1:
                        S.mm([("T", pb[6][0:64, a * 128:(a + 1) * 128], Mst[:, a, :], ident[:]) for a in range(4)], reads=[B_M, B_ident], writes=[pbB[6]])
                        S.do("vector", lambda e: e.tensor_copy(out=ysq[0:64, :], in_=pb[6][0:64, :]), reads=[pbB[6]], writes=[B_ysq])
                        S.dma("sync", wkv_p_d[8 * hh:8 * hh + 8].rearrange("h i j -> i h j"), ysq[0:64, :].rearrange("p (h j) -> p h j", h=8), reads=[B_ysq])
                else:
                    for b in range(NS):
                        Snat = ysq
                        S.dma("sync", Snat[0:64, :].rearrange("p (h j) -> p h j", h=8), swkv_d[b, 8 * hh:8 * hh + 8].rearrange("h i j -> i h j"), writes=[B_ysq])
                        S.mm([("T", pb[6][:, a * 64:(a + 1) * 64], Snat[0:64, a * 128:(a + 1) * 128], ident[0:64, 0:64]) for a in range(4)],
                             reads=[B_ysq, B_ident], writes=[pbB[6]])
                        S.do("vector", lambda e: e.tensor_copy(out=Mst[:].rearrange("p a i -> p (a i)"), in_=pb[6][:, 0:256]), reads=[pbB[6]], writes=[B_M])
                        for a in range(4):
                            S.do("vector", lambda e, a=a, b=b: e.tensor_scalar(out=tt[:, 0:128], in0=bones[:], scalar1=S1[:, a, b:b + 1], scalar2=-1.0, op0=ALU.mult, op1=ALU.mult),
                                 reads=[B_S1, Bc], writes=[B_tt])
                            S.do("vector", lambda e, a=a, b=b: e.tensor_scalar(out=tt[:, 128:192], in0=i64x2[:], scalar1=vT[:, a, b:b + 1], scalar2=None, op0=ALU.mult),
                                 reads=[B_v, Bc], writes=[B_tt])
                            bu = 4 + a % 2
                            S.mm([dict(out=pb[bu][:, 0:64], lhsT=tt[:, 0:128], rhs=Mst[:, a, :]),
                                  dict(out=pb[bu][:, 64:128], lhsT=bones[:], rhs=tt[:, 128:192])], reads=[B_tt, B_M, Bc], writes=[pbB[bu]])
                            S.do("vector", lambda e, a=a, b=b: e.tensor_scalar(out=Mg[:, a, :], in0=Mst[:, a, :], scalar1=gamx[:, a, 1 + b:2 + b], scalar2=None, op0=ALU.mult),
                                 reads=[B_M, B_gam], writes=[B_Mg])
                            S.do("vector", lambda e, a=a, b=b, bu=bu: e.scalar_tensor_tensor(out=Mg[:, a, :], in0=pb[bu][:, 0:64], scalar=S3[:, a, b:b + 1], in1=Mg[:, a, :],
                                                                                         op0=ALU.mult, op1=ALU.add),
                                 reads=[pbB[bu], B_S3], writes=[B_Mg])
                            S.do("vector", lambda e, a=a, b=b, bu=bu: e.scalar_tensor_tensor(out=Mg[:, a, :], in0=pb[bu][:, 64:128], scalar=S2[:, a, b:b + 1], in1=Mg[:, a, :],
                                                                                         op0=ALU.mult, op1=ALU.add),
                                 reads=[pbB[bu], B_S2], writes=[B_Mg])
                        S.do("vector", lambda e: e.memset(bv[:], 0.0), writes=[B_bv])
                        for hp in range(2):
                            rows = slice(hp * 64, (hp + 1) * 64)
                            S.do("vector", lambda e, b=b, hp=hp, rows=rows: e.tensor_copy(out=bv[rows, :].rearrange("p (a h q) -> p a h q", a=4, h=2)[:, :, hp, b:b + 1],
                                                                                      in_=rT[rows, :, b:b + 1]), reads=[B_r], writes=[B_bv])
                        specs = []
                        for a in range(4):
                            for hp in range(2):
                                lhsT = bv[:, :].rearrange("p (a h q) -> p a h q", a=4, h=2)[:, a, hp, 0:NS]
                                specs.append(dict(out=pb[7][0:NS, a * 128 + hp * 64:a * 128 + (hp + 1) * 64], lhsT=lhsT, rhs=Mg[:, a, :],
                                                  start=(b == 0 and a == 0 and hp == 0), stop=(b == NS - 1 and a == 3 and hp == 1)))
                        S.mm(specs, reads=[B_bv, B_Mg], writes=[pbB[7]])
                        S.mm([("T", pb[6][0:64, a * 128:(a + 1) * 128], Mg[:, a, :], ident[:]) for a in range(4)], reads=[B_Mg, B_ident], writes=[pbB[6]])
                        S.do("vector", lambda e: e.tensor_copy(out=tt[0:64, :], in_=pb[6][0:64, :]), reads=[pbB[6]], writes=[B_tt])
                        S.dma("sync", wkv_s_d[b, 8 * hh:8 * hh + 8].rearrange("h i j -> i h j"), tt[0:64, :].rearrange("p (h j) -> p h j", h=8), reads=[B_tt])
                if DLIM < 7:
                    return
                yield
                y3 = pb[7][0:np_, :].rearrange("p (h i) -> p h i", h=8)

                def b8(c0_):
                    return stt[0:np_, c0_:c0_ + 8].unsqueeze(2).broadcast_to([np_, 8, 64])

                def t3(tile_):
                    return tile_[0:np_, :].rearrange("p (h i) -> p h i", h=8)
                S.do("vector", lambda e: e.tensor_reduce(out=stt[0:np_, 0:8], in_=y3, axis=AX.X, op=ALU.add), reads=[pbB[7]], writes=[B_st])
                S.do("scalar", lambda e: e.activation(out=ysq[0:np_, :], in_=pb[7][0:np_, :], func=AF.Square), reads=[pbB[7], B_st], writes=[B_ysq])
                S.do("vector", lambda e: e.tensor_reduce(out=stt[0:np_, 8:16], in_=t3(ysq), axis=AX.X, op=ALU.add), reads=[B_ysq], writes=[B_st])
                S.do("vector", lambda e: e.tensor_scalar(out=stt[0:np_, 16:24], in0=stt[0:np_, 0:8], scalar1=1.0 / 64, scalar2=None, op0=ALU.mult), reads=[B_st], writes=[B_st])
                S.do("vector", lambda e: e.tensor_tensor(out=stt[0:np_, 24:32], in0=stt[0:np_, 16:24], in1=stt[0:np_, 16:24], op=ALU.mult), reads=[B_st], writes=[B_st])
                S.do("vector", lambda e: e.scalar_tensor_tensor(out=stt[0:np_, 32:40], in0=stt[0:np_, 8:16], scalar=1.0 / 64, in1=stt[0:np_, 24:32], op0=ALU.mult, op1=ALU.subtract),
                     reads=[B_st], writes=[B_st])
                S.do("vector", lambda e: e.tensor_scalar(out=stt[0:np_, 32:40], in0=stt[0:np_, 32:40], scalar1=64e-5, scalar2=None, op0=ALU.add), reads=[B_st], writes=[B_st])
                S.do("scalar", lambda e: e.activation(out=stt[0:np_, 32:40], in_=stt[0:np_, 32:40], func=AF.Sqrt), reads=[B_st], writes=[B_st])
                S.do("vector", lambda e: e.reciprocal(out=stt[0:np_, 40:48], in_=stt[0:np_, 32:40]), reads=[B_st], writes=[B_st])
                S.do("vector", lambda e: e.tensor_tensor(out=t3(tt), in0=y3, in1=b8(16), op=ALU.subtract), reads=[pbB[7], B_st], writes=[B_tt])
                S.do("vector", lambda e: e.tensor_tensor(out=t3(tt), in0=t3(tt), in1=b8(40), op=ALU.mult), reads=[B_tt, B_st], writes=[B_tt])
                S.do("vector", lambda e: e.tensor_tensor(out=tt[0:np_, :], in0=tt[0:np_, :], in1=gnw[0:np_, :], op=ALU.mult), reads=[B_tt, Bc], writes=[B_tt])
                S.do("vector", lambda e: e.tensor_tensor(out=tt[0:np_, :], in0=tt[0:np_, :], in1=gnb[0:np_, :], op=ALU.add), reads=[B_tt, Bc], writes=[B_tt])
                S.do("vector", lambda e: e.tensor_tensor(out=t3(bv), in0=t3(Vtok), in1=bon[0:np_, :].unsqueeze(2).broadcast_to([np_, 8, 64]), op=ALU.mult),
                     reads=[B_Vtok, B_bon], writes=[B_bv])
                S.do("vector", lambda e: e.tensor_tensor(out=tt[0:np_, :], in0=tt[0:np_, :], in1=bv[0:np_, :], op=ALU.add), reads=[B_tt, B_bv], writes=[B_tt])
                S.do("vector", lambda e: e.tensor_tensor(out=ob[0:np_, :], in0=tt[0:np_, :], in1=gate_sb[0:np_, :], op=ALU.mult), reads=[B_tt, B_gate], writes=[B_ob])
                pbb = pb[6][:].bitcast(BF16)
                S.mm([("T", pbb[:, a * 128:a * 128 + np_], ob[0:np_, a * 128:(a + 1) * 128], identb[0:np_, 0:np_]) for a in range(4)], reads=[B_ob, B_identb], writes=[pbB[6]])
                col0 = T if isX else c * 128
                S.do("scalar", lambda e, pbb=pbb, col0=col0, np_=np_: e.copy(out=o_bT[:, a0g:a0g + 4, col0:col0 + np_],
                                                                          in_=pbb[:, 0:512].rearrange("p (a t) -> p a t", a=4)[:, :, 0:np_]),
                     reads=[pbB[6]], writes=[B_obT])
                if "rw" in debug and hh == 0 and c in (0, 1, NB):
                    for nm, tl, Bt in (("rT", rT, B_r), ("kap", S1, B_S1), ("k2", S2, B_S2), ("aT", aT, B_a), ("sg", sgT, B_sg)):
                        o_ = dbg_out(f"{nm}_{c}", [128, 4, 128])
                        S.dma("sync", o_, tl[:], reads=[Bt])
                    o_ = dbg_out(f"tt_{c}", [128, 512])
                    S.dma("sync", o_, tt[:], reads=[B_tt])
                    o_ = dbg_out(f"gate_{c}", [128, 512])
                    S.dma("sync", o_, gate_sb[:], reads=[B_gate])
            def interleave(g1, g2):
                gens = [g for g in (g1, g2) if g is not None]
                while gens:
                    for g in list(gens):
                        try:
                            next(g)
                        except StopIteration:
                            gens.remove(g)

            for c in range(NB + 1):
                interleave(rwkv_block(c, "A"), None)
                interleave(rwkv_block(c, "B"), None)
            S.barrier()

        phW = st.enter_context(ExitStack())
        Wsh = sb(phW, "Wsh", [128, 8 * 1824], BF16)
        Ws_g = Wsh[:, :].rearrange("p (k c) -> p k c", k=8)
        Wpb_v = Wsh[:, 0:8192].rearrange("p (k c) -> p k c", k=8)
        Wpa_v = Wsh[:, 8192:12288].rearrange("p (k c) -> p k c", k=4)
        B_Ws_g = Buf()

        def load_Wp():
            S.dma("gpsimd", Wpb_v, w_pb_d.rearrange("(kc p) c -> p kc c", p=128), writes=[B_Ws_g])
            S.dma("gpsimd", Wpa_v, w_pa_d.rearrange("(kc p) c -> p kc c", p=128), writes=[B_Ws_g])

        def load_Ws(hh_):
            for j, base in enumerate((0, 1024, 2048)):
                S.dma("gpsimd", Ws_g[:, :, j * 512:(j + 1) * 512], w_in_v[:, :, A_COLS + base + hh_ * 512:A_COLS + base + hh_ * 512 + 512], writes=[B_Ws_g])
            if hh_ == 0:
                S.dma("gpsimd", Ws_g[:, :, 1536:1824], w_in_v[:, :, A_COLS + 3072:A_COLS + 3360], writes=[B_Ws_g])
        for hh in range(2):
            if "stopC" in debug or PH_STOP == "B":
                break
            with ExitStack() as ph:
                rwkv_pass(hh, ph)


        def wload(dst_tile, src_ap, B_):
            S.dma("gpsimd", dst_tile, src_ap, writes=[B_])

        try:
          if "stopD" not in debug and PH_STOP not in ("B", "D"):
            S.barrier()
            phWo = ExitStack()
            Wout = sb(phWo, "Wout", [128, 8, 1024], BF16)
            B_Wout = Buf()
            with ExitStack() as ph:
                Wg = sb(ph, "Wg", [128, 8, 2048], BF16)
                Wpa, Wpb = Wpa_v, Wpb_v
                B_W = Buf()
                for q in range(2):
                    wload(Wg[:, :, q * 1024:(q + 1) * 1024], w_in_v[:, :, A_COLS + SH + q * 1024:A_COLS + SH + (q + 1) * 1024], B_W)
                wload(Wout[:], w_out_d.rearrange("(kc p) c -> p kc c", p=128), B_Wout)
                oaT2 = sb(ph, "oaT", [128, 4, T + 128], BF16)
                B_oaT2 = Buf("oaT2")
                S.dma("sync", oaT2[:], oaT_d, writes=[B_oaT2])
                mtmp = sb(ph, "mtmp", [128, 8, 512], BF16)
                B_mtmp = Buf()
                sga = [sb(ph, f"sga{i}", [128, 512], F32) for i in range(2)]
                sgb = [sb(ph, f"sgb{i}", [128, 512], F32) for i in range(2)]
                B_sga, B_sgb = [Buf(), Buf()], [Buf(), Buf()]
                for ch in range(5):
                    if E1LIM < 1 or (E1LIM < 3 and ch >= 1):
                        continue
                    n_ = 512 if ch < 4 else 128
                    hc0 = C0 + ch * 512 if ch < 4 else SC0
                    oc0 = ch * 512 if ch < 4 else T
                    for m in range(8):
                        u = (ch * 8 + m) % 2
                        bks = [4 * u + j for j in range(4)]
                        S.mm([dict(out=pb[bks[0]][:, 0:n_], lhsT=Wg[:, kc, m * 128:(m + 1) * 128], rhs=hT[:, kc, hc0:hc0 + n_], start=(kc == 0), stop=(kc == 7)) for kc in range(8)],
                             reads=[B_W, B_hT], writes=[pbB[bks[0]]])
                        S.mm([dict(out=pb[bks[1]][:, 0:n_], lhsT=Wg[:, kc, 1024 + m * 128:1024 + (m + 1) * 128], rhs=hT[:, kc, hc0:hc0 + n_], start=(kc == 0), stop=(kc == 7)) for kc in range(8)],
                             reads=[B_W, B_hT], writes=[pbB[bks[1]]])
                        S.mm([dict(out=pb[bks[2]][:, 0:n_], lhsT=Wpa[:, kc, m * 128:(m + 1) * 128], rhs=oaT2[:, kc, oc0:oc0 + n_], start=(kc == 0), stop=(kc == 3)) for kc in range(4)],
                             reads=[B_Ws_g, B_oaT2], writes=[pbB[bks[2]]])
                        S.mm([dict(out=pb[bks[3]][:, 0:n_], lhsT=Wpb[:, kc, m * 128:(m + 1) * 128], rhs=o_bT[:, kc, oc0:oc0 + n_], start=(kc == 0), stop=(kc == 7)) for kc in range(8)],
                             reads=[B_Ws_g, B_obT], writes=[pbB[bks[3]]])
                        if E1LIM < 2:
                            continue
                        S.do("scalar", lambda e, u=u, b_=bks[0], n_=n_: e.activation(out=sga[u][:, 0:n_], in_=pb[b_][:, 0:n_], func=AF.Sigmoid), reads=[pbB[bks[0]]], writes=[B_sga[u]])
                        S.do("scalar", lambda e, u=u, b_=bks[1], n_=n_: e.activation(out=sgb[u][:, 0:n_], in_=pb[b_][:, 0:n_], func=AF.Sigmoid), reads=[pbB[bks[1]]], writes=[B_sgb[u]])
                        S.do("vector", lambda e, u=u, b_=bks[2], n_=n_: e.tensor_tensor(out=sga[u][:, 0:n_], in0=sga[u][:, 0:n_], in1=pb[b_][:, 0:n_], op=ALU.mult),
                             reads=[pbB[bks[2]], B_sga[u]], writes=[B_sga[u]])
                        S.do("vector", lambda e, u=u, b_=bks[3], n_=n_: e.tensor_tensor(out=sgb[u][:, 0:n_], in0=sgb[u][:, 0:n_], in1=pb[b_][:, 0:n_], op=ALU.mult),
                             reads=[pbB[bks[3]], B_sgb[u]], writes=[B_sgb[u]])
                        S.do("vector", lambda e, u=u, m=m, n_=n_: e.tensor_tensor(out=mtmp[:, m, 0:n_], in0=sga[u][:, 0:n_], in1=sgb[u][:, 0:n_], op=ALU.add),
                             reads=[B_sga[u], B_sgb[u]], writes=[B_mtmp])
                    S.do("scalar", lambda e, oc0=oc0, n_=n_: e.copy(out=o_bT[:, :, oc0:oc0 + n_], in_=mtmp[:, :, 0:n_]), reads=[B_mtmp], writes=[B_obT])
                S.barrier()
            if ELIM < 2:
                raise StopIteration
            with ExitStack() as ph:
                B_W = B_Wout

                def prod_e2(i, xt, B_xt):
                    oc0 = i * 128 if i < NB else T
                    for hf in range(2):
                        bk = 2 * (i % 2) + hf
                        S.mm([dict(out=pb[bk][:, :], lhsT=o_bT[:, kc, oc0:oc0 + 128], rhs=Wout[:, kc, hf * 512:(hf + 1) * 512], start=(kc == 0), stop=(kc == 7)) for kc in range(8)],
                             reads=[B_obT, B_W], writes=[pbB[bk]])
                        S.do("vector", lambda e, bk=bk, hf=hf, xt=xt: e.tensor_tensor(out=xt[:, hf * 512:(hf + 1) * 512], in0=xt[:, hf * 512:(hf + 1) * 512], in1=pb[bk][:, :], op=ALU.add),
                             reads=[pbB[bk]], writes=[B_xt])
                    S.dma("sync", x1_d[oc0:oc0 + 128, :], xt[:], reads=[B_xt])
                blocks = [(x_d[i * 128:(i + 1) * 128, :], 128, C0 + i * 128) for i in range(NB)]
                blocks.append((xs_d, NS, SC0))
                rmsnorm_to_T(ph, blocks, norm_ffn_d, hT, B_hT, "E2", producer=prod_e2)
                S.barrier()
            phWo.close()
            phW.close()
            if ELIM < 3:
                raise StopIteration
            with ExitStack() as ph:
                actT2 = sb(ph, "actT2", [128, 14, T + 128], BF16)
                B_act = Buf()

                def act_tile(f):
                    if f < 8:
                        return o_bT[:, f, :]
                    return actT2[:, f - 8, :]
                Wd = sb(ph, "Wd", [128, 22, 1024], BF16)
                B_Wd = Buf()
                wd_v = w_down_d.rearrange("(f p) c -> p f c", p=128)
                with ExitStack() as ph3:
                    WG = [sb(ph3, f"WG{i}", [128, 8, 128], BF16) for i in range(2)]
                    WU = [sb(ph3, f"WU{i}", [128, 8, 128], BF16) for i in range(2)]
                    B_WG, B_WU = [Buf(), Buf()], [Buf(), Buf()]
                    sgl = [sb(ph3, f"sgl{i}", [128, 512], F32) for i in range(2)]
                    B_sgl = [Buf(), Buf()]
                    wg_v = w_gate_d.rearrange("(kc p) c -> p kc c", p=128)
                    wu_v = w_up_d.rearrange("(kc p) c -> p kc c", p=128)
                    for f in range(22):
                        s_ = f % 2
                        wload(WG[s_][:], wg_v[:, :, f * 128:(f + 1) * 128], B_WG[s_])
                        wload(WU[s_][:], wu_v[:, :, f * 128:(f + 1) * 128], B_WU[s_])
                        if 2 <= f < 13:
                            q = f - 2
                            wload(Wd[:, 2 * q:2 * q + 2, :], wd_v[:, 2 * q:2 * q + 2, :], B_Wd)
                        for ch in range(5):
                            n_ = 512 if ch < 4 else 128
                            hc0 = C0 + ch * 512 if ch < 4 else SC0
                            oc0 = ch * 512 if ch < 4 else T
                            u = (f * 5 + ch) % 2
                            bg, bu_ = 2 * u, 2 * u + 1
                            S.mm([dict(out=pb[bg][:, 0:n_], lhsT=WG[s_][:, kc, :], rhs=hT[:, kc, hc0:hc0 + n_], start=(kc == 0), stop=(kc == 7)) for kc in range(8)],
                                 reads=[B_WG[s_], B_hT], writes=[pbB[bg]])
                            S.mm([dict(out=pb[bu_][:, 0:n_], lhsT=WU[s_][:, kc, :], rhs=hT[:, kc, hc0:hc0 + n_], start=(kc == 0), stop=(kc == 7)) for kc in range(8)],
                                 reads=[B_WU[s_], B_hT], writes=[pbB[bu_]])
                            S.do("scalar", lambda e, u=u, bg=bg, n_=n_: e.activation(out=sgl[u][:, 0:n_], in_=pb[bg][:, 0:n_], func=AF.Silu), reads=[pbB[bg]], writes=[B_sgl[u]])
                            S.do("vector", lambda e, u=u, bu_=bu_, n_=n_, f=f, oc0=oc0: e.tensor_tensor(out=act_tile(f)[:, oc0:oc0 + n_], in0=sgl[u][:, 0:n_], in1=pb[bu_][:, 0:n_], op=ALU.mult),
                                 reads=[B_sgl[u], pbB[bu_]], writes=[B_act])
                    S.barrier()
                if ELIM < 4:
                    raise StopIteration
                B_W = B_Wd

                def prod_e4(i, xt, B_xt):
                    oc0 = i * 128 if i < NB else T
                    for hf in range(2):
                        bk = 2 * (i % 2) + hf
                        S.mm([dict(out=pb[bk][:, :], lhsT=act_tile(f)[:, oc0:oc0 + 128], rhs=Wd[:, f, hf * 512:(hf + 1) * 512], start=(f == 0), stop=(f == 21)) for f in range(22)],
                             reads=[B_act, B_W], writes=[pbB[bk]])
                        S.do("vector", lambda e, bk=bk, hf=hf, xt=xt: e.tensor_tensor(out=xt[:, hf * 512:(hf + 1) * 512], in0=xt[:, hf * 512:(hf + 1) * 512], in1=pb[bk][:, :], op=ALU.add),
                             reads=[pbB[bk]], writes=[B_xt])
                blocks = [(x1_d[i * 128:(i + 1) * 128, :], 128, 0) for i in range(NB)]
                blocks.append((x1_d[T:T + 128, :], 128, 0))
                outs_ = [(y_d[i * 128:(i + 1) * 128, :], 128) for i in range(NB)] + [(ys_d, NS)]
                rmsnorm_to_T(ph, blocks, norm_final_d, None, None, "E4", producer=prod_e4, out_rows=outs_)
                S.barrier()
        except StopIteration:
            S.barrier()
        if "obT" in debug:
            o_ = dbg_out("obT", [128, 8, T + 128])
            with ExitStack() as phd:
                tmpd = sb(phd, "dbgtmp2", [128, 8, T + 128], F32)
                Bt = Buf()
                S.do("vector", lambda e: e.tensor_copy(out=tmpd[:], in_=o_bT[:]), reads=[B_obT], writes=[Bt])
                S.dma("sync", o_, tmpd[:], reads=[Bt])
                S.barrier()
        S.final_wait()
        S.emit()
    print("instruction counts:", S.ninst, "sem incs:", S.cnt, flush=True)
    return nc, list(dbg.keys())


_CACHE = {}


def _get_nc(debug=()):
    key = tuple(debug)
    if key not in _CACHE:
        _CACHE[key] = build(debug)
    return _CACHE[key]


def kernel(_debug=(), **inputs):
    nc, dbg_names = _get_nc(_debug)
    consts = make_consts()
    f32 = lambda a: np.ascontiguousarray(np.asarray(a, dtype=np.float32))
    in_maps = []
    for c in range(8):
        m = dict(consts)
        m["x"] = f32(inputs["x_prompt"][c])
        m["xs"] = f32(inputs["x_sample"][4 * c:4 * c + 4, 0, :])
        m["w_in"] = f32(inputs["w_in"][0])
        m["norm_mix"] = f32(inputs["norm_mix"])
        m["w_proj_a"] = f32(inputs["w_proj_a"][0])
        m["w_proj_b"] = f32(inputs["w_proj_b"][0])
        m["w_out"] = f32(inputs["w_out"][0])
        m["norm_ffn"] = f32(inputs["norm_ffn"])
        m["w_gate"] = f32(inputs["w_gate"][0])
        m["w_up"] = f32(inputs["w_up"][0])
        m["w_down"] = f32(inputs["w_down"][0])
        m["norm_final"] = f32(inputs["norm_final"]).reshape(1, D)
        m["sshift"] = f32(inputs["state_shift"][0, 4 * c:4 * c + 4])
        m["swkv"] = f32(inputs["state_wkv"][0, 4 * c:4 * c + 4])
        mu = np.zeros(27 * 128, np.float32)
        mu[:SH] = np.asarray(inputs["mu_shift"], np.float32)[0]
        m["mu_c"] = np.ascontiguousarray(mu.reshape(27, 128).T)
        m["cols8"] = np.ascontiguousarray(np.stack([np.asarray(inputs[k_], np.float32).reshape(8, 128).T for k_ in ("w0", "a0", "k_k", "k_a", "r_k")], 1))
        m["w2a2"] = np.ascontiguousarray(np.concatenate([f32(inputs["w2"][0]), f32(inputs["a2"][0])], 0))
        m["g2"] = f32(inputs["g2"][0])
        m["gn_w"] = f32(inputs["gn_w"])
        m["gn_b"] = f32(inputs["gn_b"])
        for g in range(3):
            ck = inputs[f"cache_kv_g{g + 1}"][0, 4 * c:4 * c + 4]
            m[f"ck{g + 1}"] = f32(ck).reshape(NS, ck.shape[1], 1024)
        in_maps.append(m)
    res = run_bass_kernel_spmd(nc, in_maps, core_ids=list(range(8)))
    R = res.results
    if _debug:
        return R
    def cat(name, shape=None):
        a = np.concatenate([np.asarray(R[c][name]) for c in range(8)], 0)
        return a
    y_prompt = np.stack([np.asarray(R[c]["y"]) for c in range(8)], 0).astype(np.float32)
    y_sample = cat("ys").reshape(32, 1, D).astype(np.float32)
    kvp = [np.stack([np.asarray(R[c][f"kv{g + 1}_p"]) for c in range(8)], 0).reshape(1, 8, -1, 2, 8, 64).astype(np.float32) for g in range(3)]
    shp = cat("shift_p").reshape(1, 8, SH).astype(np.float32)
    wkvp = np.stack([np.asarray(R[c]["wkv_p"]) for c in range(8)], 0).reshape(1, 8, 16, 64, 64).astype(np.float32)
    kvs = [cat(f"kv{g + 1}_s").reshape(1, 32, 1, 2, 8, 64).astype(np.float32) for g in range(3)]
    shs = cat("shift_s").reshape(1, 32, SH).astype(np.float32)
    wkvs = cat("wkv_s").reshape(1, 32, 16, 64, 64).astype(np.float32)
    return (y_prompt, y_sample, kvp[0], kvp[1], kvp[2], shp, wkvp, kvs[0], kvs[1], kvs[2], shs, wkvs)
```

```python
import numpy as np
from contextlib import ExitStack
import concourse.bass as bass
import concourse.mybir as mybir
from concourse.bass_utils import run_bass_kernel_spmd

F32 = mybir.dt.float32
BF16 = mybir.dt.bfloat16
ALU = mybir.AluOpType
AF = mybir.ActivationFunctionType
AX = mybir.AxisListType

import os
ENGS = ("tensor", "vector", "scalar", "gpsimd", "sync")
DLIM = int(os.environ.get("DLIM", "9"))
DNOX = int(os.environ.get("DNOX", "0"))
DNB = int(os.environ.get("DNB", "16"))
ELIM = int(os.environ.get("ELIM", "9"))
PH_STOP = os.environ.get("PH_STOP", "")
E1LIM = int(os.environ.get("E1LIM", "9"))
EPOCH = 12000

D = 1024
T = 2048
NB = 16
NS = 4
A_COLS = 4608
SH = 3360
DFF = 2816
GROUPS = ((128, 1), (512, 4), (2048, 16))
C0 = 2
HTW = C0 + T + 128
SC0 = C0 + T


class Buf:
    __slots__ = ("w", "r", "name")

    def __init__(self, name=""):
        self.w = None
        self.r = {}
        self.name = name


def _hkey(h):
    if h[0] == "dma":
        return ("dma", h[3]), h[2]
    return (h[0], h[1]), h[2]


class Sched:
    def __init__(self, nc, stack):
        self.nc = nc
        self.stack = stack
        self.q = {e: [] for e in ENGS}
        self.cnt = {e: 0 for e in ENGS}
        self.sems = {e: [] for e in ENGS}
        self.waited = {e: {} for e in ENGS}
        self.pools = {}
        self.outstanding_dma = []
        self.ninst = {e: 0 for e in ENGS}

    def _sem(self, eng, epoch):
        while len(self.sems[eng]) <= epoch:
            s = self.stack.enter_context(self.nc.semaphore(f"s_{eng}_{len(self.sems[eng])}"))
            self.sems[eng].append(s)
        return self.sems[eng][epoch]

    def _emit_waits(self, eng, deps):
        w = self.waited[eng]
        for d in deps:
            if d is None:
                continue
            key, n = _hkey(d)
            if w.get(key, 0) >= n:
                continue
            w[key] = n
            if d[0] == "dma":
                sem = d[1]
            else:
                sem = self._sem(d[0], d[1])
            self.q[eng].append(lambda e, sem=sem, n=n: e.wait_ge(sem, n))

    def op(self, eng, fn, deps=()):
        self._emit_waits(eng, deps)
        c = self.cnt[eng]
        epoch = c // EPOCH
        n = c % EPOCH + 1
        self.cnt[eng] = c + 1
        sem = self._sem(eng, epoch)
        self.q[eng].append(lambda e, fn=fn, sem=sem: fn(e).then_inc(sem, 1))
        self.ninst[eng] += 1
        return (eng, epoch, n)

    def op_noinc(self, eng, fn):
        self.q[eng].append(lambda e, fn=fn: fn(e))
        self.ninst[eng] += 1

    def dma_raw(self, eng, out, in_, deps, **kw):
        if eng not in self.pools:
            k = 4 if eng == "gpsimd" else 10
            self.pools[eng] = {"sems": [self.stack.enter_context(self.nc.semaphore(f"d_{eng}_{i}")) for i in range(k)],
                               "vals": [0] * k, "last": [None] * k, "i": 0}
        p = self.pools[eng]
        i = p["i"]
        p["i"] = (i + 1) % len(p["sems"])
        deps = list(deps) + [p["last"][i]]
        self._emit_waits(eng, deps)
        p["vals"][i] += 16
        sem, val = p["sems"][i], p["vals"][i]
        self.q[eng].append(lambda e, out=out, in_=in_, sem=sem, kw=kw: e.dma_start(out=out, in_=in_, **kw).then_inc(sem, 16))
        h = ("dma", sem, val, (eng, i))
        p["last"][i] = h
        self.outstanding_dma.append(h)
        self.ninst[eng] += 1
        return h

    @staticmethod
    def _deps(reads, writes):
        deps = []
        for b in reads:
            deps.append(b.w)
        for b in writes:
            deps.append(b.w)
            deps.extend(b.r.values())
        return deps

    @staticmethod
    def _update(h, reads, writes):
        key, n = _hkey(h)
        for b in reads:
            old = b.r.get(key)
            if old is None or _hkey(old)[1] < n:
                b.r[key] = h
        for b in writes:
            b.w = h
            b.r = {}

    def do(self, eng, fn, reads=(), writes=()):
        h = self.op(eng, fn, self._deps(reads, writes))
        self._update(h, reads, writes)
        return h

    def mm(self, specs, reads=(), writes=()):
        self._emit_waits("tensor", [d_ for d_ in self._deps(reads, writes) if d_ is not None and d_[0] != "tensor"])

        def mk(sp):
            if isinstance(sp, tuple):
                _, o, i, idn = sp
                return lambda e: e.transpose(o, i, idn)
            return lambda e: e.matmul(sp["out"], lhsT=sp["lhsT"], rhs=sp["rhs"], start=sp.get("start", True),
                                      stop=sp.get("stop", True), skip_group_check=True)
        for sp in specs[:-1]:
            self.op_noinc("tensor", mk(sp))
        h = self.op("tensor", mk(specs[-1]))
        self._update(h, reads, writes)
        return h

    def dma(self, eng, out, in_, reads=(), writes=(), **kw):
        h = self.dma_raw(eng, out, in_, self._deps(reads, writes), **kw)
        self._update(h, reads, writes)
        return h

    def barrier(self):
        hs = []
        for e in ENGS:
            c = self.cnt[e]
            if c:
                hs.append((e, (c - 1) // EPOCH, (c - 1) % EPOCH + 1))
        hs += self.outstanding_dma
        self.outstanding_dma = []
        for e in ENGS:
            self._emit_waits(e, hs)

    def final_wait(self):
        self._emit_waits("sync", self.outstanding_dma)

    def emit(self):
        with self.nc.Block() as block:
            for name in ENGS:
                lst = self.q[name]
                if not lst:
                    continue

                def body(e, lst=lst):
                    for f in lst:
                        f(e)
                getattr(block, name)(body)


def make_consts():
    c = {}
    c["ident"] = np.eye(128, dtype=np.float32)
    k = np.arange(128)[:, None]
    q = np.arange(128)[None, :]
    mP = (k >= q).astype(np.float32)
    mC = (k <= q).astype(np.float32)
    c["amask"] = np.stack([mP, mC, mP, mC], 1).copy()
    half = 32
    freqs = (10000.0 ** (-np.arange(half, dtype=np.float32) / half)).astype(np.float32)

    def tab(pos):
        ang = pos.astype(np.float32)[..., None] * freqs
        return np.cos(ang).astype(np.float32), np.sin(ang).astype(np.float32)
    rc = np.zeros((3, 128, NB, 32), np.float32)
    rs = np.zeros((3, 128, NB, 2, 32), np.float32)
    for g, (_, d) in enumerate(GROUPS):
        nb = NB // d
        for r in range(d):
            for n in range(nb):
                sb = r * nb + n
                pos = r + d * (128 * n + np.arange(128))
                co, si = tab(pos)
                rc[g, :, sb] = co
                rs[g, :, sb, 0] = -si
                rs[g, :, sb, 1] = si
    c["rope_c"] = rc
    c["rope_s"] = rs
    co, si = tab(np.array([16384]))
    c["rope_cs"] = np.concatenate([co[0], -si[0], si[0]])[None, :].copy()
    ep = np.zeros((128, 4, 4), np.float32)
    for p in range(4):
        ep[:, p, p] = 1.0
    c["epair"] = ep
    selb = np.zeros((NS, NS, 128), np.float32)
    selp = np.zeros((4, 4, 64), np.float32)
    for b in range(4):
        selb[b, b, :] = 1.0
        selp[b, b, :] = 1.0
    c["selb"] = selb
    c["selp"] = selp
    ss_, tt_ = np.arange(128)[:, None], np.arange(128)[None, :]
    su = (ss_ < tt_).astype(np.float32)
    iu = (ss_ <= tt_).astype(np.float32)
    c["rmask"] = np.stack([su, iu, su, iu], 1).copy()
    c["bones"] = ((ss_ // 64) == (tt_ // 64)).astype(np.float32)
    c["bo2"] = ((np.arange(128)[:, None] // 64) == np.arange(2)[None, :]).astype(np.float32)
    c["i64x2"] = np.concatenate([np.eye(64, dtype=np.float32)] * 2, 0)
    return c


def build(debug=()):
    nc = bass.Bass("TRN2", target_bir_lowering=False)

    def din(name, shape, dt=F32):
        return nc.dram_tensor(name, list(shape), dt, kind="ExternalInput").ap()

    def dout(name, shape, dt=F32):
        return nc.dram_tensor(name, list(shape), dt, kind="ExternalOutput").ap()

    x_d = din("x", [T, D])
    xs_d = din("xs", [NS, D])
    w_in_d = din("w_in", [D, 10016])
    norm_mix_d = din("norm_mix", [1, D])
    ident_d = din("ident", [128, 128])
    amask_d = din("amask", [128, 4, 128])
    rope_c_d = din("rope_c", [3, 128, NB, 32])
    rope_s_d = din("rope_s", [3, 128, NB, 2, 32])
    rope_cs_d = din("rope_cs", [1, 96])
    epair_d = din("epair", [128, 4, 4])
    selb_d = din("selb", [NS, NS, 128])
    selp_d = din("selp", [4, 4, 64])
    ck_d = [din(f"ck{g + 1}", [NS, min(w_, 16384), 1024]) for g, (w_, _) in enumerate(GROUPS)]

    w_pa_d = din("w_proj_a", [512, D])
    w_pb_d = din("w_proj_b", [D, D])
    w_out_d = din("w_out", [D, D])
    norm_ffn_d = din("norm_ffn", [1, D])
    w_gate_d = din("w_gate", [D, DFF])
    w_up_d = din("w_up", [D, DFF])
    w_down_d = din("w_down", [DFF, D])
    norm_final_d = din("norm_final", [1, D])
    y_d = dout("y", [T, D])
    ys_d = dout("ys", [NS, D])
    x1_d = nc.dram_tensor("x1_scr", [T + 128, D], F32, kind="Internal").ap()
    sshift_d = din("sshift", [NS, SH])
    swkv_d = din("swkv", [NS, 16, 64, 64])
    mu_c_d = din("mu_c", [128, 27])
    cols8_d = din("cols8", [128, 5, 8])
    w2a2_d = din("w2a2", [128, 1024])
    g2_d = din("g2", [160, 1024])
    gn_w_d = din("gn_w", [1, 1024])
    gn_b_d = din("gn_b", [1, 1024])
    rmask_d = din("rmask", [128, 4, 128])
    bones_d = din("bones", [128, 128])
    bo2_d = din("bo2", [128, 2])
    i64x2_d = din("i64x2", [128, 64])
    shift_p_d = dout("shift_p", [1, SH])
    wkv_p_d = dout("wkv_p", [16, 64, 64])
    shift_s_d = dout("shift_s", [NS, SH])
    wkv_s_d = dout("wkv_s", [NS, 16, 64, 64])
    kvp_d = [dout("kv1_p", [128, 1024]), dout("kv2_p", [512, 1024]), dout("kv3_p", [2048, 1024])]
    kvs_d = [dout(f"kv{g + 1}_s", [NS, 1024]) for g in range(3)]
    dbg = {}

    def dbg_out(name, shape):
        dbg[name] = dout("dbg_" + name, shape)
        return dbg[name]

    w_in_v = w_in_d.rearrange("(kc p) c -> p kc c", p=128)

    with ExitStack() as st:
        S = Sched(nc, st)

        _uid = [0]

        def sb(stack, name, shape, dt):
            _uid[0] += 1
            return stack.enter_context(nc.sbuf_tensor(f"t{_uid[0]}_{name}", list(shape), dt))

        pb = [st.enter_context(nc.psum_tensor(f"pb{i}", [128, 512], F32)) for i in range(8)]
        pbB = [Buf(f"pb{i}") for i in range(8)]

        ident = sb(st, "ident", [128, 128], F32)
        identb = sb(st, "identb", [128, 128], BF16)
        hT = sb(st, "hT", [128, 8, HTW], BF16)
        B_ident, B_identb, B_hT = Buf("ident"), Buf("identb"), Buf("hT")
        oaT_d = dout("scr_oaT", [128, 4, T + 128], BF16)
        S.dma("sync", ident[:], ident_d, writes=[B_ident])
        S.do("vector", lambda e: e.tensor_copy(out=identb[:], in_=ident[:]), reads=[B_ident], writes=[B_identb])
        S.do("vector", lambda e: e.memset(hT[:], 0.0), writes=[B_hT])

        def rmsnorm_to_T(ph, src_blocks, gain_d, dstT, B_dstT, tag, producer=None, out_rows=None):
            g_bc = sb(ph, tag + "g_bc", [128, D], F32)
            B_g = Buf()
            S.dma("sync", g_bc[:], gain_d.partition_broadcast(128), writes=[B_g])
            NX = 4 if producer is None else 2
            xst = [sb(ph, f"{tag}xst{i}", [128, D], F32) for i in range(NX)]
            B_xst = [Buf() for _ in range(NX)]
            junk = sb(ph, tag + "junk", [128, D], F32)
            B_junk = Buf()
            ss = sb(ph, tag + "ss", [128, 4 * len(src_blocks)], F32)
            B_ss = [Buf() for _ in src_blocks]
            hb = [sb(ph, f"{tag}hb{i}", [128, D], BF16 if out_rows is None else F32) for i in range(2)]
            B_hb = [Buf(), Buf()]
            for i, (src, rows, col0) in enumerate(src_blocks):
                s = i % 2
                sx = i % NX
                if rows < 128:
                    S.do("vector", lambda e, sx=sx: e.memset(xst[sx][:], 0.0), writes=[B_xst[sx]])
                S.dma("sync", xst[sx][0:rows, :], src, writes=[B_xst[sx]])
                if producer is not None:
                    producer(i, xst[sx], B_xst[sx])
                S.do("scalar", lambda e, sx=sx, i=i: e.activation(out=junk[:], in_=xst[sx][:], func=AF.Square,
                                                                 accum_out=ss[:, 4 * i:4 * i + 1]),
                     reads=[B_xst[sx]], writes=[B_junk, B_ss[i]])
                S.do("vector", lambda e, i=i: e.tensor_scalar(out=ss[:, 4 * i + 1:4 * i + 2], in0=ss[:, 4 * i:4 * i + 1],
                                                              scalar1=1.0 / D, scalar2=1e-6, op0=ALU.mult, op1=ALU.add),
                     reads=[B_ss[i]], writes=[B_ss[i]])
                S.do("scalar", lambda e, i=i: e.activation(out=ss[:, 4 * i + 2:4 * i + 3], in_=ss[:, 4 * i + 1:4 * i + 2], func=AF.Sqrt),
                     reads=[B_ss[i]], writes=[B_ss[i]])
                S.do("vector", lambda e, i=i: e.reciprocal(out=ss[:, 4 * i + 3:4 * i + 4], in_=ss[:, 4 * i + 2:4 * i + 3]),
                     reads=[B_ss[i]], writes=[B_ss[i]])
                S.do("vector", lambda e, s=s, sx=sx, i=i: e.scalar_tensor_tensor(out=hb[s][:], in0=xst[sx][:], scalar=ss[:, 4 * i + 3:4 * i + 4],
                                                                                in1=g_bc[:], op0=ALU.mult, op1=ALU.mult),
                     reads=[B_xst[sx], B_ss[i], B_g], writes=[B_hb[s]])
                if out_rows is not None:
                    dst, nrow = out_rows[i]
                    S.dma("sync", dst, hb[s][0:nrow, :], reads=[B_hb[s]])
                    continue
                bk = 6 + (i % 2)
                pbb = pb[bk][:].bitcast(BF16)
                S.mm([("T", pbb[:, kc * 128:(kc + 1) * 128], hb[s][:, kc * 128:(kc + 1) * 128], identb[:]) for kc in range(8)],
                     reads=[B_hb[s], B_identb], writes=[pbB[bk]])
                S.do("scalar", lambda e, pbb=pbb, col0=col0: e.copy(out=dstT[:, :, col0:col0 + 128],
                                                                   in_=pbb.rearrange("p (k t) -> p k t", k=8)),
                     reads=[pbB[bk]], writes=[B_dstT])

        phB = st.enter_context(ExitStack())
        W3 = [sb(phB, f"W3_{j}", [128, 8, 512], BF16) for j in range(3)]
        B_W3 = [Buf() for _ in range(3)]
        for j in range(3):
            S.dma("gpsimd", W3[j][:], w_in_v[:, :, j * 512:(j + 1) * 512], writes=[B_W3[j]])
        with ExitStack() as ph:
            blocks = [(x_d[i * 128:(i + 1) * 128, :], 128, C0 + i * 128) for i in range(NB)]
            if "nosample" not in debug:
                blocks.append((xs_d, NS, SC0))
            rmsnorm_to_T(ph, blocks, norm_mix_d, hT, B_hT, "A")
            S.barrier()

        if "hT" in debug:
            o = dbg_out("hT", [128, 8, HTW])
            with ExitStack() as ph:
                tmp = sb(ph, "dbgtmp", [128, 8, HTW], F32)
                Bt = Buf()
                S.do("vector", lambda e: e.tensor_copy(out=tmp[:], in_=hT[:]), reads=[B_hT], writes=[Bt])
                S.dma("sync", o, tmp[:], reads=[Bt])
                S.barrier()

        with phB as ph:
          if "stopA" not in debug:
              QT = sb(ph, "QTo", [128, 4, T + 128], BF16)
              B_oaT = Buf("oaT")
              S.do("vector", lambda e: e.memset(QT[:, :, T:T + 128], 0.0), writes=[B_oaT])
              amask = sb(ph, "amask", [128, 4, 128], BF16)
              epair = sb(ph, "epair", [128, 4, 4], BF16)
              B_const = Buf()
              S.dma("gpsimd", amask[:], amask_d, writes=[B_const])
              S.dma("gpsimd", epair[:], epair_d, writes=[B_const])
              ropes = sb(ph, "ropes", [128, 96], F32)
              S.dma("sync", ropes[:], rope_cs_d.partition_broadcast(128), writes=[B_const])
              acc_num = sb(ph, "acc_num", [128, 4, T], F32)
              acc_den = sb(ph, "acc_den", [4, 2, T], F32)
              B_acc = [Buf() for _ in range(NB)]
              qkv_s = sb(ph, "qkv_s", [NS, 3, 512], F32)
              B_qkvs = Buf()
              selb = sb(ph, "selb", [NS, NS, 128], F32)
              S.dma("sync", selb[:], selb_d, writes=[B_const])
              KVc = [sb(ph, f"KVc{i}", [128, 2, 8, 65], F32) for i in range(2)]
              B_KVc = [Buf(), Buf()]
              for i in range(2):
                  S.do("vector", lambda e, i=i: e.memset(KVc[i][:], 1.0), writes=[B_KVc[i]])
              prod = sb(ph, "prod", [128, 512], F32)
              B_prod = Buf()
              sc = sb(ph, "sc", [128, 16], F32)
              B_sc = Buf()
              Pz = [sb(ph, f"Pz{b}", [128, 8, NS], F32) for b in range(NS)]
              B_Pz = [Buf() for _ in range(NS)]
              for b in range(NS):
                  S.do("vector", lambda e, b=b: e.memset(Pz[b][:], 0.0), writes=[B_Pz[b]])
              acc_s = sb(ph, "acc_s", [NS, 8, 65], F32)
              cn = sb(ph, "cn", [NS, 8, 65], F32)
              sn = sb(ph, "sn", [NS, 16], F32)
              B_accs, B_cn, B_sn = Buf(), Buf(), Buf()
              KT = sb(ph, "KT", [128, 4, T], BF16)
              Vt = sb(ph, "Vt", [128, NB, 512], BF16)
              B_Q = [Buf() for _ in range(NB)]
              B_K = [Buf() for _ in range(NB)]
              B_V = [Buf() for _ in range(NB)]
              rc = sb(ph, "rc", [128, NB, 32], F32)
              rs = sb(ph, "rs", [128, NB, 2, 32], F32)
              B_rope = Buf()
              t1_ = sb(ph, "t1", [128, 512], F32)
              t2_ = sb(ph, "t2", [128, 512], F32)
              t1, t2 = [t1_, t1_], [t2_, t2_]
              B_t1_, B_t2_ = Buf(), Buf()
              B_t1, B_t2 = [B_t1_, B_t1_], [B_t2_, B_t2_]
              qb = [sb(ph, f"qb{i}", [128, 512], BF16) for i in range(2)]
              kb = [sb(ph, f"kb{i}", [128, 512], BF16) for i in range(2)]
              B_qb = [Buf(), Buf()]
              B_kb = [Buf(), Buf()]
              kv32 = [sb(ph, f"kv32_{i}", [128, 2, 512], F32) for i in range(2)]
              B_kv32 = [Buf(), Buf()]
              PT = [sb(ph, f"PT{i}", [128, 512], BF16) for i in range(2)]
              PM = [sb(ph, f"PM{i}", [128, 512], BF16) for i in range(2)]
              B_PT = [Buf(), Buf()]
              B_PM = [Buf(), Buf()]

              def v4(ap):
                  return ap.rearrange("p (h t d) -> p h t d", h=8, t=2, d=32)

              def rope_ops(np_, src_ps, B_src, cos_ap, sin_ap, B_tab, out_ap, B_out, s):
                  S.do("vector", lambda e: e.tensor_tensor(out=v4(t1[s][0:np_, :]), in0=v4(src_ps), in1=cos_ap, op=ALU.mult),
                       reads=[B_src, B_tab], writes=[B_t1[s]])
                  S.do("vector", lambda e: e.tensor_tensor(out=v4(t2[s][0:np_, :]), in0=v4(src_ps)[:, :, ::-1, :], in1=sin_ap, op=ALU.mult),
                       reads=[B_src, B_tab], writes=[B_t2[s]])
                  S.do("vector", lambda e: e.tensor_tensor(out=out_ap, in0=t1[s][0:np_, :], in1=t2[s][0:np_, :], op=ALU.add),
                       reads=[B_t1[s], B_t2[s]], writes=[B_out])

              for g, (window, d) in enumerate(GROUPS):
                  nbs = NB // d
                  if "g1" in debug and g > 0:
                      continue
                  c0 = g * 1536
                  for j in range(3):
                      if g > 0:
                          S.dma("gpsimd", W3[j][:], w_in_v[:, :, c0 + j * 512:c0 + (j + 1) * 512], writes=[B_W3[j]])
                  S.dma("sync", rc[:], rope_c_d[g], writes=[B_rope])
                  S.dma("sync", rs[:], rope_s_d[g], writes=[B_rope])
                  keep0 = T - min(window, T)
                  for sbk in range(NB + 1):
                      s = sbk % 2
                      if sbk == NB and "nosampleB" in debug:
                          continue
                      if sbk < NB:
                          r, n = divmod(sbk, nbs)
                          tok0 = r + d * 128 * n
                          np_ = 128
                          lcol = lambda kc: hT[:, kc, C0 + tok0:C0 + tok0 + d * 127 + 1:d]
                          cos_ap = rc[:, sbk, :].unsqueeze(1).unsqueeze(1).broadcast_to([128, 8, 2, 32])
                          sin_ap = rs[:, sbk, :, :].unsqueeze(1).broadcast_to([128, 8, 2, 32])
                          B_tab = B_rope
                      else:
                          np_ = NS
                          lcol = lambda kc: hT[:, kc, SC0:SC0 + NS]
                          cos_ap = ropes[0:NS, 0:32].unsqueeze(1).unsqueeze(1).broadcast_to([NS, 8, 2, 32])
                          sin_ap = ropes[0:NS, 32:96].rearrange("p (t d) -> p t d", t=2).unsqueeze(1).broadcast_to([NS, 8, 2, 32])
                          B_tab = B_const
                      banks = [3 * s + j for j in range(3)]
                      for j in range(3):
                          bk = banks[j]
                          S.mm([dict(out=pb[bk][0:np_, :], lhsT=lcol(kc), rhs=W3[j][:, kc, :], start=(kc == 0), stop=(kc == 7)) for kc in range(8)],
                               reads=[B_hT, B_W3[j]], writes=[pbB[bk]])
                      if sbk < NB:
                          rope_ops(128, pb[banks[0]][:, :], pbB[banks[0]], cos_ap, sin_ap, B_tab, qb[s][:], B_qb[s], s)
                          tb = 6
                          pbb = pb[tb][:].bitcast(BF16)
                          S.mm([("T", pbb[:, p * 128:(p + 1) * 128], qb[s][:, p * 128:(p + 1) * 128], identb[:]) for p in range(4)],
                               reads=[B_qb[s], B_identb], writes=[pbB[tb]])
                          S.do("scalar", lambda e, pbb=pbb, sbk=sbk: e.copy(out=QT[:, :, sbk * 128:(sbk + 1) * 128],
                                                                         in_=pbb[:, 0:512].rearrange("p (k t) -> p k t", k=4)),
                               reads=[pbB[tb]], writes=[B_Q[sbk]])
                          rope_ops(128, pb[banks[1]][:, :], pbB[banks[1]], cos_ap, sin_ap, B_tab, kv32[s][:, 0, :], B_kv32[s], s)
                          S.do("scalar", lambda e, s=s: e.copy(out=kb[s][:], in_=kv32[s][:, 0, :]), reads=[B_kv32[s]], writes=[B_kb[s]])
                          tb = 7
                          pbb = pb[tb][:].bitcast(BF16)
                          S.mm([("T", pbb[:, p * 128:(p + 1) * 128], kb[s][:, p * 128:(p + 1) * 128], identb[:]) for p in range(4)],
                               reads=[B_kb[s], B_identb], writes=[pbB[tb]])
                          S.do("scalar", lambda e, pbb=pbb, sbk=sbk: e.copy(out=KT[:, :, sbk * 128:(sbk + 1) * 128],
                                                                         in_=pbb[:, 0:512].rearrange("p (k t) -> p k t", k=4)),
                               reads=[pbB[tb]], writes=[B_K[sbk]])
                          S.do("scalar", lambda e, s=s, bk=banks[2]: e.copy(out=kv32[s][:, 1, :], in_=pb[bk][:, :]), reads=[pbB[banks[2]]], writes=[B_kv32[s]])
                          S.do("scalar", lambda e, s=s, sbk=sbk: e.copy(out=Vt[:, sbk, :], in_=kv32[s][:, 1, :]), reads=[B_kv32[s]], writes=[B_V[sbk]])
                          if tok0 + d * 127 >= keep0 and tok0 >= keep0:
                              dst = kvp_d[g][tok0 - keep0:tok0 - keep0 + d * 127 + 1:d, :].rearrange("t (s c) -> t s c", s=2)
                              S.dma("sync", dst, kv32[s][:], reads=[B_kv32[s]])
                      else:
                          rope_ops(NS, pb[banks[0]][0:NS, :], pbB[banks[0]], cos_ap, sin_ap, B_tab, qkv_s[:, 0, :], B_qkvs, s)
                          rope_ops(NS, pb[banks[1]][0:NS, :], pbB[banks[1]], cos_ap, sin_ap, B_tab, qkv_s[:, 1, :], B_qkvs, s)
                          S.do("scalar", lambda e, bk=banks[2], g=g: e.copy(out=qkv_s[:, 2, :], in_=pb[bk][0:NS, :]), reads=[pbB[banks[2]]], writes=[B_qkvs])
                          S.dma("sync", kvs_d[g].rearrange("t (s c) -> t s c", s=2), qkv_s[:, 1:3, :], reads=[B_qkvs])
                  for sbk in range(NB):
                      if "noattn" in debug:
                          continue
                      r, n = divmod(sbk, nbs)
                      tok0 = r + d * 128 * n
                      kbs = [sbk - 1, sbk] if n > 0 else [sbk]
                      nk = len(kbs)
                      ncol = nk * 256
                      moff = 0 if nk == 2 else 256
                      bO = 4 + (sbk % 2)
                      bD = 6 + (sbk % 2)
                      for p in range(4):
                          u = (sbk * 4 + p) % 2
                          bS = [2 * u, 2 * u + 1]
                          specs = []
                          for ki, kbk in enumerate(kbs):
                              for hp in range(2):
                                  specs.append(dict(out=pb[bS[hp]][:, ki * 128:(ki + 1) * 128],
                                                    lhsT=KT[hp * 64:(hp + 1) * 64, p, kbk * 128:(kbk + 1) * 128],
                                                    rhs=QT[hp * 64:(hp + 1) * 64, p, sbk * 128:(sbk + 1) * 128], start=True, stop=True))
                          S.mm(specs, reads=[B_K[k_] for k_ in kbs] + [B_Q[sbk]], writes=[pbB[bS[0]], pbB[bS[1]]])
                          for hp in range(2):
                              S.do("scalar", lambda e, u=u, b_=bS[hp], hp=hp, nk=nk: e.activation(out=PT[u][:, hp * 256:hp * 256 + nk * 128], in_=pb[b_][:, 0:nk * 128],
                                                                                                  func=AF.Exp, scale=0.125),
                                   reads=[pbB[bS[hp]]], writes=[B_PT[u]])
                          pt3 = PT[u][:].rearrange("p (a c) -> p a c", a=2)[:, :, 0:nk * 128]
                          pm3 = PM[u][:].rearrange("p (a c) -> p a c", a=2)[:, :, 0:nk * 128]
                          mk3 = amask[:].rearrange("p (a b) c -> p a (b c)", a=2)[:, :, (2 - nk) * 128:256]
                          S.do("vector", lambda e, pt3=pt3, pm3=pm3, mk3=mk3: e.tensor_tensor(out=pm3, in0=pt3, in1=mk3, op=ALU.mult),
                               reads=[B_PT[u], B_const], writes=[B_PM[u]])
                          specs = []
                          for hp in range(2):
                              for ki, kbk in enumerate(kbs):
                                  col0 = hp * 256 + ki * 128
                                  specs.append(dict(out=pb[bO][hp * 64:(hp + 1) * 64, p * 128:(p + 1) * 128],
                                                    lhsT=Vt[:, kbk, p * 128 + hp * 64:p * 128 + (hp + 1) * 64],
                                                    rhs=PM[u][:, col0:col0 + 128], start=(ki == 0), stop=(ki == nk - 1)))
                          for ki in range(nk):
                              rhs = PM[u][:].rearrange("p (a c) -> p a c", a=2)[:, :, ki * 128:(ki + 1) * 128]
                              specs.append(dict(out=pb[bD][0:4, 0:256], lhsT=epair[:, p, :], rhs=rhs,
                                                start=(p == 0 and ki == 0), stop=(p == 3 and ki == nk - 1)))
                          S.mm(specs, reads=[B_PM[u]] + [B_V[k_] for k_ in kbs] + [B_const], writes=[pbB[bO], pbB[bD]])
                      num_dst = acc_num[:, :, tok0:tok0 + d * 127 + 1:d]
                      den_dst = acc_den[:, :, tok0:tok0 + d * 127 + 1:d]
                      num_src = pb[bO][:, :].rearrange("p (a q) -> p a q", a=4)
                      den_src = pb[bD][0:4, 0:256].rearrange("p (a q) -> p a q", a=2)
                      Bacc = B_acc[0]
                      if g == 0:
                          S.do("scalar", lambda e, num_dst=num_dst, num_src=num_src: e.copy(out=num_dst, in_=num_src), reads=[pbB[bO]], writes=[Bacc])
                          S.do("scalar", lambda e, den_dst=den_dst, den_src=den_src: e.copy(out=den_dst, in_=den_src), reads=[pbB[bD]], writes=[Bacc])
                      else:
                          S.do("vector", lambda e, num_dst=num_dst, num_src=num_src: e.tensor_tensor(out=num_dst, in0=num_src, in1=num_dst, op=ALU.add),
                               reads=[pbB[bO]], writes=[Bacc])
                          S.do("vector", lambda e, den_dst=den_dst, den_src=den_src: e.tensor_tensor(out=den_dst, in0=den_src, in1=den_dst, op=ALU.add),
                               reads=[pbB[bD]], writes=[Bacc])
                  if "nosampleB" not in debug:
                      buf_len = min(window, 16384)
                      for b in range(NS):
                          sl = b % 2
                          bq = b % 2
                          S.dma("sync", KVc[sl][:, :, :, 0:64], ck_d[g][b, 0:buf_len:d, :].rearrange("m (s h e) -> m s h e", s=2, h=8), writes=[B_KVc[sl]])
                          S.mm([dict(out=pb[bq][:, :], lhsT=selb[0:NS, b, :], rhs=qkv_s[0:NS, 0, :])], reads=[B_qkvs, B_const], writes=[pbB[bq]])
                          S.do("vector", lambda e, sl=sl, bq=bq: e.tensor_tensor(out=prod[:].rearrange("p (h e) -> p h e", h=8), in0=KVc[sl][:, 0, :, 0:64],
                                                                              in1=pb[bq][:, :].rearrange("p (h e) -> p h e", h=8), op=ALU.mult),
                               reads=[B_KVc[sl], pbB[bq]], writes=[B_prod])
                          S.do("vector", lambda e: e.tensor_reduce(out=sc[:, 0:8], in_=prod[:].rearrange("p (h e) -> p h e", h=8), axis=AX.X, op=ALU.add),
                               reads=[B_prod], writes=[B_sc])
                          S.do("scalar", lambda e, b=b: e.activation(out=Pz[b][:, :, b], in_=sc[:, 0:8], func=AF.Exp, scale=0.125),
                               reads=[B_sc], writes=[B_Pz[b]])
                          specs = []
                          for h in range(8):
                              bn = 2 + h // 4
                              specs.append(dict(out=pb[bn][0:NS, (h % 4) * 65:(h % 4 + 1) * 65], lhsT=Pz[b][:, h, :], rhs=KVc[sl][:, 1, h, :],
                                                start=(b == 0 and h % 4 == 0), stop=(b == NS - 1 and h % 4 == 3)))
                          S.mm(specs, reads=[B_Pz[b], B_KVc[sl]], writes=[pbB[2], pbB[3]])
                      S.do("vector", lambda e: e.tensor_tensor(out=prod[0:NS, :], in0=qkv_s[:, 0, :], in1=qkv_s[:, 1, :], op=ALU.mult),
                           reads=[B_qkvs], writes=[B_prod])
                      S.do("vector", lambda e: e.tensor_reduce(out=sn[:, 0:8], in_=prod[0:NS, :].rearrange("p (h e) -> p h e", h=8), axis=AX.X, op=ALU.add),
                           reads=[B_prod], writes=[B_sn])
                      S.do("scalar", lambda e: e.activation(out=sn[:, 8:16], in_=sn[:, 0:8], func=AF.Exp, scale=0.125), reads=[B_sn], writes=[B_sn])
                      S.do("vector", lambda e: e.tensor_tensor(out=cn[:, :, 0:64], in0=qkv_s[:, 2, :].rearrange("p (h e) -> p h e", h=8),
                                                               in1=sn[:, 8:16].unsqueeze(2).broadcast_to([NS, 8, 64]), op=ALU.mult),
                           reads=[B_qkvs, B_sn], writes=[B_cn])
                      S.do("vector", lambda e: e.tensor_copy(out=cn[:, :, 64:65], in_=sn[:, 8:16].unsqueeze(2)), reads=[B_sn], writes=[B_cn])
                      for hb in range(2):
                          src = pb[2 + hb][0:NS, 0:260].rearrange("p (h e) -> p h e", h=4)
                          dst = acc_s[:, hb * 4:(hb + 1) * 4, :]
                          cns = cn[:, hb * 4:(hb + 1) * 4, :]
                          S.do("vector", lambda e, src=src, cns=cns: e.tensor_tensor(out=cns, in0=src, in1=cns, op=ALU.add),
                               reads=[pbB[2 + hb]], writes=[B_cn])
                          if g == 0:
                              S.do("vector", lambda e, dst=dst, cns=cns: e.tensor_copy(out=dst, in_=cns), reads=[B_cn], writes=[B_accs])
                          else:
                              S.do("vector", lambda e, dst=dst, cns=cns: e.tensor_tensor(out=dst, in0=dst, in1=cns, op=ALU.add), reads=[B_cn], writes=[B_accs])
              S.barrier()
              selp = sb(ph, "selp", [4, 4, 64], F32)
              S.dma("sync", selp[:], selp_d, writes=[B_const])
              if "noattn" not in debug:
                  S.do("vector", lambda e: e.reciprocal(out=acc_den[:], in_=acc_den[:]), reads=[B_acc[0]], writes=[B_acc[0]])
                  for p in range(4):
                      for ch in range(4):
                          bk = (p * 4 + ch) % 2
                          S.mm([dict(out=pb[bk][hp * 64:(hp + 1) * 64, :], lhsT=selp[0:4, p, :], rhs=acc_den[0:4, hp, ch * 512:(ch + 1) * 512]) for hp in range(2)],
                               reads=[B_acc[0], B_const], writes=[pbB[bk]])
                          S.do("vector", lambda e, p=p, ch=ch, bk=bk: e.tensor_tensor(out=QT[:, p, ch * 512:(ch + 1) * 512], in0=acc_num[:, p, ch * 512:(ch + 1) * 512],
                                                                                  in1=pb[bk][:, :], op=ALU.mult),
                               reads=[B_acc[0], pbB[bk]], writes=[B_oaT])
              if "nosampleB" not in debug:
                  S.do("vector", lambda e: e.reciprocal(out=sn[:, 0:8], in_=acc_s[:, :, 64]), reads=[B_accs], writes=[B_sn])
                  S.do("vector", lambda e: e.tensor_tensor(out=prod[0:NS, :].rearrange("p (h e) -> p h e", h=8), in0=acc_s[:, :, 0:64],
                                                           in1=sn[:, 0:8].unsqueeze(2).broadcast_to([NS, 8, 64]), op=ALU.mult),
                       reads=[B_accs, B_sn], writes=[B_prod])
                  S.mm([("T", pb[4][:, p * 4:p * 4 + NS], prod[0:NS, p * 128:(p + 1) * 128], ident[0:NS, 0:NS]) for p in range(4)],
                       reads=[B_prod, B_ident], writes=[pbB[4]])
                  S.do("vector", lambda e: e.tensor_copy(out=QT[:, :, T:T + NS], in_=pb[4][:, 0:16].rearrange("p (a b) -> p a b", a=4)),
                       reads=[pbB[4]], writes=[B_oaT])
                  if "acc" in debug:
                      o3 = dbg_out("oas", [NS, 512])
                      S.dma("sync", o3, prod[0:NS, :], reads=[B_prod])
              S.dma("sync", oaT_d, QT[:], reads=[B_oaT])
              if "acc" in debug:
                  o1 = dbg_out("num", [128, 4, T])
                  o2 = dbg_out("den", [4, 2, T])
                  S.dma("sync", o1, acc_num[:], reads=[B_acc[0]])
                  S.dma("sync", o2, acc_den[:], reads=[B_acc[0]])
              S.barrier()

        o_bT = sb(st, "o_bT", [128, 8, T + 128], BF16)
        B_obT = Buf("obT")
        S.do("vector", lambda e: e.memset(o_bT[:, :, T:T + 128], 0.0), writes=[B_obT])
        S.do("vector", lambda e: e.tensor_copy(out=hT[:, :, SC0 + NS:SC0 + NS + 1], in_=hT[:, :, C0 + T - 1:C0 + T]), reads=[B_hT], writes=[B_hT])
        C0D = 0.6065306597126334
        def rwkv_pass(hh, ph):
            a0g = 4 * hh
            Bc = Buf("dconst")
            Ws, B_Ws = Ws_g, B_Ws_g
            if hh == 0:
                load_Ws(0)

            def gct(lt):
                return (lt // 4) * 8 + a0g + lt % 4 if lt < 12 else 24 + (lt - 12)
            mu_c = sb(ph, "mu_c", [128, 27], F32)
            omm_c = sb(ph, "omm_c", [128, 27], F32)
            cols8 = sb(ph, "cols8", [128, 5, 8], F32)
            omka = sb(ph, "omka", [128, 8], F32)
            S.dma("sync", mu_c[:], mu_c_d, writes=[Bc])
            S.dma("sync", cols8[:], cols8_d, writes=[Bc])
            S.do("vector", lambda e: e.tensor_scalar(out=omm_c[:], in0=mu_c[:], scalar1=-1.0, scalar2=1.0, op0=ALU.mult, op1=ALU.add), reads=[Bc], writes=[Bc])
            S.do("vector", lambda e: e.tensor_scalar(out=omka[:], in0=cols8[:, 3, :], scalar1=-1.0, scalar2=1.0, op0=ALU.mult, op1=ALU.add), reads=[Bc], writes=[Bc])
            w2a2 = sb(ph, "w2a2", [128, 512], F32)
            g2a = sb(ph, "g2a", [128, 512], BF16)
            g2b = sb(ph, "g2b", [128, 512], BF16)
            xgb = sb(ph, "xgb", [128, 2, 128], BF16)
            sqb = sb(ph, "sqb", [128, 512], BF16)
            bonesb = sb(ph, "bonesb", [128, 128], BF16)
            B_xgb, B_sqb = Buf(), Buf()
            S.do("vector", lambda e: e.memset(xgb[:], 0.0), writes=[B_xgb])
            S.dma("gpsimd", bonesb[:], bones_d, writes=[Bc])
            gnw = sb(ph, "gnw", [128, 512], F32)
            gnb = sb(ph, "gnb", [128, 512], F32)
            hs = slice(hh * 512, (hh + 1) * 512)
            S.dma("sync", w2a2[:], w2a2_d[:, hs], writes=[Bc])
            S.dma("gpsimd", g2a[:], g2_d[0:128, hs], writes=[Bc])
            S.do("vector", lambda e: e.memset(g2b[:], 0.0), writes=[Bc])
            S.dma("gpsimd", g2b[0:32, :], g2_d[128:160, hs], writes=[Bc])
            S.dma("sync", gnw[:], gn_w_d[:, hs].partition_broadcast(128), writes=[Bc])
            S.dma("sync", gnb[:], gn_b_d[:, hs].partition_broadcast(128), writes=[Bc])
            rmask = sb(ph, "rmask", [128, 4, 128], BF16)
            bones = sb(ph, "bones", [128, 128], F32)
            bo2 = sb(ph, "bo2", [128, 2], BF16)
            i64x2 = sb(ph, "i64x2", [128, 64], F32)
            ones128 = sb(ph, "ones128", [128, 128], F32)
            S.dma("gpsimd", rmask[:], rmask_d, writes=[Bc])
            S.dma("sync", bones[:], bones_d, writes=[Bc])
            S.dma("gpsimd", bo2[:], bo2_d, writes=[Bc])
            S.dma("sync", i64x2[:], i64x2_d, writes=[Bc])
            S.do("vector", lambda e: e.memset(ones128[:], 1.0), writes=[Bc])

            def f4(nm):
                return sb(ph, nm, [128, 4, 128], F32)
            sgT, aT, ginv, S1, S2, S3 = [f4(n) for n in ("sgT", "aT", "ginv", "S1", "S2", "S3")]
            rT3, kT3, vT3 = [sb(ph, n, [128, 4, 384], F32) for n in ("rT3", "kT3", "vT3")]
            B_r, B_k, B_v, B_sg, B_a, B_gi, B_S1, B_S2, B_S3 = [Buf() for _ in range(9)]
            cs, B_cs = S3, B_S3
            lw3 = sb(ph, "lw3", [128, 384], F32)
            xg3 = sb(ph, "xg3", [128, 2, 384], F32)
            B_lw, B_xg = Buf(), Buf()
            S.do("vector", lambda e: e.memset(xg3[:], 0.0), writes=[B_xg])
            tmix = sb(ph, "tmix", [128, 384], F32)
            B_tmix = Buf()
            gamx2 = [sb(ph, "gamx", [128, 4, 129], F32)] * 2
            B_gam2 = [Buf()] * 2
            S.do("vector", lambda e: e.memset(gamx2[0][:], 1.0), writes=[B_gam2[0]])
            AR2 = [sb(ph, "AR", [128, 4, 2, 128], BF16)] * 2
            ARbd2 = [sb(ph, "ARbd", [128, 4, 2, 2, 128], BF16)] * 2
            B_AR2, B_ARbd2 = [Buf()] * 2, [Buf()] * 2
            S.do("vector", lambda e: e.memset(ARbd2[0][:], 0.0), writes=[B_ARbd2[0]])
            BT2 = [sb(ph, "BT", [128, 4, 128], BF16)] * 2
            KhT2 = [sb(ph, "KhT", [128, 4, 128], BF16)] * 2
            B_BT2, B_KhT2 = [Buf()] * 2, [Buf()] * 2
            rks = sb(ph, "rks", [128, 512], F32)
            B_rks = Buf()
            VT = sb(ph, "VT", [128, 4, 128], BF16)
            rkT = sb(ph, "rkT", [128, 4, 128], BF16)
            B_VT, B_rk = Buf(), Buf()
            Btok2 = [sb(ph, "Btok", [128, 512], BF16)] * 2
            Ktok2 = [sb(ph, "Ktok", [128, 512], BF16)] * 2
            Vtok2 = [sb(ph, "Vtok", [128, 512], BF16)] * 2
            B_Btok2, B_Ktok2, B_Vtok2 = [Buf()] * 2, [Buf()] * 2, [Buf()] * 2
            bon2 = [sb(ph, "bon", [128, 8], F32)] * 2
            B_bon2 = [Buf()] * 2
            gate_sb2 = [sb(ph, "gate_sb", [128, 512], F32)] * 2
            B_gate2 = [Buf()] * 2
            XS1 = [sb(ph, f"XS1_{a}", [128, 512], BF16) for a in range(4)]
            XS2 = [sb(ph, f"XS2_{a}", [128, 512], BF16) for a in range(4)]
            B_XS1 = [Buf() for _ in range(4)]
            B_XS2 = [Buf() for _ in range(4)]
            L0 = [sb(ph, f"L0_{a}", [128, 2, 128], BF16) for a in range(4)]
            B_L0 = [Buf() for _ in range(4)]
            LN = [[sb(ph, f"LN_{a}_{q}", [128, 512], BF16) for q in range(2)] for a in range(4)]
            B_LN = [[Buf(), Buf()] for _ in range(4)]
            Ut = [[sb(ph, f"U_{a}_{q}", [128, 128], BF16) for q in range(2)] for a in range(4)]
            B_U = [[Buf(), Buf()] for _ in range(4)]
            Mst = sb(ph, "Mst", [128, 4, 64], F32)
            Mg = sb(ph, "Mg", [128, 4, 64], F32)
            M0bd = sb(ph, "M0bd", [128, 4, 2, 64], BF16)
            B_M, B_Mg, B_M0bd = Buf(), Buf(), Buf()
            S.do("vector", lambda e: e.memset(Mst[:], 0.0), writes=[B_M])
            S.do("vector", lambda e: e.memset(M0bd[:], 0.0), writes=[B_M0bd])
            ob = sb(ph, "ob", [128, 512], BF16)
            stt = sb(ph, "stt", [128, 48], F32)
            B_ob, B_st = Buf(), Buf()
            ssT = sb(ph, "ssT", [128, 15, NS], F32)
            B_ssT = Buf()
            praw = sb(ph, "praw", [128, 15, 8], F32)
            B_praw = Buf()
            ysq = sb(ph, "ysq", [128, 512], F32)
            tt = sb(ph, "tt", [128, 512], F32)
            bv = sb(ph, "bv", [128, 512], F32)
            B_ysq, B_tt, B_bv = Buf(), Buf(), Buf()
            for q in range(4):
                w_ = 512 if q < 3 else 288
                src_c0 = (q * 1024 + a0g * 128) if q < 3 else 3072
                S.dma("sync", tt[0:NS, 0:w_], sshift_d[:, src_c0:src_c0 + w_], writes=[B_tt])
                for j in range((w_ + 127) // 128):
                    lt = q * 4 + j
                    cw = 32 if lt == 14 else 128
                    bk = lt % 2
                    S.mm([("T", pb[bk][0:cw, 0:NS], tt[0:NS, j * 128:j * 128 + cw], ident[0:NS, 0:NS])], reads=[B_tt, B_ident], writes=[pbB[bk]])
                    S.do("vector", lambda e, lt=lt, cw=cw, bk=bk: e.tensor_copy(out=ssT[0:cw, lt, :], in_=pb[bk][0:cw, 0:NS]), reads=[pbB[bk]], writes=[B_ssT])

            def bc3(ap2, n=128):
                return ap2.unsqueeze(2).broadcast_to([128, 4, n])

            def dest_of(lt):
                if lt < 4:
                    return rT[:, lt, :], B_r
                if lt < 8:
                    return kT[:, lt - 4, :], B_k
                if lt < 12:
                    return vT[:, lt - 8, :], B_v
                if lt == 12:
                    return lw[:, :], B_lw
                if lt == 13:
                    return xg[:, 0, :], B_xg
                return xg[0:32, 1, :], B_xg

            def rwkv_block(c, stage):
                isX = (c == NB)
                np_ = NS if isX else 128
                if (isX and DNOX) or (not isX and c >= DNB):
                    return
                pc = c % 2
                gate_sb, B_gate = gate_sb2[pc], B_gate2[pc]
                gamx, B_gam = gamx2[pc], B_gam2[pc]
                Vtok, B_Vtok = Vtok2[pc], B_Vtok2[pc]
                Btok, B_Btok = Btok2[pc], B_Btok2[pc]
                Ktok, B_Ktok = Ktok2[pc], B_Ktok2[pc]
                bon, B_bon = bon2[pc], B_bon2[pc]
                AR, B_AR = AR2[pc], B_AR2[pc]
                ARbd, B_ARbd = ARbd2[pc], B_ARbd2[pc]
                BT, B_BT = BT2[pc], B_BT2[pc]
                KhT, B_KhT = KhT2[pc], B_KhT2[pc]
                jb = 0 if isX else c % 3
                rT = rT3[:, :, jb * 128:(jb + 1) * 128]
                kT = kT3[:, :, jb * 128:(jb + 1) * 128]
                vT = vT3[:, :, jb * 128:(jb + 1) * 128]
                lw = lw3[:, jb * 128:(jb + 1) * 128]
                xg = xg3[:, :, jb * 128:(jb + 1) * 128]
                if stage == "B":
                    yield from rwkv_stage_b(c, isX, np_, gate_sb, B_gate, gamx, B_gam, Vtok, B_Vtok, Btok, B_Btok, Ktok, B_Ktok, bon, B_bon, AR, B_AR, ARbd, B_ARbd, BT, B_BT, KhT, B_KhT)
                    return
                def dest3(lt):
                    if lt < 4:
                        return rT3[:, lt, :], B_r
                    if lt < 8:
                        return kT3[:, lt - 4, :], B_k
                    if lt < 12:
                        return vT3[:, lt - 8, :], B_v
                    if lt == 12:
                        return lw3[:, :], B_lw
                    if lt == 13:
                        return xg3[:, 0, :], B_xg
                    return xg3[0:32, 1, :], B_xg
                if isX or c % 3 == 0:
                    nb_ = 1 if isX else min(3, NB - c)
                    for lt in range(15):
                        bk = lt % 4
                        cw = 32 if lt == 14 else 128
                        ct = gct(lt)
                        dst, B_dst = dest3(lt)
                        if isX:
                            S.mm([dict(out=pb[bk][0:cw, 0:128], lhsT=Ws[:, kc, lt * 128:lt * 128 + cw], rhs=hT[:, kc, SC0:SC0 + 128], start=(kc == 0), stop=(kc == 7)) for kc in range(8)],
                                 reads=[B_hT, B_Ws], writes=[pbB[bk]])
                            S.do("vector", lambda e, lt=lt, cw=cw, bk=bk: e.tensor_copy(out=praw[0:cw, lt, :], in_=pb[bk][0:cw, 0:8]),
                                 reads=[pbB[bk]], writes=[B_praw])
                            S.do("vector", lambda e, lt=lt, cw=cw, ct=ct: e.tensor_scalar(out=tmix[0:cw, 0:NS], in0=ssT[0:cw, lt, :], scalar1=mu_c[0:cw, ct:ct + 1],
                                                                                  scalar2=None, op0=ALU.mult),
                                 reads=[B_ssT, Bc], writes=[B_tmix])
                            S.do("vector", lambda e, cw=cw, bk=bk, ct=ct, dst=dst: e.scalar_tensor_tensor(out=dst[0:cw, 0:NS], in0=pb[bk][0:cw, 0:NS],
                                                                                                 scalar=omm_c[0:cw, ct:ct + 1], in1=tmix[0:cw, 0:NS],
                                                                                                 op0=ALU.mult, op1=ALU.add),
                                 reads=[pbB[bk], B_tmix, Bc], writes=[B_dst])
                        else:
                            n_ = nb_ * 128
                            S.mm([dict(out=pb[bk][0:cw, 0:n_ + 1], lhsT=Ws[:, kc, lt * 128:lt * 128 + cw], rhs=hT[:, kc, C0 + c * 128 - 1:C0 + c * 128 + n_],
                                       start=(kc == 0), stop=(kc == 7)) for kc in range(8)], reads=[B_hT, B_Ws], writes=[pbB[bk]])
                            S.do("vector", lambda e, cw=cw, bk=bk, ct=ct, n_=n_: e.tensor_scalar(out=tmix[0:cw, 0:n_], in0=pb[bk][0:cw, 0:n_],
                                                                                         scalar1=mu_c[0:cw, ct:ct + 1], scalar2=None, op0=ALU.mult),
                                 reads=[pbB[bk], Bc], writes=[B_tmix])
                            S.do("vector", lambda e, cw=cw, bk=bk, ct=ct, dst=dst, n_=n_: e.scalar_tensor_tensor(out=dst[0:cw, 0:n_], in0=pb[bk][0:cw, 1:n_ + 1],
                                                                                                        scalar=omm_c[0:cw, ct:ct + 1], in1=tmix[0:cw, 0:n_],
                                                                                                        op0=ALU.mult, op1=ALU.add),
                                 reads=[pbB[bk], B_tmix, Bc], writes=[B_dst])
                        if lt % 3 == 2:
                            yield
                    if isX and hh == 0:
                        load_Ws(1)
                    if isX and hh == 1:
                        load_Wp()
                if isX:
                    for q in range(4):
                        w_ = 512 if q < 3 else 288
                        bk = q % 2
                        for lt in range(4 * q, min(4 * q + 4, 15)):
                            cw = 32 if lt == 14 else 128
                            S.mm([("T", pb[bk][0:8, (lt % 4) * 128:(lt % 4) * 128 + cw], praw[0:cw, lt, :], ident[0:cw, 0:cw])], reads=[B_praw, B_ident], writes=[pbB[bk]])
                        if q == 3 and hh == 1:
                            continue
                        dc0 = (q * 1024 + a0g * 128) if q < 3 else 3072
                        s1f = S1[0:8, :, :].rearrange("p a t -> p (a t)")
                        S.do("vector", lambda e, bk=bk, w_=w_, s1f=s1f: e.tensor_copy(out=s1f[:, 0:w_], in_=pb[bk][0:8, 0:w_]), reads=[pbB[bk]], writes=[B_S1])
                        S.dma("sync", shift_s_d[:, dc0:dc0 + w_], s1f[0:NS, 0:w_], reads=[B_S1])
                        S.dma("sync", shift_p_d[:, dc0:dc0 + w_], s1f[NS:NS + 1, 0:w_], reads=[B_S1])
                    yield
                if DLIM < 2:
                    return
                yield
                S.do("scalar", lambda e: e.activation(out=lw[0:64, :], in_=lw[0:64, :], func=AF.Tanh), reads=[B_lw], writes=[B_lw])
                S.do("scalar", lambda e: e.activation(out=xgb[:, 0, :], in_=xg[:, 0, :], func=AF.Sigmoid), reads=[B_xg], writes=[B_xgb])
                S.do("scalar", lambda e: e.activation(out=xgb[0:32, 1, :], in_=xg[0:32, 1, :], func=AF.Sigmoid), reads=[B_xg], writes=[B_xgb])
                S.mm([dict(out=pb[4][:, a * 128:(a + 1) * 128], lhsT=w2a2[0:64, a * 128:(a + 1) * 128], rhs=lw[0:64, :]) for a in range(4)],
                     reads=[B_lw, Bc], writes=[pbB[4]])
                S.mm([dict(out=pb[5][:, a * 128:(a + 1) * 128], lhsT=w2a2[64:128, a * 128:(a + 1) * 128], rhs=lw[64:128, :]) for a in range(4)],
                     reads=[B_lw, Bc], writes=[pbB[5]])
                for a in range(4):
                    S.do("scalar", lambda e, a=a: e.activation(out=sgT[:, a, :], in_=pb[4][:, a * 128:(a + 1) * 128], func=AF.Sigmoid, bias=cols8[:, 0, a0g + a:a0g + a + 1]),
                         reads=[pbB[4], Bc], writes=[B_sg])
                    S.do("scalar", lambda e, a=a: e.activation(out=aT[:, a, :], in_=pb[5][:, a * 128:(a + 1) * 128], func=AF.Sigmoid, bias=cols8[:, 1, a0g + a:a0g + a + 1]),
                         reads=[pbB[5], Bc], writes=[B_a])
                S.mm([dict(out=pb[6][:, :], lhsT=xgb[:, 0, :], rhs=g2a[:, :], start=True, stop=False),
                      dict(out=pb[6][:, :], lhsT=xgb[:, 1, :], rhs=g2b[:, :], start=False, stop=True)], reads=[B_xgb, Bc], writes=[pbB[6]])
                S.do("scalar", lambda e: e.copy(out=gate_sb[:], in_=pb[6][:, :]), reads=[pbB[6]], writes=[B_gate])
                if DLIM < 3:
                    return
                yield
                if isX:
                    src_cs, B_src = sgT, B_sg
                else:
                    for a in range(4):
                        S.do("vector", lambda e, a=a: e.tensor_tensor_scan(out=cs[:, a, :], data0=ones128[:], data1=sgT[:, a, :], initial=0.0, op0=ALU.mult, op1=ALU.add),
                             reads=[B_sg, Bc], writes=[B_cs])
                    src_cs, B_src = cs, B_cs
                S.do("scalar", lambda e, src_cs=src_cs: e.activation(out=gamx[:, :, 1:129], in_=src_cs[:], func=AF.Exp, scale=-C0D), reads=[B_src], writes=[B_gam])
                S.do("scalar", lambda e, src_cs=src_cs: e.activation(out=ginv[:], in_=src_cs[:], func=AF.Exp, scale=C0D), reads=[B_src], writes=[B_gi])
                if DLIM < 4:
                    return
                yield
                ag = slice(a0g, a0g + 4)
                S.do("vector", lambda e: e.tensor_tensor(out=S1[:], in0=kT[:], in1=bc3(cols8[:, 2, ag]), op=ALU.mult), reads=[B_k, Bc], writes=[B_S1])
                S.do("vector", lambda e: e.tensor_tensor(out=sqb[:].rearrange("p (a t) -> p a t", a=4), in0=S1[:], in1=S1[:], op=ALU.mult), reads=[B_S1], writes=[B_sqb])
                S.mm([dict(out=pb[7][:, :], lhsT=bonesb[:], rhs=sqb[:])], reads=[B_sqb, Bc], writes=[pbB[7]])
                S.do("vector", lambda e: e.tensor_scalar(out=S2[:].rearrange("p a t -> p (a t)"), in0=pb[7][:, :], scalar1=1e-24, scalar2=None, op0=ALU.max),
                     reads=[pbB[7]], writes=[B_S2])
                S.do("scalar", lambda e: e.activation(out=S2[:], in_=S2[:], func=AF.Ln), reads=[B_S2], writes=[B_S2])
                S.do("scalar", lambda e: e.activation(out=S2[:], in_=S2[:], func=AF.Exp, scale=-0.5), reads=[B_S2], writes=[B_S2])
                S.do("vector", lambda e: e.tensor_tensor(out=S1[:], in0=S1[:], in1=S2[:], op=ALU.mult), reads=[B_S1, B_S2], writes=[B_S1])
                S.do("vector", lambda e: e.tensor_tensor(out=S2[:], in0=aT[:], in1=bc3(cols8[:, 3, ag]), op=ALU.mult), reads=[B_a, Bc], writes=[B_S2])
                S.do("vector", lambda e: e.tensor_tensor(out=S2[:], in0=S2[:], in1=bc3(omka[:, ag]), op=ALU.add), reads=[B_S2, Bc], writes=[B_S2])
                S.do("vector", lambda e: e.tensor_tensor(out=S2[:], in0=kT[:], in1=S2[:], op=ALU.mult), reads=[B_k, B_S2], writes=[B_S2])
                S.do("vector", lambda e: e.tensor_tensor(out=S3[:], in0=S1[:], in1=aT[:], op=ALU.mult), reads=[B_S1, B_a], writes=[B_S3])
                if DLIM < 5:
                    return
                yield
                S.do("scalar", lambda e: e.copy(out=VT[:], in_=vT[:]), reads=[B_v], writes=[B_VT])
                pbb = pb[2][:].bitcast(BF16)
                S.mm([("T", pbb[:, a * 128:(a + 1) * 128], VT[:, a, :], identb[:]) for a in range(4)], reads=[B_VT, B_identb], writes=[pbB[2]])
                S.do("scalar", lambda e, pbb=pbb: e.copy(out=Vtok[:], in_=pbb[:, 0:512]), reads=[pbB[2]], writes=[B_Vtok])
                S.do("vector", lambda e: e.tensor_tensor(out=rks[:].rearrange("p (a t) -> p a t", a=4), in0=rT[:], in1=S2[:], op=ALU.mult), reads=[B_r, B_S2], writes=[B_rks])
                S.do("vector", lambda e: e.tensor_tensor(out=rkT[:], in0=rks[:].rearrange("p (a t) -> p a t", a=4), in1=bc3(cols8[:, 4, ag]), op=ALU.mult),
                     reads=[B_rks, Bc], writes=[B_rk])
                S.mm([dict(out=pb[3][:, 2 * a:2 * a + 2], lhsT=rkT[:, a, :], rhs=bo2[:, :]) for a in range(4)], reads=[B_rk, Bc], writes=[pbB[3]])
                S.do("scalar", lambda e: e.copy(out=bon[:], in_=pb[3][:, 0:8]), reads=[pbB[3]], writes=[B_bon])
                if not isX:
                    S.do("vector", lambda e: e.scalar_tensor_tensor(out=AR[:, :, 0, :], in0=S1[:], scalar=-1.0, in1=gamx[:, :, 0:128], op0=ALU.mult, op1=ALU.mult),
                         reads=[B_S1, B_gam], writes=[B_AR])
                    S.do("vector", lambda e: e.tensor_tensor(out=AR[:, :, 1, :], in0=rT[:], in1=gamx[:, :, 1:129], op=ALU.mult), reads=[B_r, B_gam], writes=[B_AR])
                    S.do("scalar", lambda e: e.copy(out=ARbd[0:64, :, 0, :, :], in_=AR[0:64, :, :, :]), reads=[B_AR], writes=[B_ARbd])
                    S.do("scalar", lambda e: e.copy(out=ARbd[64:128, :, 1, :, :], in_=AR[64:128, :, :, :]), reads=[B_AR], writes=[B_ARbd])
                    S.do("vector", lambda e: e.tensor_tensor(out=BT[:], in0=S3[:], in1=ginv[:], op=ALU.mult), reads=[B_S3, B_gi], writes=[B_BT])
                    S.do("vector", lambda e: e.tensor_tensor(out=KhT[:], in0=S2[:], in1=ginv[:], op=ALU.mult), reads=[B_S2, B_gi], writes=[B_KhT])
                    for src, B_src2, dstk, B_dk, bk in ((BT, B_BT, Btok, B_Btok, 0), (KhT, B_KhT, Ktok, B_Ktok, 1)):
                        pbb = pb[bk][:].bitcast(BF16)
                        S.mm([("T", pbb[:, a * 128:(a + 1) * 128], src[:, a, :], identb[:]) for a in range(4)], reads=[B_src2, B_identb], writes=[pbB[bk]])
                        S.do("scalar", lambda e, pbb=pbb, dstk=dstk: e.copy(out=dstk[:], in_=pbb[:, 0:512]), reads=[pbB[bk]], writes=[B_dk])

            def rwkv_stage_b(c, isX, np_, gate_sb, B_gate, gamx, B_gam, Vtok, B_Vtok, Btok, B_Btok, Ktok, B_Ktok, bon, B_bon, AR, B_AR, ARbd, B_ARbd, BT, B_BT, KhT, B_KhT):
                rT = rT3[:, :, 0:128]
                vT = vT3[:, :, 0:128]
                if not isX:
                    if DLIM < 6:
                        return
                    S.do("vector", lambda e: e.tensor_tensor(out=Mg[:], in0=Mst[:], in1=gamx[:, :, 128:129].broadcast_to([128, 4, 64]), op=ALU.mult),
                         reads=[B_M, B_gam], writes=[B_Mg])
                    for a in range(4):
                        arbd = ARbd[:, a, :, :, :].rearrange("p h q t -> p (h q t)")
                        S.mm([dict(out=pb[2][:, :], lhsT=BT[:, a, :], rhs=arbd)], reads=[B_BT, B_ARbd], writes=[pbB[2]])
                        S.mm([dict(out=pb[3][:, :], lhsT=KhT[:, a, :], rhs=arbd)], reads=[B_KhT, B_ARbd], writes=[pbB[3]])
                        rm = rmask[:].rearrange("p a t -> p (a t)")
                        S.do("vector", lambda e, a=a, rm=rm: e.tensor_tensor(out=XS1[a][:], in0=pb[2][:, :], in1=rm, op=ALU.mult), reads=[pbB[2], Bc], writes=[B_XS1[a]])
                        S.do("vector", lambda e, a=a, rm=rm: e.tensor_tensor(out=XS2[a][:], in0=pb[3][:, :], in1=rm, op=ALU.mult), reads=[pbB[3], Bc], writes=[B_XS2[a]])
                        pbb = pb[6][:].bitcast(BF16)
                        S.mm([("T", pbb[:, hp * 128:(hp + 1) * 128], XS1[a][:, hp * 256:hp * 256 + 128], identb[:]) for hp in range(2)],
                             reads=[B_XS1[a], B_identb], writes=[pbB[6]])
                        S.do("scalar", lambda e, a=a, pbb=pbb: e.copy(out=L0[a][:].rearrange("p h t -> p (h t)"), in_=pbb[:, 0:256]), reads=[pbB[6]], writes=[B_L0[a]])
                        bw = (4, 5, 0, 1)[a]
                        specs = [dict(out=pb[bw][:, 0:128], lhsT=AR[:, a, 0, :], rhs=M0bd[:, a, :, :].rearrange("p h i -> p (h i)"), start=True, stop=False)]
                        for hp in range(2):
                            specs.append(dict(out=pb[bw][:, hp * 64:(hp + 1) * 64], lhsT=XS2[a][:, hp * 256:hp * 256 + 128],
                                              rhs=Vtok[:, a * 128 + hp * 64:a * 128 + (hp + 1) * 64], start=False, stop=(hp == 1)))
                        S.mm(specs, reads=[B_AR, B_M0bd, B_XS2[a], B_Vtok], writes=[pbB[bw]])
                        S.do("scalar", lambda e, a=a, bw=bw: e.copy(out=Ut[a][0][:], in_=pb[bw][:, 0:128]), reads=[pbB[bw]], writes=[B_U[a][0]])
                        yield
                    for k in range(7):
                        for a in range(4):
                            q0, q1 = k % 2, (k + 1) % 2
                            if k == 0:
                                Nk = [XS1[a][:, hp * 256:hp * 256 + 128] for hp in range(2)]
                                Lk = [L0[a][:, hp, :] for hp in range(2)]
                                rdN, rdL = [B_XS1[a]], [B_L0[a]]
                            else:
                                Nk = [LN[a][q0][:, hp * 256 + 128:hp * 256 + 256] for hp in range(2)]
                                Lk = [LN[a][q0][:, hp * 256:hp * 256 + 128] for hp in range(2)]
                                rdN, rdL = [B_LN[a][q0]], [B_LN[a][q0]]
                            bu = (4, 5, 0, 1)[a]
                            specs = [dict(out=pb[bu][:, 0:128], lhsT=identb[:], rhs=Ut[a][q0][:], start=True, stop=False)]
                            for hp in range(2):
                                specs.append(dict(out=pb[bu][:, hp * 64:(hp + 1) * 64], lhsT=Nk[hp], rhs=Ut[a][q0][:, hp * 64:(hp + 1) * 64], start=False, stop=(hp == 1)))
                            S.mm(specs, reads=[B_identb, B_U[a][q0]] + rdN, writes=[pbB[bu]])
                            S.do("scalar", lambda e, a=a, q1=q1, bu=bu: e.copy(out=Ut[a][q1][:], in_=pb[bu][:, 0:128]), reads=[pbB[bu]], writes=[B_U[a][q1]])
                            if k < 6:
                                bq = (2, 3, 6)[(k * 4 + a) % 3]
                                specs = []
                                for hp in range(2):
                                    specs.append(dict(out=pb[bq][:, hp * 256:hp * 256 + 128], lhsT=Nk[hp], rhs=Lk[hp]))
                                    specs.append(dict(out=pb[bq][:, hp * 256 + 128:hp * 256 + 256], lhsT=Lk[hp], rhs=Nk[hp]))
                                S.mm(specs, reads=rdN + rdL, writes=[pbB[bq]])
                                if a % 2 == 0:
                                    S.do("vector", lambda e, a=a, q1=q1, bq=bq: e.tensor_copy(out=LN[a][q1][:], in_=pb[bq][:, :]), reads=[pbB[bq]], writes=[B_LN[a][q1]])
                                else:
                                    S.do("scalar", lambda e, a=a, q1=q1, bq=bq: e.copy(out=LN[a][q1][:], in_=pb[bq][:, :]), reads=[pbB[bq]], writes=[B_LN[a][q1]])
                            if a % 2 == 1:
                                yield
                    for a in range(4):
                        Uf, B_Uf = Ut[a][1], B_U[a][1]
                        specs = [dict(out=pb[7][:, a * 128:(a + 1) * 128], lhsT=AR[:, a, 1, :], rhs=M0bd[:, a, :, :].rearrange("p h i -> p (h i)"), start=True, stop=False)]
                        for hp in range(2):
                            oc = pb[7][:, a * 128 + hp * 64:a * 128 + (hp + 1) * 64]
                            specs.append(dict(out=oc, lhsT=XS1[a][:, hp * 256 + 128:hp * 256 + 256], rhs=Uf[:, hp * 64:(hp + 1) * 64], start=False, stop=False))
                            specs.append(dict(out=oc, lhsT=XS2[a][:, hp * 256 + 128:hp * 256 + 256], rhs=Vtok[:, a * 128 + hp * 64:a * 128 + (hp + 1) * 64], start=False, stop=(hp == 1)))
                        S.mm(specs, reads=[B_AR, B_M0bd, B_XS1[a], B_XS2[a], B_Uf, B_Vtok], writes=[pbB[7]])
                        bs = (4, 5, 0, 1)[a]
                        S.mm([dict(out=pb[bs][:, 0:128], lhsT=Btok[:, a * 128:(a + 1) * 128], rhs=Uf[:], start=True, stop=False),
                              dict(out=pb[bs][:, 0:128], lhsT=Ktok[:, a * 128:(a + 1) * 128], rhs=Vtok[:, a * 128:(a + 1) * 128], start=False, stop=True)],
                             reads=[B_Btok, B_Ktok, B_Vtok, B_Uf], writes=[pbB[bs]])
                        for hp in range(2):
                            rows = slice(hp * 64, (hp + 1) * 64)
                            S.do("vector", lambda e, a=a, hp=hp, rows=rows, bs=bs: e.scalar_tensor_tensor(out=Mst[rows, a, :], in0=pb[bs][rows, hp * 64:(hp + 1) * 64],
                                                                                                      scalar=gamx[rows, a, 128:129], in1=Mg[rows, a, :],
                                                                                                      op0=ALU.mult, op1=ALU.add),
                                 reads=[pbB[bs], B_gam, B_Mg], writes=[B_M])
                    S.do("scalar", lambda e: e.copy(out=M0bd[0:64, :, 0, :], in_=Mst[0:64, :, :]), reads=[B_M], writes=[B_M0bd])
                    S.do("scalar", lambda e: e.copy(out=M0bd[64:128, :, 1, :], in_=Mst[64:128, :, :]), reads=[B_M], writes=[B_M0bd])
                    if c == NB - 1:
                        S.mm([("T", pb[6][0:64, a * 128:(a + 1) * 128], Mst[:, a, :], ident[:]) for a in range(4)], reads=[B_M, B_ident], writes=[pbB[6]])
                        S.do("vector", lambda e: e.tensor_copy(out=ysq[0:64, :], in_=pb[6][0:64, :]), reads=[pbB[6]], writes=[B_ysq])
                        S.dma("sync", wkv_p_d[8 * hh:8 * hh + 8].rearrange("h i j -> i h j"), ysq[0:64, :].rearrange("p (h j) -> p h j", h=8), reads=[B_ysq])
                else:
                    for b in range(NS):
                        Snat = ysq
                        S.dma("sync", Snat[0:64, :].rearrange("p (h j) -> p h j", h=8), swkv_d[b, 8 * hh:8 * hh + 8].rearrange("h i j -> i h j"), writes=[B_ysq])
                        S.mm([("T", pb[6][:, a * 64:(a + 1) * 64], Snat[0:64, a * 128:(a + 1) * 128], ident[0:64, 0:64]) for a in range(4)],
                             reads=[B_ysq, B_ident], writes=[pbB[6]])
                        S.do("vector", lambda e: e.tensor_copy(out=Mst[:].rearrange("p a i -> p (a i)"), in_=pb[6][:, 0:256]), reads=[pbB[6]], writes=[B_M])
                        for a in range(4):
                            S.do("vector", lambda e, a=a, b=b: e.tensor_scalar(out=tt[:, 0:128], in0=bones[:], scalar1=S1[:, a, b:b + 1], scalar2=-1.0, op0=ALU.mult, op1=ALU.mult),
                                 reads=[B_S1, Bc], writes=[B_tt])
                            S.do("vector", lambda e, a=a, b=b: e.tensor_scalar(out=tt[:, 128:192], in0=i64x2[:], scalar1=vT[:, a, b:b + 1], scalar2=None, op0=ALU.mult),
                                 reads=[B_v, Bc], writes=[B_tt])
                            bu = 4 + a % 2
                            S.mm([dict(out=pb[bu][:, 0:64], lhsT=tt[:, 0:128], rhs=Mst[:, a, :]),
                                  dict(out=pb[bu][:, 64:128], lhsT=bones[:], rhs=tt[:, 128:192])], reads=[B_tt, B_M, Bc], writes=[pbB[bu]])
                            S.do("vector", lambda e, a=a, b=b: e.tensor_scalar(out=Mg[:, a, :], in0=Mst[:, a, :], scalar1=gamx[:, a, 1 + b:2 + b], scalar2=None, op0=ALU.mult),
                                 reads=[B_M, B_gam], writes=[B_Mg])
                            S.do("vector", lambda e, a=a, b=b, bu=bu: e.scalar_tensor_tensor(out=Mg[:, a, :], in0=pb[bu][:, 0:64], scalar=S3[:, a, b:b + 1], in1=Mg[:, a, :],
                                                                                         op0=ALU.mult, op1=ALU.add),
                                 reads=[pbB[bu], B_S3], writes=[B_Mg])
                            S.do("vector", lambda e, a=a, b=b, bu=bu: e.scalar_tensor_tensor(out=Mg[:, a, :], in0=pb[bu][:, 64:128], scalar=S2[:, a, b:b + 1], in1=Mg[:, a, :],
                                                                                         op0=ALU.mult, op1=ALU.add),
                                 reads=[pbB[bu], B_S2], writes=[B_Mg])
                        S.do("vector", lambda e: e.memset(bv[:], 0.0), writes=[B_bv])
                        for hp in range(2):
                            rows = slice(hp * 64, (hp + 1) * 64)
                            S.do("vector", lambda e, b=b, hp=hp, rows=rows: e.tensor_copy(out=bv[rows, :].rearrange("p (a h q) -> p a h q", a=4, h=2)[:, :, hp, b:b + 1],
                                                                                      in_=rT[rows, :, b:b + 1]), reads=[B_r], writes=[B_bv])
                        specs = []
                        for a in range(4):
                            for hp in range(2):
                                lhsT = bv[:, :].rearrange("p (a h q) -> p a h q", a=4, h=2)[:, a, hp, 0:NS]
                                specs.append(dict(out=pb[7][0:NS, a * 128 + hp * 64:a * 128 + (hp + 1) * 64], lhsT=lhsT, rhs=Mg[:, a, :],
                                                  start=(b == 0 and a == 0 and hp == 0), stop=(b == NS - 1 and a == 3 and hp == 1)))
                        S.mm(specs, reads=[B_bv, B_Mg], writes=[pbB[7]])
                        S.mm([("T", pb[6][0:64, a * 128:(a + 1) * 128], Mg[:, a, :], ident[:]) for a in range(4)], reads=[B_Mg, B_ident], writes=[pbB[6]])
                        S.do("vector", lambda e: e.tensor_copy(out=tt[0:64, :], in_=pb[6][0:64, :]), reads=[pbB[6]], writes=[B_tt])
                        S.dma("sync", wkv_s_d[b, 8 * hh:8 * hh + 8].rearrange("h i j -> i h j"), tt[0:64, :].rearrange("p (h j) -> p h j", h=8), reads=[B_tt])
                if DLIM < 7:
                    return
                yield
                y3 = pb[7][0:np_, :].rearrange("p (h i) -> p h i", h=8)

                def b8(c0_):
                    return stt[0:np_, c0_:c0_ + 8].unsqueeze(2).broadcast_to([np_, 8, 64])

                def t3(tile_):
                    return tile_[0:np_, :].rearrange("p (h i) -> p h i", h=8)
                S.do("vector", lambda e: e.tensor_reduce(out=stt[0:np_, 0:8], in_=y3, axis=AX.X, op=ALU.add), reads=[pbB[7]], writes=[B_st])
                S.do("scalar", lambda e: e.activation(out=ysq[0:np_, :], in_=pb[7][0:np_, :], func=AF.Square), reads=[pbB[7], B_st], writes=[B_ysq])
                S.do("vector", lambda e: e.tensor_reduce(out=stt[0:np_, 8:16], in_=t3(ysq), axis=AX.X, op=ALU.add), reads=[B_ysq], writes=[B_st])
                S.do("vector", lambda e: e.tensor_scalar(out=stt[0:np_, 16:24], in0=stt[0:np_, 0:8], scalar1=1.0 / 64, scalar2=None, op0=ALU.mult), reads=[B_st], writes=[B_st])
                S.do("vector", lambda e: e.tensor_tensor(out=stt[0:np_, 24:32], in0=stt[0:np_, 16:24], in1=stt[0:np_, 16:24], op=ALU.mult), reads=[B_st], writes=[B_st])
                S.do("vector", lambda e: e.scalar_tensor_tensor(out=stt[0:np_, 32:40], in0=stt[0:np_, 8:16], scalar=1.0 / 64, in1=stt[0:np_, 24:32], op0=ALU.mult, op1=ALU.subtract),
                     reads=[B_st], writes=[B_st])
                S.do("vector", lambda e: e.tensor_scalar(out=stt[0:np_, 32:40], in0=stt[0:np_, 32:40], scalar1=64e-5, scalar2=None, op0=ALU.add), reads=[B_st], writes=[B_st])
                S.do("scalar", lambda e: e.activation(out=stt[0:np_, 32:40], in_=stt[0:np_, 32:40], func=AF.Sqrt), reads=[B_st], writes=[B_st])
                S.do("vector", lambda e: e.reciprocal(out=stt[0:np_, 40:48], in_=stt[0:np_, 32:40]), reads=[B_st], writes=[B_st])
                S.do("vector", lambda e: e.tensor_tensor(out=t3(tt), in0=y3, in1=b8(16), op=ALU.subtract), reads=[pbB[7], B_st], writes=[B_tt])
                S.do("vector", lambda e: e.tensor_tensor(out=t3(tt), in0=t3(tt), in1=b8(40), op=ALU.mult), reads=[B_tt, B_st], writes=[B_tt])
                S.do("vector", lambda e: e.tensor_tensor(out=tt[0:np_, :], in0=tt[0:np_, :], in1=gnw[0:np_, :], op=ALU.mult), reads=[B_tt, Bc], writes=[B_tt])
                S.do("vector", lambda e: e.tensor_tensor(out=tt[0:np_, :], in0=tt[0:np_, :], in1=gnb[0:np_, :], op=ALU.add), reads=[B_tt, Bc], writes=[B_tt])
                S.do("vector", lambda e: e.tensor_tensor(out=t3(bv), in0=t3(Vtok), in1=bon[0:np_, :].unsqueeze(2).broadcast_to([np_, 8, 64]), op=ALU.mult),
                     reads=[B_Vtok, B_bon], writes=[B_bv])
                S.do("vector", lambda e: e.tensor_tensor(out=tt[0:np_, :], in0=tt[0:np_, :], in1=bv[0:np_, :], op=ALU.add), reads=[B_tt, B_bv], writes=[B_tt])
                S.do("vector", lambda e: e.tensor_tensor(out=ob[0:np_, :], in0=tt[0:np_, :], in1=gate_sb[0:np_, :], op=ALU.mult), reads=[B_tt, B_gate], writes=[B_ob])
                pbb = pb[6][:].bitcast(BF16)
                S.mm([("T", pbb[:, a * 128:a * 128 + np_], ob[0:np_, a * 128:(a + 1) * 128], identb[0:np_, 0:np_]) for a in range(4)], reads=[B_ob, B_identb], writes=[pbB[6]])
                col0 = T if isX else c * 128
                S.do("scalar", lambda e, pbb=pbb, col0=col0, np_=np_: e.copy(out=o_bT[:, a0g:a0g + 4, col0:col0 + np_],
                                                                          in_=pbb[:, 0:512].rearrange("p (a t) -> p a t", a=4)[:, :, 0:np_]),
                     reads=[pbB[6]], writes=[B_obT])
                if "rw" in debug and hh == 0 and c in (0, 1, NB):
                    for nm, tl, Bt in (("rT", rT, B_r), ("kap", S1, B_S1), ("k2", S2, B_S2), ("aT", aT, B_a), ("sg", sgT, B_sg)):
                        o_ = dbg_out(f"{nm}_{c}", [128, 4, 128])
                        S.dma("sync", o_, tl[:], reads=[Bt])
                    o_ = dbg_out(f"tt_{c}", [128, 512])
                    S.dma("sync", o_, tt[:], reads=[B_tt])
                    o_ = dbg_out(f"gate_{c}", [128, 512])
                    S.dma("sync", o_, gate_sb[:], reads=[B_gate])
            def interleave(g1, g2):
                gens = [g for g in (g1, g2) if g is not None]
                while gens:
                    for g in list(gens):
                        try:
                            next(g)
                        except StopIteration:
                            gens.remove(g)

            for c in range(NB + 1):
                interleave(rwkv_block(c, "A"), None)
                interleave(rwkv_block(c, "B"), None)
            S.barrier()

        phW = st.enter_context(ExitStack())
        Wsh = sb(phW, "Wsh", [128, 8 * 1824], BF16)
        Ws_g = Wsh[:, :].rearrange("p (k c) -> p k c", k=8)
        Wpb_v = Wsh[:, 0:8192].rearrange("p (k c) -> p k c", k=8)
        Wpa_v = Wsh[:, 8192:12288].rearrange("p (k c) -> p k c", k=4)
        B_Ws_g = Buf()

        def load_Wp():
            S.dma("gpsimd", Wpb_v, w_pb_d.rearrange("(kc p) c -> p kc c", p=128), writes=[B_Ws_g])
            S.dma("gpsimd", Wpa_v, w_pa_d.rearrange("(kc p) c -> p kc c", p=128), writes=[B_Ws_g])

        def load_Ws(hh_):
            for j, base in enumerate((0, 1024, 2048)):
                S.dma("gpsimd", Ws_g[:, :, j * 512:(j + 1) * 512], w_in_v[:, :, A_COLS + base + hh_ * 512:A_COLS + base + hh_ * 512 + 512], writes=[B_Ws_g])
            if hh_ == 0:
                S.dma("gpsimd", Ws_g[:, :, 1536:1824], w_in_v[:, :, A_COLS + 3072:A_COLS + 3360], writes=[B_Ws_g])
        for hh in range(2):
            if "stopC" in debug or PH_STOP == "B":
                break
            with ExitStack() as ph:
                rwkv_pass(hh, ph)


        def wload(dst_tile, src_ap, B_):
            S.dma("gpsimd", dst_tile, src_ap, writes=[B_])

        try:
          if "stopD" not in debug and PH_STOP not in ("B", "D"):
            S.barrier()
            phWo = ExitStack()
            Wout = sb(phWo, "Wout", [128, 8, 1024], BF16)
            B_Wout = Buf()
            with ExitStack() as ph:
                Wg = sb(ph, "Wg", [128, 8, 2048], BF16)
                Wpa, Wpb = Wpa_v, Wpb_v
                B_W = Buf()
                for q in range(2):
                    wload(Wg[:, :, q * 1024:(q + 1) * 1024], w_in_v[:, :, A_COLS + SH + q * 1024:A_COLS + SH + (q + 1) * 1024], B_W)
                wload(Wout[:], w_out_d.rearrange("(kc p) c -> p kc c", p=128), B_Wout)
                oaT2 = sb(ph, "oaT", [128, 4, T + 128], BF16)
                B_oaT2 = Buf("oaT2")
                S.dma("sync", oaT2[:], oaT_d, writes=[B_oaT2])
                mtmp = sb(ph, "mtmp", [128, 8, 512], BF16)
                B_mtmp = Buf()
                sga = [sb(ph, f"sga{i}", [128, 512], F32) for i in range(2)]
                sgb = [sb(ph, f"sgb{i}", [128, 512], F32) for i in range(2)]
                B_sga, B_sgb = [Buf(), Buf()], [Buf(), Buf()]
                B_ob_ch = [Buf() for _ in range(5)]
                for ch in range(5):
                    if E1LIM < 1 or (E1LIM < 3 and ch >= 1):
                        continue
                    n_ = 512 if ch < 4 else 128
                    hc0 = C0 + ch * 512 if ch < 4 else SC0
                    oc0 = ch * 512 if ch < 4 else T
                    for m in range(8):
                        u = (ch * 8 + m) % 2
                        bks = [4 * u + j for j in range(4)]
                        S.mm([dict(out=pb[bks[0]][:, 0:n_], lhsT=Wg[:, kc, m * 128:(m + 1) * 128], rhs=hT[:, kc, hc0:hc0 + n_], start=(kc == 0), stop=(kc == 7)) for kc in range(8)],
                             reads=[B_W, B_hT], writes=[pbB[bks[0]]])
                        S.mm([dict(out=pb[bks[1]][:, 0:n_], lhsT=Wg[:, kc, 1024 + m * 128:1024 + (m + 1) * 128], rhs=hT[:, kc, hc0:hc0 + n_], start=(kc == 0), stop=(kc == 7)) for kc in range(8)],
                             reads=[B_W, B_hT], writes=[pbB[bks[1]]])
                        S.mm([dict(out=pb[bks[2]][:, 0:n_], lhsT=Wpa[:, kc, m * 128:(m + 1) * 128], rhs=oaT2[:, kc, oc0:oc0 + n_], start=(kc == 0), stop=(kc == 3)) for kc in range(4)],
                             reads=[B_Ws_g, B_oaT2], writes=[pbB[bks[2]]])
                        S.mm([dict(out=pb[bks[3]][:, 0:n_], lhsT=Wpb[:, kc, m * 128:(m + 1) * 128], rhs=o_bT[:, kc, oc0:oc0 + n_], start=(kc == 0), stop=(kc == 7)) for kc in range(8)],
                             reads=[B_Ws_g, B_ob_ch[ch]], writes=[pbB[bks[3]]])
                        if E1LIM < 2:
                            continue
                        S.do("scalar", lambda e, u=u, b_=bks[0], n_=n_: e.activation(out=sga[u][:, 0:n_], in_=pb[b_][:, 0:n_], func=AF.Sigmoid), reads=[pbB[bks[0]]], writes=[B_sga[u]])
                        S.do("scalar", lambda e, u=u, b_=bks[1], n_=n_: e.activation(out=sgb[u][:, 0:n_], in_=pb[b_][:, 0:n_], func=AF.Sigmoid), reads=[pbB[bks[1]]], writes=[B_sgb[u]])
                        S.do("vector", lambda e, u=u, b_=bks[2], n_=n_: e.tensor_tensor(out=sga[u][:, 0:n_], in0=sga[u][:, 0:n_], in1=pb[b_][:, 0:n_], op=ALU.mult),
                             reads=[pbB[bks[2]], B_sga[u]], writes=[B_sga[u]])
                        S.do("vector", lambda e, u=u, b_=bks[3], n_=n_: e.tensor_tensor(out=sgb[u][:, 0:n_], in0=sgb[u][:, 0:n_], in1=pb[b_][:, 0:n_], op=ALU.mult),
                             reads=[pbB[bks[3]], B_sgb[u]], writes=[B_sgb[u]])
                        S.do("vector", lambda e, u=u, m=m, n_=n_: e.tensor_tensor(out=mtmp[:, m, 0:n_], in0=sga[u][:, 0:n_], in1=sgb[u][:, 0:n_], op=ALU.add),
                             reads=[B_sga[u], B_sgb[u]], writes=[B_mtmp])
                    S.do("scalar", lambda e, oc0=oc0, n_=n_: e.copy(out=o_bT[:, :, oc0:oc0 + n_], in_=mtmp[:, :, 0:n_]), reads=[B_mtmp], writes=[B_ob_ch[ch]])
                S.barrier()
            if ELIM < 2:
                raise StopIteration
            with ExitStack() as ph:
                B_W = B_Wout

                def prod_e2(i, xt, B_xt):
                    oc0 = i * 128 if i < NB else T
                    for hf in range(2):
                        bk = 2 * (i % 2) + hf
                        S.mm([dict(out=pb[bk][:, :], lhsT=o_bT[:, kc, oc0:oc0 + 128], rhs=Wout[:, kc, hf * 512:(hf + 1) * 512], start=(kc == 0), stop=(kc == 7)) for kc in range(8)],
                             reads=[B_obT, B_W], writes=[pbB[bk]])
                        S.do("vector", lambda e, bk=bk, hf=hf, xt=xt: e.tensor_tensor(out=xt[:, hf * 512:(hf + 1) * 512], in0=xt[:, hf * 512:(hf + 1) * 512], in1=pb[bk][:, :], op=ALU.add),
                             reads=[pbB[bk]], writes=[B_xt])
                    S.dma("sync", x1_d[oc0:oc0 + 128, :], xt[:], reads=[B_xt])
                blocks = [(x_d[i * 128:(i + 1) * 128, :], 128, C0 + i * 128) for i in range(NB)]
                blocks.append((xs_d, NS, SC0))
                rmsnorm_to_T(ph, blocks, norm_ffn_d, hT, B_hT, "E2", producer=prod_e2)
                S.barrier()
            phWo.close()
            phW.close()
            if ELIM < 3:
                raise StopIteration
            with ExitStack() as ph:
                actT2 = sb(ph, "actT2", [128, 14, T + 128], BF16)
                B_act = Buf()

                def act_tile(f):
                    if f < 8:
                        return o_bT[:, f, :]
                    return actT2[:, f - 8, :]
                Wd = sb(ph, "Wd", [128, 22, 1024], BF16)
                B_Wd = Buf()
                wd_v = w_down_d.rearrange("(f p) c -> p f c", p=128)
                with ExitStack() as ph3:
                    WG = [sb(ph3, f"WG{i}", [128, 8, 128], BF16) for i in range(2)]
                    WU = [sb(ph3, f"WU{i}", [128, 8, 128], BF16) for i in range(2)]
                    B_WG, B_WU = [Buf(), Buf()], [Buf(), Buf()]
                    sgl = [sb(ph3, f"sgl{i}", [128, 512], F32) for i in range(2)]
                    B_sgl = [Buf(), Buf()]
                    wg_v = w_gate_d.rearrange("(kc p) c -> p kc c", p=128)
                    wu_v = w_up_d.rearrange("(kc p) c -> p kc c", p=128)
                    for f in range(22):
                        s_ = f % 2
                        wload(WG[s_][:], wg_v[:, :, f * 128:(f + 1) * 128], B_WG[s_])
                        wload(WU[s_][:], wu_v[:, :, f * 128:(f + 1) * 128], B_WU[s_])
                        if 2 <= f < 13:
                            q = f - 2
                            wload(Wd[:, 2 * q:2 * q + 2, :], wd_v[:, 2 * q:2 * q + 2, :], B_Wd)
                        for ch in range(5):
                            n_ = 512 if ch < 4 else 128
                            hc0 = C0 + ch * 512 if ch < 4 else SC0
                            oc0 = ch * 512 if ch < 4 else T
                            u = (f * 5 + ch) % 2
                            bg, bu_ = 2 * u, 2 * u + 1
                            S.mm([dict(out=pb[bg][:, 0:n_], lhsT=WG[s_][:, kc, :], rhs=hT[:, kc, hc0:hc0 + n_], start=(kc == 0), stop=(kc == 7)) for kc in range(8)],
                                 reads=[B_WG[s_], B_hT], writes=[pbB[bg]])
                            S.mm([dict(out=pb[bu_][:, 0:n_], lhsT=WU[s_][:, kc, :], rhs=hT[:, kc, hc0:hc0 + n_], start=(kc == 0), stop=(kc == 7)) for kc in range(8)],
                                 reads=[B_WU[s_], B_hT], writes=[pbB[bu_]])
                            S.do("scalar", lambda e, u=u, bg=bg, n_=n_: e.activation(out=sgl[u][:, 0:n_], in_=pb[bg][:, 0:n_], func=AF.Silu), reads=[pbB[bg]], writes=[B_sgl[u]])
                            S.do("vector", lambda e, u=u, bu_=bu_, n_=n_, f=f, oc0=oc0: e.tensor_tensor(out=act_tile(f)[:, oc0:oc0 + n_], in0=sgl[u][:, 0:n_], in1=pb[bu_][:, 0:n_], op=ALU.mult),
                                 reads=[B_sgl[u], pbB[bu_]], writes=[B_act])
                    S.barrier()
                if ELIM < 4:
                    raise StopIteration
                B_W = B_Wd

                def prod_e4(i, xt, B_xt):
                    oc0 = i * 128 if i < NB else T
                    for hf in range(2):
                        bk = 2 * (i % 2) + hf
                        S.mm([dict(out=pb[bk][:, :], lhsT=act_tile(f)[:, oc0:oc0 + 128], rhs=Wd[:, f, hf * 512:(hf + 1) * 512], start=(f == 0), stop=(f == 21)) for f in range(22)],
                             reads=[B_act, B_W], writes=[pbB[bk]])
                        S.do("vector", lambda e, bk=bk, hf=hf, xt=xt: e.tensor_tensor(out=xt[:, hf * 512:(hf + 1) * 512], in0=xt[:, hf * 512:(hf + 1) * 512], in1=pb[bk][:, :], op=ALU.add),
                             reads=[pbB[bk]], writes=[B_xt])
                blocks = [(x1_d[i * 128:(i + 1) * 128, :], 128, 0) for i in range(NB)]
                blocks.append((x1_d[T:T + 128, :], 128, 0))
                outs_ = [(y_d[i * 128:(i + 1) * 128, :], 128) for i in range(NB)] + [(ys_d, NS)]
                rmsnorm_to_T(ph, blocks, norm_final_d, None, None, "E4", producer=prod_e4, out_rows=outs_)
                S.barrier()
        except StopIteration:
            S.barrier()
        if "obT" in debug:
            o_ = dbg_out("obT", [128, 8, T + 128])
            with ExitStack() as phd:
                tmpd = sb(phd, "dbgtmp2", [128, 8, T + 128], F32)
                Bt = Buf()
                S.do("vector", lambda e: e.tensor_copy(out=tmpd[:], in_=o_bT[:]), reads=[B_obT], writes=[Bt])
                S.dma("sync", o_, tmpd[:], reads=[Bt])
                S.barrier()
        S.final_wait()
        S.emit()
    print("instruction counts:", S.ninst, "sem incs:", S.cnt, flush=True)
    return nc, list(dbg.keys())


_CACHE = {}


def _get_nc(debug=()):
    key = tuple(debug)
    if key not in _CACHE:
        _CACHE[key] = build(debug)
    return _CACHE[key]


def kernel(_debug=(), **inputs):
    nc, dbg_names = _get_nc(_debug)
    consts = make_consts()
    f32 = lambda a: np.ascontiguousarray(np.asarray(a, dtype=np.float32))
    in_maps = []
    for c in range(8):
        m = dict(consts)
        m["x"] = f32(inputs["x_prompt"][c])
        m["xs"] = f32(inputs["x_sample"][4 * c:4 * c + 4, 0, :])
        m["w_in"] = f32(inputs["w_in"][0])
        m["norm_mix"] = f32(inputs["norm_mix"])
        m["w_proj_a"] = f32(inputs["w_proj_a"][0])
        m["w_proj_b"] = f32(inputs["w_proj_b"][0])
        m["w_out"] = f32(inputs["w_out"][0])
        m["norm_ffn"] = f32(inputs["norm_ffn"])
        m["w_gate"] = f32(inputs["w_gate"][0])
        m["w_up"] = f32(inputs["w_up"][0])
        m["w_down"] = f32(inputs["w_down"][0])
        m["norm_final"] = f32(inputs["norm_final"]).reshape(1, D)
        m["sshift"] = f32(inputs["state_shift"][0, 4 * c:4 * c + 4])
        m["swkv"] = f32(inputs["state_wkv"][0, 4 * c:4 * c + 4])
        mu = np.zeros(27 * 128, np.float32)
        mu[:SH] = np.asarray(inputs["mu_shift"], np.float32)[0]
        m["mu_c"] = np.ascontiguousarray(mu.reshape(27, 128).T)
        m["cols8"] = np.ascontiguousarray(np.stack([np.asarray(inputs[k_], np.float32).reshape(8, 128).T for k_ in ("w0", "a0", "k_k", "k_a", "r_k")], 1))
        m["w2a2"] = np.ascontiguousarray(np.concatenate([f32(inputs["w2"][0]), f32(inputs["a2"][0])], 0))
        m["g2"] = f32(inputs["g2"][0])
        m["gn_w"] = f32(inputs["gn_w"])
        m["gn_b"] = f32(inputs["gn_b"])
        for g in range(3):
            ck = inputs[f"cache_kv_g{g + 1}"][0, 4 * c:4 * c + 4]
            m[f"ck{g + 1}"] = f32(ck).reshape(NS, ck.shape[1], 1024)
        in_maps.append(m)
    res = run_bass_kernel_spmd(nc, in_maps, core_ids=list(range(8)))
    R = res.results
    if _debug:
        return R
    def cat(name, shape=None):
        a = np.concatenate([np.asarray(R[c][name]) for c in range(8)], 0)
        return a
    y_prompt = np.stack([np.asarray(R[c]["y"]) for c in range(8)], 0).astype(np.float32)
    y_sample = cat("ys").reshape(32, 1, D).astype(np.float32)
    kvp = [np.stack([np.asarray(R[c][f"kv{g + 1}_p"]) for c in range(8)], 0).reshape(1, 8, -1, 2, 8, 64).astype(np.float32) for g in range(3)]
    shp = cat("shift_p").reshape(1, 8, SH).astype(np.float32)
    wkvp = np.stack([np.asarray(R[c]["wkv_p"]) for c in range(8)], 0).reshape(1, 8, 16, 64, 64).astype(np.float32)
    kvs = [cat(f"kv{g + 1}_s").reshape(1, 32, 1, 2, 8, 64).astype(np.float32) for g in range(3)]
    shs = cat("shift_s").reshape(1, 32, SH).astype(np.float32)
    wkvs = cat("wkv_s").reshape(1, 32, 16, 64, 64).astype(np.float32)
    return (y_prompt, y_sample, kvp[0], kvp[1], kvp[2], shp, wkvp, kvs[0], kvs[1], kvs[2], shs, wkvs)
```

```python
import numpy as np
from contextlib import ExitStack
import concourse.bass as bass
import concourse.mybir as mybir
from concourse.bass_utils import run_bass_kernel_spmd

F32 = mybir.dt.float32
BF16 = mybir.dt.bfloat16
ALU = mybir.AluOpType
AF = mybir.ActivationFunctionType
AX = mybir.AxisListType

import os
ENGS = ("tensor", "vector", "scalar", "gpsimd", "sync")
DLIM = int(os.environ.get("DLIM", "9"))
DNOX = int(os.environ.get("DNOX", "0"))
DNB = int(os.environ.get("DNB", "16"))
ELIM = int(os.environ.get("ELIM", "9"))
PH_STOP = os.environ.get("PH_STOP", "")
E1LIM = int(os.environ.get("E1LIM", "9"))
EPOCH = 12000

D = 1024
T = 2048
NB = 16
NS = 4
A_COLS = 4608
SH = 3360
DFF = 2816
GROUPS = ((128, 1), (512, 4), (2048, 16))
C0 = 2
HTW = C0 + T + 128
SC0 = C0 + T


class Buf:
    __slots__ = ("w", "r", "name")

    def __init__(self, name=""):
        self.w = None
        self.r = {}
        self.name = name


def _hkey(h):
    if h[0] == "dma":
        return ("dma", h[3]), h[2]
    return (h[0], h[1]), h[2]


class Sched:
    def __init__(self, nc, stack):
        self.nc = nc
        self.stack = stack
        self.q = {e: [] for e in ENGS}
        self.cnt = {e: 0 for e in ENGS}
        self.sems = {e: [] for e in ENGS}
        self.waited = {e: {} for e in ENGS}
        self.pools = {}
        self.outstanding_dma = []
        self.ninst = {e: 0 for e in ENGS}

    def _sem(self, eng, epoch):
        while len(self.sems[eng]) <= epoch:
            s = self.stack.enter_context(self.nc.semaphore(f"s_{eng}_{len(self.sems[eng])}"))
            self.sems[eng].append(s)
        return self.sems[eng][epoch]

    def _emit_waits(self, eng, deps):
        w = self.waited[eng]
        for d in deps:
            if d is None:
                continue
            key, n = _hkey(d)
            if w.get(key, 0) >= n:
                continue
            w[key] = n
            if d[0] == "dma":
                sem = d[1]
            else:
                sem = self._sem(d[0], d[1])
            self.q[eng].append(lambda e, sem=sem, n=n: e.wait_ge(sem, n))

    def op(self, eng, fn, deps=()):
        self._emit_waits(eng, deps)
        c = self.cnt[eng]
        epoch = c // EPOCH
        n = c % EPOCH + 1
        self.cnt[eng] = c + 1
        sem = self._sem(eng, epoch)
        self.q[eng].append(lambda e, fn=fn, sem=sem: fn(e).then_inc(sem, 1))
        self.ninst[eng] += 1
        return (eng, epoch, n)

    def op_noinc(self, eng, fn):
        self.q[eng].append(lambda e, fn=fn: fn(e))
        self.ninst[eng] += 1

    def dma_raw(self, eng, out, in_, deps, **kw):
        if eng not in self.pools:
            k = 4 if eng == "gpsimd" else 10
            self.pools[eng] = {"sems": [self.stack.enter_context(self.nc.semaphore(f"d_{eng}_{i}")) for i in range(k)],
                               "vals": [0] * k, "last": [None] * k, "i": 0}
        p = self.pools[eng]
        i = p["i"]
        p["i"] = (i + 1) % len(p["sems"])
        deps = list(deps) + [p["last"][i]]
        self._emit_waits(eng, deps)
        p["vals"][i] += 16
        sem, val = p["sems"][i], p["vals"][i]
        self.q[eng].append(lambda e, out=out, in_=in_, sem=sem, kw=kw: e.dma_start(out=out, in_=in_, **kw).then_inc(sem, 16))
        h = ("dma", sem, val, (eng, i))
        p["last"][i] = h
        self.outstanding_dma.append(h)
        self.ninst[eng] += 1
        return h

    @staticmethod
    def _deps(reads, writes):
        deps = []
        for b in reads:
            deps.append(b.w)
        for b in writes:
            deps.append(b.w)
            deps.extend(b.r.values())
        return deps

    @staticmethod
    def _update(h, reads, writes):
        key, n = _hkey(h)
        for b in reads:
            old = b.r.get(key)
            if old is None or _hkey(old)[1] < n:
                b.r[key] = h
        for b in writes:
            b.w = h
            b.r = {}

    def do(self, eng, fn, reads=(), writes=()):
        h = self.op(eng, fn, self._deps(reads, writes))
        self._update(h, reads, writes)
        return h

    def mm(self, specs, reads=(), writes=()):
        self._emit_waits("tensor", [d_ for d_ in self._deps(reads, writes) if d_ is not None and d_[0] != "tensor"])

        def mk(sp):
            if isinstance(sp, tuple):
                _, o, i, idn = sp
                return lambda e: e.transpose(o, i, idn)
            return lambda e: e.matmul(sp["out"], lhsT=sp["lhsT"], rhs=sp["rhs"], start=sp.get("start", True),
                                      stop=sp.get("stop", True), skip_group_check=True)
        for sp in specs[:-1]:
            self.op_noinc("tensor", mk(sp))
        h = self.op("tensor", mk(specs[-1]))
        self._update(h, reads, writes)
        return h

    def dma(self, eng, out, in_, reads=(), writes=(), **kw):
        h = self.dma_raw(eng, out, in_, self._deps(reads, writes), **kw)
        self._update(h, reads, writes)
        return h

    def barrier(self):
        hs = []
        for e in ENGS:
            c = self.cnt[e]
            if c:
                hs.append((e, (c - 1) // EPOCH, (c - 1) % EPOCH + 1))
        hs += self.outstanding_dma
        self.outstanding_dma = []
        for e in ENGS:
            self._emit_waits(e, hs)

    def final_wait(self):
        self._emit_waits("sync", self.outstanding_dma)

    def emit(self):
        with self.nc.Block() as block:
            for name in ENGS:
                lst = self.q[name]
                if not lst:
                    continue

                def body(e, lst=lst):
                    for f in lst:
                        f(e)
                getattr(block, name)(body)


def make_consts():
    c = {}
    c["ident"] = np.eye(128, dtype=np.float32)
    k = np.arange(128)[:, None]
    q = np.arange(128)[None, :]
    mP = (k >= q).astype(np.float32)
    mC = (k <= q).astype(np.float32)
    c["amask"] = np.stack([mP, mC, mP, mC], 1).copy()
    half = 32
    freqs = (10000.0 ** (-np.arange(half, dtype=np.float32) / half)).astype(np.float32)

    def tab(pos):
        ang = pos.astype(np.float32)[..., None] * freqs
        return np.cos(ang).astype(np.float32), np.sin(ang).astype(np.float32)
    rc = np.zeros((3, 128, NB, 32), np.float32)
    rs = np.zeros((3, 128, NB, 2, 32), np.float32)
    for g, (_, d) in enumerate(GROUPS):
        nb = NB // d
        for r in range(d):
            for n in range(nb):
                sb = r * nb + n
                pos = r + d * (128 * n + np.arange(128))
                co, si = tab(pos)
                rc[g, :, sb] = co
                rs[g, :, sb, 0] = -si
                rs[g, :, sb, 1] = si
    c["rope_c"] = rc
    c["rope_s"] = rs
    co, si = tab(np.array([16384]))
    c["rope_cs"] = np.concatenate([co[0], -si[0], si[0]])[None, :].copy()
    ep = np.zeros((128, 4, 4), np.float32)
    for p in range(4):
        ep[:, p, p] = 1.0
    c["epair"] = ep
    selb = np.zeros((NS, NS, 128), np.float32)
    selp = np.zeros((4, 4, 64), np.float32)
    for b in range(4):
        selb[b, b, :] = 1.0
        selp[b, b, :] = 1.0
    c["selb"] = selb
    c["selp"] = selp
    ss_, tt_ = np.arange(128)[:, None], np.arange(128)[None, :]
    su = (ss_ < tt_).astype(np.float32)
    iu = (ss_ <= tt_).astype(np.float32)
    c["rmask"] = np.stack([su, iu, su, iu], 1).copy()
    c["bones"] = ((ss_ // 64) == (tt_ // 64)).astype(np.float32)
    c["bo2"] = ((np.arange(128)[:, None] // 64) == np.arange(2)[None, :]).astype(np.float32)
    c["i64x2"] = np.concatenate([np.eye(64, dtype=np.float32)] * 2, 0)
    return c


def build(debug=()):
    nc = bass.Bass("TRN2", target_bir_lowering=False)

    def din(name, shape, dt=F32):
        return nc.dram_tensor(name, list(shape), dt, kind="ExternalInput").ap()

    def dout(name, shape, dt=F32):
        return nc.dram_tensor(name, list(shape), dt, kind="ExternalOutput").ap()

    x_d = din("x", [T, D])
    xs_d = din("xs", [NS, D])
    w_in_d = din("w_in", [D, 10016])
    norm_mix_d = din("norm_mix", [1, D])
    ident_d = din("ident", [128, 128])
    amask_d = din("amask", [128, 4, 128])
    rope_c_d = din("rope_c", [3, 128, NB, 32])
    rope_s_d = din("rope_s", [3, 128, NB, 2, 32])
    rope_cs_d = din("rope_cs", [1, 96])
    epair_d = din("epair", [128, 4, 4])
    selb_d = din("selb", [NS, NS, 128])
    selp_d = din("selp", [4, 4, 64])
    ck_d = [din(f"ck{g + 1}", [NS, min(w_, 16384), 1024]) for g, (w_, _) in enumerate(GROUPS)]

    w_pa_d = din("w_proj_a", [512, D])
    w_pb_d = din("w_proj_b", [D, D])
    w_out_d = din("w_out", [D, D])
    norm_ffn_d = din("norm_ffn", [1, D])
    w_gate_d = din("w_gate", [D, DFF])
    w_up_d = din("w_up", [D, DFF])
    w_down_d = din("w_down", [DFF, D])
    norm_final_d = din("norm_final", [1, D])
    y_d = dout("y", [T, D])
    ys_d = dout("ys", [NS, D])
    x1_d = nc.dram_tensor("x1_scr", [T + 128, D], F32, kind="Internal").ap()
    sshift_d = din("sshift", [NS, SH])
    swkv_d = din("swkv", [NS, 16, 64, 64])
    mu_c_d = din("mu_c", [128, 27])
    cols8_d = din("cols8", [128, 5, 8])
    w2a2_d = din("w2a2", [128, 1024])
    g2_d = din("g2", [160, 1024])
    gn_w_d = din("gn_w", [1, 1024])
    gn_b_d = din("gn_b", [1, 1024])
    rmask_d = din("rmask", [128, 4, 128])
    bones_d = din("bones", [128, 128])
    bo2_d = din("bo2", [128, 2])
    i64x2_d = din("i64x2", [128, 64])
    shift_p_d = dout("shift_p", [1, SH])
    wkv_p_d = dout("wkv_p", [16, 64, 64])
    shift_s_d = dout("shift_s", [NS, SH])
    wkv_s_d = dout("wkv_s", [NS, 16, 64, 64])
    kvp_d = [dout("kv1_p", [128, 1024]), dout("kv2_p", [512, 1024]), dout("kv3_p", [2048, 1024])]
    kvs_d = [dout(f"kv{g + 1}_s", [NS, 1024]) for g in range(3)]
    dbg = {}

    def dbg_out(name, shape):
        dbg[name] = dout("dbg_" + name, shape)
        return dbg[name]

    w_in_v = w_in_d.rearrange("(kc p) c -> p kc c", p=128)

    with ExitStack() as st:
        S = Sched(nc, st)

        _uid = [0]

        def sb(stack, name, shape, dt):
            _uid[0] += 1
            return stack.enter_context(nc.sbuf_tensor(f"t{_uid[0]}_{name}", list(shape), dt))

        pb = [st.enter_context(nc.psum_tensor(f"pb{i}", [128, 512], F32)) for i in range(8)]
        pbB = [Buf(f"pb{i}") for i in range(8)]

        ident = sb(st, "ident", [128, 128], F32)
        identb = sb(st, "identb", [128, 128], BF16)
        hT = sb(st, "hT", [128, 8, HTW], BF16)
        B_ident, B_identb, B_hT = Buf("ident"), Buf("identb"), Buf("hT")
        oaT_d = dout("scr_oaT", [128, 4, T + 128], BF16)
        S.dma("sync", ident[:], ident_d, writes=[B_ident])
        S.do("vector", lambda e: e.tensor_copy(out=identb[:], in_=ident[:]), reads=[B_ident], writes=[B_identb])
        S.do("vector", lambda e: e.memset(hT[:], 0.0), writes=[B_hT])

        def rmsnorm_to_T(ph, src_blocks, gain_d, dstT, B_dstT, tag, producer=None, out_rows=None):
            g_bc = sb(ph, tag + "g_bc", [128, D], F32)
            B_g = Buf()
            S.dma("sync", g_bc[:], gain_d.partition_broadcast(128), writes=[B_g])
            NX = 4 if producer is None else 2
            xst = [sb(ph, f"{tag}xst{i}", [128, D], F32) for i in range(NX)]
            B_xst = [Buf() for _ in range(NX)]
            junk = sb(ph, tag + "junk", [128, D], F32)
            B_junk = Buf()
            ss = sb(ph, tag + "ss", [128, 4 * len(src_blocks)], F32)
            B_ss = [Buf() for _ in src_blocks]
            hb = [sb(ph, f"{tag}hb{i}", [128, D], BF16 if out_rows is None else F32) for i in range(2)]
            B_hb = [Buf(), Buf()]
            for i, (src, rows, col0) in enumerate(src_blocks):
                s = i % 2
                sx = i % NX
                if rows < 128:
                    S.do("vector", lambda e, sx=sx: e.memset(xst[sx][:], 0.0), writes=[B_xst[sx]])
                S.dma("sync", xst[sx][0:rows, :], src, writes=[B_xst[sx]])
                if producer is not None:
                    producer(i, xst[sx], B_xst[sx])
                S.do("scalar", lambda e, sx=sx, i=i: e.activation(out=junk[:], in_=xst[sx][:], func=AF.Square,
                                                                 accum_out=ss[:, 4 * i:4 * i + 1]),
                     reads=[B_xst[sx]], writes=[B_junk, B_ss[i]])
                S.do("vector", lambda e, i=i: e.tensor_scalar(out=ss[:, 4 * i + 1:4 * i + 2], in0=ss[:, 4 * i:4 * i + 1],
                                                              scalar1=1.0 / D, scalar2=1e-6, op0=ALU.mult, op1=ALU.add),
                     reads=[B_ss[i]], writes=[B_ss[i]])
                S.do("scalar", lambda e, i=i: e.activation(out=ss[:, 4 * i + 2:4 * i + 3], in_=ss[:, 4 * i + 1:4 * i + 2], func=AF.Sqrt),
                     reads=[B_ss[i]], writes=[B_ss[i]])
                S.do("vector", lambda e, i=i: e.reciprocal(out=ss[:, 4 * i + 3:4 * i + 4], in_=ss[:, 4 * i + 2:4 * i + 3]),
                     reads=[B_ss[i]], writes=[B_ss[i]])
                S.do("vector", lambda e, s=s, sx=sx, i=i: e.scalar_tensor_tensor(out=hb[s][:], in0=xst[sx][:], scalar=ss[:, 4 * i + 3:4 * i + 4],
                                                                                in1=g_bc[:], op0=ALU.mult, op1=ALU.mult),
                     reads=[B_xst[sx], B_ss[i], B_g], writes=[B_hb[s]])
                if out_rows is not None:
                    dst, nrow = out_rows[i]
                    S.dma("sync", dst, hb[s][0:nrow, :], reads=[B_hb[s]])
                    continue
                bk = 6 + (i % 2)
                pbb = pb[bk][:].bitcast(BF16)
                S.mm([("T", pbb[:, kc * 128:(kc + 1) * 128], hb[s][:, kc * 128:(kc + 1) * 128], identb[:]) for kc in range(8)],
                     reads=[B_hb[s], B_identb], writes=[pbB[bk]])
                S.do("scalar", lambda e, pbb=pbb, col0=col0: e.copy(out=dstT[:, :, col0:col0 + 128],
                                                                   in_=pbb.rearrange("p (k t) -> p k t", k=8)),
                     reads=[pbB[bk]], writes=[B_dstT])

        phB = st.enter_context(ExitStack())
        W3 = [sb(phB, f"W3_{j}", [128, 8, 512], BF16) for j in range(3)]
        B_W3 = [Buf() for _ in range(3)]
        for j in range(3):
            S.dma("gpsimd", W3[j][:], w_in_v[:, :, j * 512:(j + 1) * 512], writes=[B_W3[j]])
        with ExitStack() as ph:
            blocks = [(x_d[i * 128:(i + 1) * 128, :], 128, C0 + i * 128) for i in range(NB)]
            if "nosample" not in debug:
                blocks.append((xs_d, NS, SC0))
            rmsnorm_to_T(ph, blocks, norm_mix_d, hT, B_hT, "A")
            S.barrier()

        if "hT" in debug:
            o = dbg_out("hT", [128, 8, HTW])
            with ExitStack() as ph:
                tmp = sb(ph, "dbgtmp", [128, 8, HTW], F32)
                Bt = Buf()
                S.do("vector", lambda e: e.tensor_copy(out=tmp[:], in_=hT[:]), reads=[B_hT], writes=[Bt])
                S.dma("sync", o, tmp[:], reads=[Bt])
                S.barrier()

        with phB as ph:
          if "stopA" not in debug:
              QT = sb(ph, "QTo", [128, 4, T + 128], BF16)
              B_oaT = Buf("oaT")
              S.do("vector", lambda e: e.memset(QT[:, :, T:T + 128], 0.0), writes=[B_oaT])
              amask = sb(ph, "amask", [128, 4, 128], BF16)
              epair = sb(ph, "epair", [128, 4, 4], BF16)
              B_const = Buf()
              S.dma("gpsimd", amask[:], amask_d, writes=[B_const])
              S.dma("gpsimd", epair[:], epair_d, writes=[B_const])
              ropes = sb(ph, "ropes", [128, 96], F32)
              S.dma("sync", ropes[:], rope_cs_d.partition_broadcast(128), writes=[B_const])
              acc_num = sb(ph, "acc_num", [128, 4, T], F32)
              acc_den = sb(ph, "acc_den", [4, 2, T], F32)
              B_acc = [Buf() for _ in range(NB)]
              qkv_s = sb(ph, "qkv_s", [NS, 3, 512], F32)
              B_qkvs = Buf()
              selb = sb(ph, "selb", [NS, NS, 128], F32)
              S.dma("sync", selb[:], selb_d, writes=[B_const])
              KVc = [sb(ph, f"KVc{i}", [128, 2, 8, 65], F32) for i in range(2)]
              B_KVc = [Buf(), Buf()]
              for i in range(2):
                  S.do("vector", lambda e, i=i: e.memset(KVc[i][:], 1.0), writes=[B_KVc[i]])
              prod = sb(ph, "prod", [128, 512], F32)
              B_prod = Buf()
              sc = sb(ph, "sc", [128, 16], F32)
              B_sc = Buf()
              Pz = [sb(ph, f"Pz{b}", [128, 8, NS], F32) for b in range(NS)]
              B_Pz = [Buf() for _ in range(NS)]
              for b in range(NS):
                  S.do("vector", lambda e, b=b: e.memset(Pz[b][:], 0.0), writes=[B_Pz[b]])
              acc_s = sb(ph, "acc_s", [NS, 8, 65], F32)
              cn = sb(ph, "cn", [NS, 8, 65], F32)
              sn = sb(ph, "sn", [NS, 16], F32)
              B_accs, B_cn, B_sn = Buf(), Buf(), Buf()
              KT = sb(ph, "KT", [128, 4, T], BF16)
              Vt = sb(ph, "Vt", [128, NB, 512], BF16)
              B_Q = [Buf() for _ in range(NB)]
              B_K = [Buf() for _ in range(NB)]
              B_V = [Buf() for _ in range(NB)]
              rc = sb(ph, "rc", [128, NB, 32], F32)
              rs = sb(ph, "rs", [128, NB, 2, 32], F32)
              B_rope = Buf()
              t1_ = sb(ph, "t1", [128, 512], F32)
              t2_ = sb(ph, "t2", [128, 512], F32)
              t1, t2 = [t1_, t1_], [t2_, t2_]
              B_t1_, B_t2_ = Buf(), Buf()
              B_t1, B_t2 = [B_t1_, B_t1_], [B_t2_, B_t2_]
              qb = [sb(ph, f"qb{i}", [128, 512], BF16) for i in range(2)]
              kb = [sb(ph, f"kb{i}", [128, 512], BF16) for i in range(2)]
              B_qb = [Buf(), Buf()]
              B_kb = [Buf(), Buf()]
              kv32 = [sb(ph, f"kv32_{i}", [128, 2, 512], F32) for i in range(2)]
              B_kv32 = [Buf(), Buf()]
              PT = [sb(ph, f"PT{i}", [128, 512], BF16) for i in range(2)]
              PM = [sb(ph, f"PM{i}", [128, 512], BF16) for i in range(2)]
              B_PT = [Buf(), Buf()]
              B_PM = [Buf(), Buf()]

              def v4(ap):
                  return ap.rearrange("p (h t d) -> p h t d", h=8, t=2, d=32)

              def rope_ops(np_, src_ps, B_src, cos_ap, sin_ap, B_tab, out_ap, B_out, s):
                  S.do("vector", lambda e: e.tensor_tensor(out=v4(t1[s][0:np_, :]), in0=v4(src_ps), in1=cos_ap, op=ALU.mult),
                       reads=[B_src, B_tab], writes=[B_t1[s]])
                  S.do("vector", lambda e: e.tensor_tensor(out=v4(t2[s][0:np_, :]), in0=v4(src_ps)[:, :, ::-1, :], in1=sin_ap, op=ALU.mult),
                       reads=[B_src, B_tab], writes=[B_t2[s]])
                  S.do("vector", lambda e: e.tensor_tensor(out=out_ap, in0=t1[s][0:np_, :], in1=t2[s][0:np_, :], op=ALU.add),
                       reads=[B_t1[s], B_t2[s]], writes=[B_out])

              for g, (window, d) in enumerate(GROUPS):
                  nbs = NB // d
                  if "g1" in debug and g > 0:
                      continue
                  c0 = g * 1536
                  for j in range(3):
                      if g > 0:
                          S.dma("gpsimd", W3[j][:], w_in_v[:, :, c0 + j * 512:c0 + (j + 1) * 512], writes=[B_W3[j]])
                  S.dma("sync", rc[:], rope_c_d[g], writes=[B_rope])
                  S.dma("sync", rs[:], rope_s_d[g], writes=[B_rope])
                  keep0 = T - min(window, T)
                  for sbk in range(NB + 1):
                      s = sbk % 2
                      if sbk == NB and "nosampleB" in debug:
                          continue
                      if sbk < NB:
                          r, n = divmod(sbk, nbs)
                          tok0 = r + d * 128 * n
                          np_ = 128
                          lcol = lambda kc: hT[:, kc, C0 + tok0:C0 + tok0 + d * 127 + 1:d]
                          cos_ap = rc[:, sbk, :].unsqueeze(1).unsqueeze(1).broadcast_to([128, 8, 2, 32])
                          sin_ap = rs[:, sbk, :, :].unsqueeze(1).broadcast_to([128, 8, 2, 32])
                          B_tab = B_rope
                      else:
                          np_ = NS
                          lcol = lambda kc: hT[:, kc, SC0:SC0 + NS]
                          cos_ap = ropes[0:NS, 0:32].unsqueeze(1).unsqueeze(1).broadcast_to([NS, 8, 2, 32])
                          sin_ap = ropes[0:NS, 32:96].rearrange("p (t d) -> p t d", t=2).unsqueeze(1).broadcast_to([NS, 8, 2, 32])
                          B_tab = B_const
                      banks = [3 * s + j for j in range(3)]
                      for j in range(3):
                          bk = banks[j]
                          S.mm([dict(out=pb[bk][0:np_, :], lhsT=lcol(kc), rhs=W3[j][:, kc, :], start=(kc == 0), stop=(kc == 7)) for kc in range(8)],
                               reads=[B_hT, B_W3[j]], writes=[pbB[bk]])
                      if sbk < NB:
                          rope_ops(128, pb[banks[0]][:, :], pbB[banks[0]], cos_ap, sin_ap, B_tab, qb[s][:], B_qb[s], s)
                          tb = 6
                          pbb = pb[tb][:].bitcast(BF16)
                          S.mm([("T", pbb[:, p * 128:(p + 1) * 128], qb[s][:, p * 128:(p + 1) * 128], identb[:]) for p in range(4)],
                               reads=[B_qb[s], B_identb], writes=[pbB[tb]])
                          S.do("scalar", lambda e, pbb=pbb, sbk=sbk: e.copy(out=QT[:, :, sbk * 128:(sbk + 1) * 128],
                                                                         in_=pbb[:, 0:512].rearrange("p (k t) -> p k t", k=4)),
                               reads=[pbB[tb]], writes=[B_Q[sbk]])
                          rope_ops(128, pb[banks[1]][:, :], pbB[banks[1]], cos_ap, sin_ap, B_tab, kv32[s][:, 0, :], B_kv32[s], s)
                          S.do("scalar", lambda e, s=s: e.copy(out=kb[s][:], in_=kv32[s][:, 0, :]), reads=[B_kv32[s]], writes=[B_kb[s]])
                          tb = 7
                          pbb = pb[tb][:].bitcast(BF16)
                          S.mm([("T", pbb[:, p * 128:(p + 1) * 128], kb[s][:, p * 128:(p + 1) * 128], identb[:]) for p in range(4)],
                               reads=[B_kb[s], B_identb], writes=[pbB[tb]])
                          S.do("scalar", lambda e, pbb=pbb, sbk=sbk: e.copy(out=KT[:, :, sbk * 128:(sbk + 1) * 128],
                                                                         in_=pbb[:, 0:512].rearrange("p (k t) -> p k t", k=4)),
                               reads=[pbB[tb]], writes=[B_K[sbk]])
                          S.do("scalar", lambda e, s=s, bk=banks[2]: e.copy(out=kv32[s][:, 1, :], in_=pb[bk][:, :]), reads=[pbB[banks[2]]], writes=[B_kv32[s]])
                          S.do("scalar", lambda e, s=s, sbk=sbk: e.copy(out=Vt[:, sbk, :], in_=kv32[s][:, 1, :]), reads=[B_kv32[s]], writes=[B_V[sbk]])
                          if tok0 + d * 127 >= keep0 and tok0 >= keep0:
                              dst = kvp_d[g][tok0 - keep0:tok0 - keep0 + d * 127 + 1:d, :].rearrange("t (s c) -> t s c", s=2)
                              S.dma("sync", dst, kv32[s][:], reads=[B_kv32[s]])
                      else:
                          rope_ops(NS, pb[banks[0]][0:NS, :], pbB[banks[0]], cos_ap, sin_ap, B_tab, qkv_s[:, 0, :], B_qkvs, s)
                          rope_ops(NS, pb[banks[1]][0:NS, :], pbB[banks[1]], cos_ap, sin_ap, B_tab, qkv_s[:, 1, :], B_qkvs, s)
                          S.do("scalar", lambda e, bk=banks[2], g=g: e.copy(out=qkv_s[:, 2, :], in_=pb[bk][0:NS, :]), reads=[pbB[banks[2]]], writes=[B_qkvs])
                          S.dma("sync", kvs_d[g].rearrange("t (s c) -> t s c", s=2), qkv_s[:, 1:3, :], reads=[B_qkvs])
                  for sbk in range(NB):
                      if "noattn" in debug:
                          continue
                      r, n = divmod(sbk, nbs)
                      tok0 = r + d * 128 * n
                      kbs = [sbk - 1, sbk] if n > 0 else [sbk]
                      nk = len(kbs)
                      ncol = nk * 256
                      moff = 0 if nk == 2 else 256
                      bO = 4 + (sbk % 2)
                      bD = 6 + (sbk % 2)
                      for p in range(4):
                          u = (sbk * 4 + p) % 2
                          bS = [2 * u, 2 * u + 1]
                          specs = []
                          for ki, kbk in enumerate(kbs):
                              for hp in range(2):
                                  specs.append(dict(out=pb[bS[hp]][:, ki * 128:(ki + 1) * 128],
                                                    lhsT=KT[hp * 64:(hp + 1) * 64, p, kbk * 128:(kbk + 1) * 128],
                                                    rhs=QT[hp * 64:(hp + 1) * 64, p, sbk * 128:(sbk + 1) * 128], start=True, stop=True))
                          S.mm(specs, reads=[B_K[k_] for k_ in kbs] + [B_Q[sbk]], writes=[pbB[bS[0]], pbB[bS[1]]])
                          for hp in range(2):
                              S.do("scalar", lambda e, u=u, b_=bS[hp], hp=hp, nk=nk: e.activation(out=PT[u][:, hp * 256:hp * 256 + nk * 128], in_=pb[b_][:, 0:nk * 128],
                                                                                                  func=AF.Exp, scale=0.125),
                                   reads=[pbB[bS[hp]]], writes=[B_PT[u]])
                          pt3 = PT[u][:].rearrange("p (a c) -> p a c", a=2)[:, :, 0:nk * 128]
                          pm3 = PM[u][:].rearrange("p (a c) -> p a c", a=2)[:, :, 0:nk * 128]
                          mk3 = amask[:].rearrange("p (a b) c -> p a (b c)", a=2)[:, :, (2 - nk) * 128:256]
                          S.do("vector", lambda e, pt3=pt3, pm3=pm3, mk3=mk3: e.tensor_tensor(out=pm3, in0=pt3, in1=mk3, op=ALU.mult),
                               reads=[B_PT[u], B_const], writes=[B_PM[u]])
                          specs = []
                          for hp in range(2):
                              for ki, kbk in enumerate(kbs):
                                  col0 = hp * 256 + ki * 128
                                  specs.append(dict(out=pb[bO][hp * 64:(hp + 1) * 64, p * 128:(p + 1) * 128],
                                                    lhsT=Vt[:, kbk, p * 128 + hp * 64:p * 128 + (hp + 1) * 64],
                                                    rhs=PM[u][:, col0:col0 + 128], start=(ki == 0), stop=(ki == nk - 1)))
                          for ki in range(nk):
                              rhs = PM[u][:].rearrange("p (a c) -> p a c", a=2)[:, :, ki * 128:(ki + 1) * 128]
                              specs.append(dict(out=pb[bD][0:4, 0:256], lhsT=epair[:, p, :], rhs=rhs,
                                                start=(p == 0 and ki == 0), stop=(p == 3 and ki == nk - 1)))
                          S.mm(specs, reads=[B_PM[u]] + [B_V[k_] for k_ in kbs] + [B_const], writes=[pbB[bO], pbB[bD]])
                      num_dst = acc_num[:, :, tok0:tok0 + d * 127 + 1:d]
                      den_dst = acc_den[:, :, tok0:tok0 + d * 127 + 1:d]
                      num_src = pb[bO][:, :].rearrange("p (a q) -> p a q", a=4)
                      den_src = pb[bD][0:4, 0:256].rearrange("p (a q) -> p a q", a=2)
                      Bacc = B_acc[0]
                      if g == 0:
                          S.do("scalar", lambda e, num_dst=num_dst, num_src=num_src: e.copy(out=num_dst, in_=num_src), reads=[pbB[bO]], writes=[Bacc])
                          S.do("scalar", lambda e, den_dst=den_dst, den_src=den_src: e.copy(out=den_dst, in_=den_src), reads=[pbB[bD]], writes=[Bacc])
                      else:
                          S.do("vector", lambda e, num_dst=num_dst, num_src=num_src: e.tensor_tensor(out=num_dst, in0=num_src, in1=num_dst, op=ALU.add),
                               reads=[pbB[bO]], writes=[Bacc])
                          S.do("vector", lambda e, den_dst=den_dst, den_src=den_src: e.tensor_tensor(out=den_dst, in0=den_src, in1=den_dst, op=ALU.add),
                               reads=[pbB[bD]], writes=[Bacc])
                  if "nosampleB" not in debug:
                      buf_len = min(window, 16384)
                      for b in range(NS):
                          sl = b % 2
                          bq = b % 2
                          S.dma("sync", KVc[sl][:, :, :, 0:64], ck_d[g][b, 0:buf_len:d, :].rearrange("m (s h e) -> m s h e", s=2, h=8), writes=[B_KVc[sl]])
                          S.mm([dict(out=pb[bq][:, :], lhsT=selb[0:NS, b, :], rhs=qkv_s[0:NS, 0, :])], reads=[B_qkvs, B_const], writes=[pbB[bq]])
                          S.do("vector", lambda e, sl=sl, bq=bq: e.tensor_tensor(out=prod[:].rearrange("p (h e) -> p h e", h=8), in0=KVc[sl][:, 0, :, 0:64],
                                                                              in1=pb[bq][:, :].rearrange("p (h e) -> p h e", h=8), op=ALU.mult),
                               reads=[B_KVc[sl], pbB[bq]], writes=[B_prod])
                          S.do("vector", lambda e: e.tensor_reduce(out=sc[:, 0:8], in_=prod[:].rearrange("p (h e) -> p h e", h=8), axis=AX.X, op=ALU.add),
                               reads=[B_prod], writes=[B_sc])
                          S.do("scalar", lambda e, b=b: e.activation(out=Pz[b][:, :, b], in_=sc[:, 0:8], func=AF.Exp, scale=0.125),
                               reads=[B_sc], writes=[B_Pz[b]])
                          specs = []
                          for h in range(8):
                              bn = 2 + h // 4
                              specs.append(dict(out=pb[bn][0:NS, (h % 4) * 65:(h % 4 + 1) * 65], lhsT=Pz[b][:, h, :], rhs=KVc[sl][:, 1, h, :],
                                                start=(b == 0 and h % 4 == 0), stop=(b == NS - 1 and h % 4 == 3)))
                          S.mm(specs, reads=[B_Pz[b], B_KVc[sl]], writes=[pbB[2], pbB[3]])
                      S.do("vector", lambda e: e.tensor_tensor(out=prod[0:NS, :], in0=qkv_s[:, 0, :], in1=qkv_s[:, 1, :], op=ALU.mult),
                           reads=[B_qkvs], writes=[B_prod])
                      S.do("vector", lambda e: e.tensor_reduce(out=sn[:, 0:8], in_=prod[0:NS, :].rearrange("p (h e) -> p h e", h=8), axis=AX.X, op=ALU.add),
                           reads=[B_prod], writes=[B_sn])
                      S.do("scalar", lambda e: e.activation(out=sn[:, 8:16], in_=sn[:, 0:8], func=AF.Exp, scale=0.125), reads=[B_sn], writes=[B_sn])
                      S.do("vector", lambda e: e.tensor_tensor(out=cn[:, :, 0:64], in0=qkv_s[:, 2, :].rearrange("p (h e) -> p h e", h=8),
                                                               in1=sn[:, 8:16].unsqueeze(2).broadcast_to([NS, 8, 64]), op=ALU.mult),
                           reads=[B_qkvs, B_sn], writes=[B_cn])
                      S.do("vector", lambda e: e.tensor_copy(out=cn[:, :, 64:65], in_=sn[:, 8:16].unsqueeze(2)), reads=[B_sn], writes=[B_cn])
                      for hb in range(2):
                          src = pb[2 + hb][0:NS, 0:260].rearrange("p (h e) -> p h e", h=4)
                          dst = acc_s[:, hb * 4:(hb + 1) * 4, :]
                          cns = cn[:, hb * 4:(hb + 1) * 4, :]
                          S.do("vector", lambda e, src=src, cns=cns: e.tensor_tensor(out=cns, in0=src, in1=cns, op=ALU.add),
                               reads=[pbB[2 + hb]], writes=[B_cn])
                          if g == 0:
                              S.do("vector", lambda e, dst=dst, cns=cns: e.tensor_copy(out=dst, in_=cns), reads=[B_cn], writes=[B_accs])
                          else:
                              S.do("vector", lambda e, dst=dst, cns=cns: e.tensor_tensor(out=dst, in0=dst, in1=cns, op=ALU.add), reads=[B_cn], writes=[B_accs])
              S.barrier()
              selp = sb(ph, "selp", [4, 4, 64], F32)
              S.dma("sync", selp[:], selp_d, writes=[B_const])
              if "noattn" not in debug:
                  S.do("vector", lambda e: e.reciprocal(out=acc_den[:], in_=acc_den[:]), reads=[B_acc[0]], writes=[B_acc[0]])
                  for p in range(4):
                      for ch in range(4):
                          bk = (p * 4 + ch) % 2
                          S.mm([dict(out=pb[bk][hp * 64:(hp + 1) * 64, :], lhsT=selp[0:4, p, :], rhs=acc_den[0:4, hp, ch * 512:(ch + 1) * 512]) for hp in range(2)],
                               reads=[B_acc[0], B_const], writes=[pbB[bk]])
                          S.do("vector", lambda e, p=p, ch=ch, bk=bk: e.tensor_tensor(out=QT[:, p, ch * 512:(ch + 1) * 512], in0=acc_num[:, p, ch * 512:(ch + 1) * 512],
                                                                                  in1=pb[bk][:, :], op=ALU.mult),
                               reads=[B_acc[0], pbB[bk]], writes=[B_oaT])
              if "nosampleB" not in debug:
                  S.do("vector", lambda e: e.reciprocal(out=sn[:, 0:8], in_=acc_s[:, :, 64]), reads=[B_accs], writes=[B_sn])
                  S.do("vector", lambda e: e.tensor_tensor(out=prod[0:NS, :].rearrange("p (h e) -> p h e", h=8), in0=acc_s[:, :, 0:64],
                                                           in1=sn[:, 0:8].unsqueeze(2).broadcast_to([NS, 8, 64]), op=ALU.mult),
                       reads=[B_accs, B_sn], writes=[B_prod])
                  S.mm([("T", pb[4][:, p * 4:p * 4 + NS], prod[0:NS, p * 128:(p + 1) * 128], ident[0:NS, 0:NS]) for p in range(4)],
                       reads=[B_prod, B_ident], writes=[pbB[4]])
                  S.do("vector", lambda e: e.tensor_copy(out=QT[:, :, T:T + NS], in_=pb[4][:, 0:16].rearrange("p (a b) -> p a b", a=4)),
                       reads=[pbB[4]], writes=[B_oaT])
                  if "acc" in debug:
                      o3 = dbg_out("oas", [NS, 512])
                      S.dma("sync", o3, prod[0:NS, :], reads=[B_prod])
              S.dma("sync", oaT_d, QT[:], reads=[B_oaT])
              if "acc" in debug:
                  o1 = dbg_out("num", [128, 4, T])
                  o2 = dbg_out("den", [4, 2, T])
                  S.dma("sync", o1, acc_num[:], reads=[B_acc[0]])
                  S.dma("sync", o2, acc_den[:], reads=[B_acc[0]])
              S.barrier()

        o_bT = sb(st, "o_bT", [128, 8, T + 128], BF16)
        B_obT = Buf("obT")
        S.do("vector", lambda e: e.memset(o_bT[:, :, T:T + 128], 0.0), writes=[B_obT])
        S.do("vector", lambda e: e.tensor_copy(out=hT[:, :, SC0 + NS:SC0 + NS + 1], in_=hT[:, :, C0 + T - 1:C0 + T]), reads=[B_hT], writes=[B_hT])
        C0D = 0.6065306597126334
        def rwkv_pass(hh, ph):
            a0g = 4 * hh
            Bc = Buf("dconst")
            Ws, B_Ws = Ws_g, B_Ws_g
            if hh == 0:
                load_Ws(0)

            def gct(lt):
                return (lt // 4) * 8 + a0g + lt % 4 if lt < 12 else 24 + (lt - 12)
            mu_c = sb(ph, "mu_c", [128, 27], F32)
            omm_c = sb(ph, "omm_c", [128, 27], F32)
            cols8 = sb(ph, "cols8", [128, 5, 8], F32)
            omka = sb(ph, "omka", [128, 8], F32)
            S.dma("sync", mu_c[:], mu_c_d, writes=[Bc])
            S.dma("sync", cols8[:], cols8_d, writes=[Bc])
            S.do("vector", lambda e: e.tensor_scalar(out=omm_c[:], in0=mu_c[:], scalar1=-1.0, scalar2=1.0, op0=ALU.mult, op1=ALU.add), reads=[Bc], writes=[Bc])
            S.do("vector", lambda e: e.tensor_scalar(out=omka[:], in0=cols8[:, 3, :], scalar1=-1.0, scalar2=1.0, op0=ALU.mult, op1=ALU.add), reads=[Bc], writes=[Bc])
            w2a2 = sb(ph, "w2a2", [128, 512], F32)
            g2a = sb(ph, "g2a", [128, 512], BF16)
            g2b = sb(ph, "g2b", [128, 512], BF16)
            xgb = sb(ph, "xgb", [128, 2, 128], BF16)
            sqb = sb(ph, "sqb", [128, 512], BF16)
            bonesb = sb(ph, "bonesb", [128, 128], BF16)
            B_xgb, B_sqb = Buf(), Buf()
            S.do("vector", lambda e: e.memset(xgb[:], 0.0), writes=[B_xgb])
            S.dma("gpsimd", bonesb[:], bones_d, writes=[Bc])
            gnw = sb(ph, "gnw", [128, 512], F32)
            gnb = sb(ph, "gnb", [128, 512], F32)
            hs = slice(hh * 512, (hh + 1) * 512)
            S.dma("sync", w2a2[:], w2a2_d[:, hs], writes=[Bc])
            S.dma("gpsimd", g2a[:], g2_d[0:128, hs], writes=[Bc])
            S.do("vector", lambda e: e.memset(g2b[:], 0.0), writes=[Bc])
            S.dma("gpsimd", g2b[0:32, :], g2_d[128:160, hs], writes=[Bc])
            S.dma("sync", gnw[:], gn_w_d[:, hs].partition_broadcast(128), writes=[Bc])
            S.dma("sync", gnb[:], gn_b_d[:, hs].partition_broadcast(128), writes=[Bc])
            rmask = sb(ph, "rmask", [128, 4, 128], BF16)
            bones = sb(ph, "bones", [128, 128], F32)
            bo2 = sb(ph, "bo2", [128, 2], BF16)
            i64x2 = sb(ph, "i64x2", [128, 64], F32)
            ones128 = sb(ph, "ones128", [128, 128], F32)
            S.dma("gpsimd", rmask[:], rmask_d, writes=[Bc])
            S.dma("sync", bones[:], bones_d, writes=[Bc])
            S.dma("gpsimd", bo2[:], bo2_d, writes=[Bc])
            S.dma("sync", i64x2[:], i64x2_d, writes=[Bc])
            S.do("vector", lambda e: e.memset(ones128[:], 1.0), writes=[Bc])

            def f4(nm):
                return sb(ph, nm, [128, 4, 128], F32)
            sgT, aT, ginv, S1, S2, S3 = [f4(n) for n in ("sgT", "aT", "ginv", "S1", "S2", "S3")]
            rT3, kT3, vT3 = [sb(ph, n, [128, 4, 384], F32) for n in ("rT3", "kT3", "vT3")]
            B_r, B_k, B_v, B_sg, B_a, B_gi, B_S1, B_S2, B_S3 = [Buf() for _ in range(9)]
            cs, B_cs = S3, B_S3
            lw3 = sb(ph, "lw3", [128, 384], F32)
            xg3 = sb(ph, "xg3", [128, 2, 384], F32)
            B_lw, B_xg = Buf(), Buf()
            S.do("vector", lambda e: e.memset(xg3[:], 0.0), writes=[B_xg])
            tmix = sb(ph, "tmix", [128, 384], F32)
            B_tmix = Buf()
            gamx2 = [sb(ph, "gamx", [128, 4, 129], F32)] * 2
            B_gam2 = [Buf()] * 2
            S.do("vector", lambda e: e.memset(gamx2[0][:], 1.0), writes=[B_gam2[0]])
            AR2 = [sb(ph, "AR", [128, 4, 2, 128], BF16)] * 2
            ARbd2 = [sb(ph, "ARbd", [128, 4, 2, 2, 128], BF16)] * 2
            B_AR2, B_ARbd2 = [Buf()] * 2, [Buf()] * 2
            S.do("vector", lambda e: e.memset(ARbd2[0][:], 0.0), writes=[B_ARbd2[0]])
            BT2 = [sb(ph, "BT", [128, 4, 128], BF16)] * 2
            KhT2 = [sb(ph, "KhT", [128, 4, 128], BF16)] * 2
            B_BT2, B_KhT2 = [Buf()] * 2, [Buf()] * 2
            rks = sb(ph, "rks", [128, 512], F32)
            B_rks = Buf()
            VT = sb(ph, "VT", [128, 4, 128], BF16)
            rkT = sb(ph, "rkT", [128, 4, 128], BF16)
            B_VT, B_rk = Buf(), Buf()
            Btok2 = [sb(ph, "Btok", [128, 512], BF16)] * 2
            Ktok2 = [sb(ph, "Ktok", [128, 512], BF16)] * 2
            Vtok2 = [sb(ph, "Vtok", [128, 512], BF16)] * 2
            B_Btok2, B_Ktok2, B_Vtok2 = [Buf()] * 2, [Buf()] * 2, [Buf()] * 2
            bon2 = [sb(ph, "bon", [128, 8], F32)] * 2
            B_bon2 = [Buf()] * 2
            gate_sb2 = [sb(ph, "gate_sb", [128, 512], F32)] * 2
            B_gate2 = [Buf()] * 2
            XS1 = [sb(ph, f"XS1_{a}", [128, 512], BF16) for a in range(4)]
            XS2 = [sb(ph, f"XS2_{a}", [128, 512], BF16) for a in range(4)]
            B_XS1 = [Buf() for _ in range(4)]
            B_XS2 = [Buf() for _ in range(4)]
            L0 = [sb(ph, f"L0_{a}", [128, 2, 128], BF16) for a in range(4)]
            B_L0 = [Buf() for _ in range(4)]
            LN = [[sb(ph, f"LN_{a}_{q}", [128, 512], BF16) for q in range(2)] for a in range(4)]
            B_LN = [[Buf(), Buf()] for _ in range(4)]
            Ut = [[sb(ph, f"U_{a}_{q}", [128, 128], BF16) for q in range(2)] for a in range(4)]
            B_U = [[Buf(), Buf()] for _ in range(4)]
            Mst = sb(ph, "Mst", [128, 4, 64], F32)
            Mg = sb(ph, "Mg", [128, 4, 64], F32)
            M0bd = sb(ph, "M0bd", [128, 4, 2, 64], BF16)
            B_M, B_Mg, B_M0bd = Buf(), Buf(), Buf()
            S.do("vector", lambda e: e.memset(Mst[:], 0.0), writes=[B_M])
            S.do("vector", lambda e: e.memset(M0bd[:], 0.0), writes=[B_M0bd])
            ob = sb(ph, "ob", [128, 512], BF16)
            stt = sb(ph, "stt", [128, 48], F32)
            B_ob, B_st = Buf(), Buf()
            ssT = sb(ph, "ssT", [128, 15, NS], F32)
            B_ssT = Buf()
            praw = sb(ph, "praw", [128, 15, 8], F32)
            B_praw = Buf()
            ysq = sb(ph, "ysq", [128, 512], F32)
            tt = sb(ph, "tt", [128, 512], F32)
            bv = sb(ph, "bv", [128, 512], F32)
            B_ysq, B_tt, B_bv = Buf(), Buf(), Buf()
            for q in range(4):
                w_ = 512 if q < 3 else 288
                src_c0 = (q * 1024 + a0g * 128) if q < 3 else 3072
                S.dma("sync", tt[0:NS, 0:w_], sshift_d[:, src_c0:src_c0 + w_], writes=[B_tt])
                for j in range((w_ + 127) // 128):
                    lt = q * 4 + j
                    cw = 32 if lt == 14 else 128
                    bk = lt % 2
                    S.mm([("T", pb[bk][0:cw, 0:NS], tt[0:NS, j * 128:j * 128 + cw], ident[0:NS, 0:NS])], reads=[B_tt, B_ident], writes=[pbB[bk]])
                    S.do("vector", lambda e, lt=lt, cw=cw, bk=bk: e.tensor_copy(out=ssT[0:cw, lt, :], in_=pb[bk][0:cw, 0:NS]), reads=[pbB[bk]], writes=[B_ssT])

            def bc3(ap2, n=128):
                return ap2.unsqueeze(2).broadcast_to([128, 4, n])

            def dest_of(lt):
                if lt < 4:
                    return rT[:, lt, :], B_r
                if lt < 8:
                    return kT[:, lt - 4, :], B_k
                if lt < 12:
                    return vT[:, lt - 8, :], B_v
                if lt == 12:
                    return lw[:, :], B_lw
                if lt == 13:
                    return xg[:, 0, :], B_xg
                return xg[0:32, 1, :], B_xg

            def rwkv_block(c, stage):
                isX = (c == NB)
                np_ = NS if isX else 128
                if (isX and DNOX) or (not isX and c >= DNB):
                    return
                pc = c % 2
                gate_sb, B_gate = gate_sb2[pc], B_gate2[pc]
                gamx, B_gam = gamx2[pc], B_gam2[pc]
                Vtok, B_Vtok = Vtok2[pc], B_Vtok2[pc]
                Btok, B_Btok = Btok2[pc], B_Btok2[pc]
                Ktok, B_Ktok = Ktok2[pc], B_Ktok2[pc]
                bon, B_bon = bon2[pc], B_bon2[pc]
                AR, B_AR = AR2[pc], B_AR2[pc]
                ARbd, B_ARbd = ARbd2[pc], B_ARbd2[pc]
                BT, B_BT = BT2[pc], B_BT2[pc]
                KhT, B_KhT = KhT2[pc], B_KhT2[pc]
                jb = 0 if isX else c % 3
                rT = rT3[:, :, jb * 128:(jb + 1) * 128]
                kT = kT3[:, :, jb * 128:(jb + 1) * 128]
                vT = vT3[:, :, jb * 128:(jb + 1) * 128]
                lw = lw3[:, jb * 128:(jb + 1) * 128]
                xg = xg3[:, :, jb * 128:(jb + 1) * 128]
                if stage == "B":
                    yield from rwkv_stage_b(c, isX, np_, gate_sb, B_gate, gamx, B_gam, Vtok, B_Vtok, Btok, B_Btok, Ktok, B_Ktok, bon, B_bon, AR, B_AR, ARbd, B_ARbd, BT, B_BT, KhT, B_KhT)
                    return
                def dest3(lt):
                    if lt < 4:
                        return rT3[:, lt, :], B_r
                    if lt < 8:
                        return kT3[:, lt - 4, :], B_k
                    if lt < 12:
                        return vT3[:, lt - 8, :], B_v
                    if lt == 12:
                        return lw3[:, :], B_lw
                    if lt == 13:
                        return xg3[:, 0, :], B_xg
                    return xg3[0:32, 1, :], B_xg
                if isX or c % 3 == 0:
                    nb_ = 1 if isX else min(3, NB - c)
                    for lt in range(15):
                        bk = lt % 4
                        cw = 32 if lt == 14 else 128
                        ct = gct(lt)
                        dst, B_dst = dest3(lt)
                        if isX:
                            S.mm([dict(out=pb[bk][0:cw, 0:128], lhsT=Ws[:, kc, lt * 128:lt * 128 + cw], rhs=hT[:, kc, SC0:SC0 + 128], start=(kc == 0), stop=(kc == 7)) for kc in range(8)],
                                 reads=[B_hT, B_Ws], writes=[pbB[bk]])
                            S.do("vector", lambda e, lt=lt, cw=cw, bk=bk: e.tensor_copy(out=praw[0:cw, lt, :], in_=pb[bk][0:cw, 0:8]),
                                 reads=[pbB[bk]], writes=[B_praw])
                            S.do("vector", lambda e, lt=lt, cw=cw, ct=ct: e.tensor_scalar(out=tmix[0:cw, 0:NS], in0=ssT[0:cw, lt, :], scalar1=mu_c[0:cw, ct:ct + 1],
                                                                                  scalar2=None, op0=ALU.mult),
                                 reads=[B_ssT, Bc], writes=[B_tmix])
                            S.do("vector", lambda e, cw=cw, bk=bk, ct=ct, dst=dst: e.scalar_tensor_tensor(out=dst[0:cw, 0:NS], in0=pb[bk][0:cw, 0:NS],
                                                                                                 scalar=omm_c[0:cw, ct:ct + 1], in1=tmix[0:cw, 0:NS],
                                                                                                 op0=ALU.mult, op1=ALU.add),
                                 reads=[pbB[bk], B_tmix, Bc], writes=[B_dst])
                        else:
                            n_ = nb_ * 128
                            S.mm([dict(out=pb[bk][0:cw, 0:n_ + 1], lhsT=Ws[:, kc, lt * 128:lt * 128 + cw], rhs=hT[:, kc, C0 + c * 128 - 1:C0 + c * 128 + n_],
                                       start=(kc == 0), stop=(kc == 7)) for kc in range(8)], reads=[B_hT, B_Ws], writes=[pbB[bk]])
                            S.do("vector", lambda e, cw=cw, bk=bk, ct=ct, n_=n_: e.tensor_scalar(out=tmix[0:cw, 0:n_], in0=pb[bk][0:cw, 0:n_],
                                                                                         scalar1=mu_c[0:cw, ct:ct + 1], scalar2=None, op0=ALU.mult),
                                 reads=[pbB[bk], Bc], writes=[B_tmix])
                            S.do("vector", lambda e, cw=cw, bk=bk, ct=ct, dst=dst, n_=n_: e.scalar_tensor_tensor(out=dst[0:cw, 0:n_], in0=pb[bk][0:cw, 1:n_ + 1],
                                                                                                        scalar=omm_c[0:cw, ct:ct + 1], in1=tmix[0:cw, 0:n_],
                                                                                                        op0=ALU.mult, op1=ALU.add),
                                 reads=[pbB[bk], B_tmix, Bc], writes=[B_dst])
                        if lt % 3 == 2:
                            yield
                    if isX and hh == 0:
                        load_Ws(1)
                    if isX and hh == 1:
                        load_Wp()
                if isX:
                    for q in range(4):
                        w_ = 512 if q < 3 else 288
                        bk = q % 2
                        for lt in range(4 * q, min(4 * q + 4, 15)):
                            cw = 32 if lt == 14 else 128
                            S.mm([("T", pb[bk][0:8, (lt % 4) * 128:(lt % 4) * 128 + cw], praw[0:cw, lt, :], ident[0:cw, 0:cw])], reads=[B_praw, B_ident], writes=[pbB[bk]])
                        if q == 3 and hh == 1:
                            continue
                        dc0 = (q * 1024 + a0g * 128) if q < 3 else 3072
                        s1f = S1[0:8, :, :].rearrange("p a t -> p (a t)")
                        S.do("vector", lambda e, bk=bk, w_=w_, s1f=s1f: e.tensor_copy(out=s1f[:, 0:w_], in_=pb[bk][0:8, 0:w_]), reads=[pbB[bk]], writes=[B_S1])
                        S.dma("sync", shift_s_d[:, dc0:dc0 + w_], s1f[0:NS, 0:w_], reads=[B_S1])
                        S.dma("sync", shift_p_d[:, dc0:dc0 + w_], s1f[NS:NS + 1, 0:w_], reads=[B_S1])
                    yield
                if DLIM < 2:
                    return
                yield
                S.do("scalar", lambda e: e.activation(out=lw[0:64, :], in_=lw[0:64, :], func=AF.Tanh), reads=[B_lw], writes=[B_lw])
                S.do("scalar", lambda e: e.activation(out=xgb[:, 0, :], in_=xg[:, 0, :], func=AF.Sigmoid), reads=[B_xg], writes=[B_xgb])
                S.do("scalar", lambda e: e.activation(out=xgb[0:32, 1, :], in_=xg[0:32, 1, :], func=AF.Sigmoid), reads=[B_xg], writes=[B_xgb])
                S.mm([dict(out=pb[4][:, a * 128:(a + 1) * 128], lhsT=w2a2[0:64, a * 128:(a + 1) * 128], rhs=lw[0:64, :]) for a in range(4)],
                     reads=[B_lw, Bc], writes=[pbB[4]])
                S.mm([dict(out=pb[5][:, a * 128:(a + 1) * 128], lhsT=w2a2[64:128, a * 128:(a + 1) * 128], rhs=lw[64:128, :]) for a in range(4)],
                     reads=[B_lw, Bc], writes=[pbB[5]])
                for a in range(4):
                    S.do("scalar", lambda e, a=a: e.activation(out=sgT[:, a, :], in_=pb[4][:, a * 128:(a + 1) * 128], func=AF.Sigmoid, bias=cols8[:, 0, a0g + a:a0g + a + 1]),
                         reads=[pbB[4], Bc], writes=[B_sg])
                    S.do("scalar", lambda e, a=a: e.activation(out=aT[:, a, :], in_=pb[5][:, a * 128:(a + 1) * 128], func=AF.Sigmoid, bias=cols8[:, 1, a0g + a:a0g + a + 1]),
                         reads=[pbB[5], Bc], writes=[B_a])
                S.mm([dict(out=pb[6][:, :], lhsT=xgb[:, 0, :], rhs=g2a[:, :], start=True, stop=False),
                      dict(out=pb[6][:, :], lhsT=xgb[:, 1, :], rhs=g2b[:, :], start=False, stop=True)], reads=[B_xgb, Bc], writes=[pbB[6]])
                S.do("scalar", lambda e: e.copy(out=gate_sb[:], in_=pb[6][:, :]), reads=[pbB[6]], writes=[B_gate])
                if DLIM < 3:
                    return
                yield
                if isX:
                    src_cs, B_src = sgT, B_sg
                else:
                    for a in range(4):
                        S.do("vector", lambda e, a=a: e.tensor_tensor_scan(out=cs[:, a, :], data0=ones128[:], data1=sgT[:, a, :], initial=0.0, op0=ALU.mult, op1=ALU.add),
                             reads=[B_sg, Bc], writes=[B_cs])
                    src_cs, B_src = cs, B_cs
                S.do("scalar", lambda e, src_cs=src_cs: e.activation(out=gamx[:, :, 1:129], in_=src_cs[:], func=AF.Exp, scale=-C0D), reads=[B_src], writes=[B_gam])
                S.do("scalar", lambda e, src_cs=src_cs: e.activation(out=ginv[:], in_=src_cs[:], func=AF.Exp, scale=C0D), reads=[B_src], writes=[B_gi])
                if DLIM < 4:
                    return
                yield
                ag = slice(a0g, a0g + 4)
                S.do("vector", lambda e: e.tensor_tensor(out=S1[:], in0=kT[:], in1=bc3(cols8[:, 2, ag]), op=ALU.mult), reads=[B_k, Bc], writes=[B_S1])
                S.do("vector", lambda e: e.tensor_tensor(out=sqb[:].rearrange("p (a t) -> p a t", a=4), in0=S1[:], in1=S1[:], op=ALU.mult), reads=[B_S1], writes=[B_sqb])
                S.mm([dict(out=pb[7][:, :], lhsT=bonesb[:], rhs=sqb[:])], reads=[B_sqb, Bc], writes=[pbB[7]])
                S.do("vector", lambda e: e.tensor_scalar(out=S2[:].rearrange("p a t -> p (a t)"), in0=pb[7][:, :], scalar1=1e-24, scalar2=None, op0=ALU.max),
                     reads=[pbB[7]], writes=[B_S2])
                S.do("scalar", lambda e: e.activation(out=S2[:], in_=S2[:], func=AF.Ln), reads=[B_S2], writes=[B_S2])
                S.do("scalar", lambda e: e.activation(out=S2[:], in_=S2[:], func=AF.Exp, scale=-0.5), reads=[B_S2], writes=[B_S2])
                S.do("vector", lambda e: e.tensor_tensor(out=S1[:], in0=S1[:], in1=S2[:], op=ALU.mult), reads=[B_S1, B_S2], writes=[B_S1])
                S.do("vector", lambda e: e.tensor_tensor(out=S2[:], in0=aT[:], in1=bc3(cols8[:, 3, ag]), op=ALU.mult), reads=[B_a, Bc], writes=[B_S2])
                S.do("vector", lambda e: e.tensor_tensor(out=S2[:], in0=S2[:], in1=bc3(omka[:, ag]), op=ALU.add), reads=[B_S2, Bc], writes=[B_S2])
                S.do("vector", lambda e: e.tensor_tensor(out=S2[:], in0=kT[:], in1=S2[:], op=ALU.mult), reads=[B_k, B_S2], writes=[B_S2])
                S.do("vector", lambda e: e.tensor_tensor(out=S3[:], in0=S1[:], in1=aT[:], op=ALU.mult), reads=[B_S1, B_a], writes=[B_S3])
                if DLIM < 5:
                    return
                yield
                S.do("scalar", lambda e: e.copy(out=VT[:], in_=vT[:]), reads=[B_v], writes=[B_VT])
                pbb = pb[2][:].bitcast(BF16)
                S.mm([("T", pbb[:, a * 128:(a + 1) * 128], VT[:, a, :], identb[:]) for a in range(4)], reads=[B_VT, B_identb], writes=[pbB[2]])
                S.do("scalar", lambda e, pbb=pbb: e.copy(out=Vtok[:], in_=pbb[:, 0:512]), reads=[pbB[2]], writes=[B_Vtok])
                S.do("vector", lambda e: e.tensor_tensor(out=rks[:].rearrange("p (a t) -> p a t", a=4), in0=rT[:], in1=S2[:], op=ALU.mult), reads=[B_r, B_S2], writes=[B_rks])
                S.do("vector", lambda e: e.tensor_tensor(out=rkT[:], in0=rks[:].rearrange("p (a t) -> p a t", a=4), in1=bc3(cols8[:, 4, ag]), op=ALU.mult),
                     reads=[B_rks, Bc], writes=[B_rk])
                S.mm([dict(out=pb[3][:, 2 * a:2 * a + 2], lhsT=rkT[:, a, :], rhs=bo2[:, :]) for a in range(4)], reads=[B_rk, Bc], writes=[pbB[3]])
                S.do("scalar", lambda e: e.copy(out=bon[:], in_=pb[3][:, 0:8]), reads=[pbB[3]], writes=[B_bon])
                if not isX:
                    S.do("vector", lambda e: e.scalar_tensor_tensor(out=AR[:, :, 0, :], in0=S1[:], scalar=-1.0, in1=gamx[:, :, 0:128], op0=ALU.mult, op1=ALU.mult),
                         reads=[B_S1, B_gam], writes=[B_AR])
                    S.do("vector", lambda e: e.tensor_tensor(out=AR[:, :, 1, :], in0=rT[:], in1=gamx[:, :, 1:129], op=ALU.mult), reads=[B_r, B_gam], writes=[B_AR])
                    S.do("scalar", lambda e: e.copy(out=ARbd[0:64, :, 0, :, :], in_=AR[0:64, :, :, :]), reads=[B_AR], writes=[B_ARbd])
                    S.do("scalar", lambda e: e.copy(out=ARbd[64:128, :, 1, :, :], in_=AR[64:128, :, :, :]), reads=[B_AR], writes=[B_ARbd])
                    S.do("vector", lambda e: e.tensor_tensor(out=BT[:], in0=S3[:], in1=ginv[:], op=ALU.mult), reads=[B_S3, B_gi], writes=[B_BT])
                    S.do("vector", lambda e: e.tensor_tensor(out=KhT[:], in0=S2[:], in1=ginv[:], op=ALU.mult), reads=[B_S2, B_gi], writes=[B_KhT])
                    for src, B_src2, dstk, B_dk, bk in ((BT, B_BT, Btok, B_Btok, 0), (KhT, B_KhT, Ktok, B_Ktok, 1)):
                        pbb = pb[bk][:].bitcast(BF16)
                        S.mm([("T", pbb[:, a * 128:(a + 1) * 128], src[:, a, :], identb[:]) for a in range(4)], reads=[B_src2, B_identb], writes=[pbB[bk]])
                        S.do("scalar", lambda e, pbb=pbb, dstk=dstk: e.copy(out=dstk[:], in_=pbb[:, 0:512]), reads=[pbB[bk]], writes=[B_dk])

            def rwkv_stage_b(c, isX, np_, gate_sb, B_gate, gamx, B_gam, Vtok, B_Vtok, Btok, B_Btok, Ktok, B_Ktok, bon, B_bon, AR, B_AR, ARbd, B_ARbd, BT, B_BT, KhT, B_KhT):
                rT = rT3[:, :, 0:128]
                vT = vT3[:, :, 0:128]
                if not isX:
                    if DLIM < 6:
                        return
                    S.do("vector", lambda e: e.tensor_tensor(out=Mg[:], in0=Mst[:], in1=gamx[:, :, 128:129].broadcast_to([128, 4, 64]), op=ALU.mult),
                         reads=[B_M, B_gam], writes=[B_Mg])
                    for a in range(4):
                        arbd = ARbd[:, a, :, :, :].rearrange("p h q t -> p (h q t)")
                        x1b = 2 if a % 2 == 0 else 7
                        S.mm([dict(out=pb[x1b][:, :], lhsT=BT[:, a, :], rhs=arbd)], reads=[B_BT, B_ARbd], writes=[pbB[x1b]])
                        S.mm([dict(out=pb[3][:, :], lhsT=KhT[:, a, :], rhs=arbd)], reads=[B_KhT, B_ARbd], writes=[pbB[3]])
                        rm = rmask[:].rearrange("p a t -> p (a t)")
                        S.do("vector", lambda e, a=a, rm=rm, x1b=x1b: e.tensor_tensor(out=XS1[a][:], in0=pb[x1b][:, :], in1=rm, op=ALU.mult), reads=[pbB[x1b], Bc], writes=[B_XS1[a]])
                        S.do("vector", lambda e, a=a, rm=rm: e.tensor_tensor(out=XS2[a][:], in0=pb[3][:, :], in1=rm, op=ALU.mult), reads=[pbB[3], Bc], writes=[B_XS2[a]])
                        pbb = pb[6][:].bitcast(BF16)
                        S.mm([("T", pbb[:, hp * 128:(hp + 1) * 128], XS1[a][:, hp * 256:hp * 256 + 128], identb[:]) for hp in range(2)],
                             reads=[B_XS1[a], B_identb], writes=[pbB[6]])
                        S.do("scalar", lambda e, a=a, pbb=pbb: e.copy(out=L0[a][:].rearrange("p h t -> p (h t)"), in_=pbb[:, 0:256]), reads=[pbB[6]], writes=[B_L0[a]])
                        bw = (4, 5, 0, 1)[a]
                        specs = [dict(out=pb[bw][:, 0:128], lhsT=AR[:, a, 0, :], rhs=M0bd[:, a, :, :].rearrange("p h i -> p (h i)"), start=True, stop=False)]
                        for hp in range(2):
                            specs.append(dict(out=pb[bw][:, hp * 64:(hp + 1) * 64], lhsT=XS2[a][:, hp * 256:hp * 256 + 128],
                                              rhs=Vtok[:, a * 128 + hp * 64:a * 128 + (hp + 1) * 64], start=False, stop=(hp == 1)))
                        S.mm(specs, reads=[B_AR, B_M0bd, B_XS2[a], B_Vtok], writes=[pbB[bw]])
                        S.do("scalar", lambda e, a=a, bw=bw: e.copy(out=Ut[a][0][:], in_=pb[bw][:, 0:128]), reads=[pbB[bw]], writes=[B_U[a][0]])
                        yield
                    for k in range(7):
                        for a in range(4):
                            q0, q1 = k % 2, (k + 1) % 2
                            if k == 0:
                                Nk = [XS1[a][:, hp * 256:hp * 256 + 128] for hp in range(2)]
                                Lk = [L0[a][:, hp, :] for hp in range(2)]
                                rdN, rdL = [B_XS1[a]], [B_L0[a]]
                            else:
                                Nk = [LN[a][q0][:, hp * 256 + 128:hp * 256 + 256] for hp in range(2)]
                                Lk = [LN[a][q0][:, hp * 256:hp * 256 + 128] for hp in range(2)]
                                rdN, rdL = [B_LN[a][q0]], [B_LN[a][q0]]
                            bu = (4, 5, 0, 1)[a]
                            specs = [dict(out=pb[bu][:, 0:128], lhsT=identb[:], rhs=Ut[a][q0][:], start=True, stop=False)]
                            for hp in range(2):
                                specs.append(dict(out=pb[bu][:, hp * 64:(hp + 1) * 64], lhsT=Nk[hp], rhs=Ut[a][q0][:, hp * 64:(hp + 1) * 64], start=False, stop=(hp == 1)))
                            S.mm(specs, reads=[B_identb, B_U[a][q0]] + rdN, writes=[pbB[bu]])
                            S.do("scalar", lambda e, a=a, q1=q1, bu=bu: e.copy(out=Ut[a][q1][:], in_=pb[bu][:, 0:128]), reads=[pbB[bu]], writes=[B_U[a][q1]])
                            if k < 6:
                                bq = (2, 3, 6)[(k * 4 + a) % 3]
                                specs = []
                                for hp in range(2):
                                    specs.append(dict(out=pb[bq][:, hp * 256:hp * 256 + 128], lhsT=Nk[hp], rhs=Lk[hp]))
                                    specs.append(dict(out=pb[bq][:, hp * 256 + 128:hp * 256 + 256], lhsT=Lk[hp], rhs=Nk[hp]))
                                S.mm(specs, reads=rdN + rdL, writes=[pbB[bq]])
                                if a % 2 == 0:
                                    S.do("vector", lambda e, a=a, q1=q1, bq=bq: e.tensor_copy(out=LN[a][q1][:], in_=pb[bq][:, :]), reads=[pbB[bq]], writes=[B_LN[a][q1]])
                                else:
                                    S.do("scalar", lambda e, a=a, q1=q1, bq=bq: e.copy(out=LN[a][q1][:], in_=pb[bq][:, :]), reads=[pbB[bq]], writes=[B_LN[a][q1]])
                            if a % 2 == 1:
                                yield
                    for a in range(4):
                        Uf, B_Uf = Ut[a][1], B_U[a][1]
                        specs = [dict(out=pb[7][:, a * 128:(a + 1) * 128], lhsT=AR[:, a, 1, :], rhs=M0bd[:, a, :, :].rearrange("p h i -> p (h i)"), start=True, stop=False)]
                        for hp in range(2):
                            oc = pb[7][:, a * 128 + hp * 64:a * 128 + (hp + 1) * 64]
                            specs.append(dict(out=oc, lhsT=XS1[a][:, hp * 256 + 128:hp * 256 + 256], rhs=Uf[:, hp * 64:(hp + 1) * 64], start=False, stop=False))
                            specs.append(dict(out=oc, lhsT=XS2[a][:, hp * 256 + 128:hp * 256 + 256], rhs=Vtok[:, a * 128 + hp * 64:a * 128 + (hp + 1) * 64], start=False, stop=(hp == 1)))
                        S.mm(specs, reads=[B_AR, B_M0bd, B_XS1[a], B_XS2[a], B_Uf, B_Vtok], writes=[pbB[7]])
                        bs = (4, 5, 0, 1)[a]
                        S.mm([dict(out=pb[bs][:, 0:128], lhsT=Btok[:, a * 128:(a + 1) * 128], rhs=Uf[:], start=True, stop=False),
                              dict(out=pb[bs][:, 0:128], lhsT=Ktok[:, a * 128:(a + 1) * 128], rhs=Vtok[:, a * 128:(a + 1) * 128], start=False, stop=True)],
                             reads=[B_Btok, B_Ktok, B_Vtok, B_Uf], writes=[pbB[bs]])
                        for hp in range(2):
                            rows = slice(hp * 64, (hp + 1) * 64)
                            S.do("vector", lambda e, a=a, hp=hp, rows=rows, bs=bs: e.scalar_tensor_tensor(out=Mst[rows, a, :], in0=pb[bs][rows, hp * 64:(hp + 1) * 64],
                                                                                                      scalar=gamx[rows, a, 128:129], in1=Mg[rows, a, :],
                                                                                                      op0=ALU.mult, op1=ALU.add),
                                 reads=[pbB[bs], B_gam, B_Mg], writes=[B_M])
                    S.do("scalar", lambda e: e.copy(out=M0bd[0:64, :, 0, :], in_=Mst[0:64, :, :]), reads=[B_M], writes=[B_M0bd])
                    S.do("scalar", lambda e: e.copy(out=M0bd[64:128, :, 1, :], in_=Mst[64:128, :, :]), reads=[B_M], writes=[B_M0bd])
                    if c == NB - 1:
                        S.mm([("T", pb[6][0:64, a * 128:(a + 1) * 128], Mst[:, a, :], ident[:]) for a in range(4)], reads=[B_M, B_ident], writes=[pbB[6]])
                        S.do("vector", lambda e: e.tensor_copy(out=ysq[0:64, :], in_=pb[6][0:64, :]), reads=[pbB[6]], writes=[B_ysq])
                        S.dma("sync", wkv_p_d[8 * hh:8 * hh + 8].rearrange("h i j -> i h j"), ysq[0:64, :].rearrange("p (h j) -> p h j", h=8), reads=[B_ysq])
                else:
                    for b in range(NS):
                        Snat = ysq
                        S.dma("sync", Snat[0:64, :].rearrange("p (h j) -> p h j", h=8), swkv_d[b, 8 * hh:8 * hh + 8].rearrange("h i j -> i h j"), writes=[B_ysq])
                        S.mm([("T", pb[6][:, a * 64:(a + 1) * 64], Snat[0:64, a * 128:(a + 1) * 128], ident[0:64, 0:64]) for a in range(4)],
                             reads=[B_ysq, B_ident], writes=[pbB[6]])
                        S.do("vector", lambda e: e.tensor_copy(out=Mst[:].rearrange("p a i -> p (a i)"), in_=pb[6][:, 0:256]), reads=[pbB[6]], writes=[B_M])
                        for a in range(4):
                            S.do("vector", lambda e, a=a, b=b: e.tensor_scalar(out=tt[:, 0:128], in0=bones[:], scalar1=S1[:, a, b:b + 1], scalar2=-1.0, op0=ALU.mult, op1=ALU.mult),
                                 reads=[B_S1, Bc], writes=[B_tt])
                            S.do("vector", lambda e, a=a, b=b: e.tensor_scalar(out=tt[:, 128:192], in0=i64x2[:], scalar1=vT[:, a, b:b + 1], scalar2=None, op0=ALU.mult),
                                 reads=[B_v, Bc], writes=[B_tt])
                            bu = 4 + a % 2
                            S.mm([dict(out=pb[bu][:, 0:64], lhsT=tt[:, 0:128], rhs=Mst[:, a, :]),
                                  dict(out=pb[bu][:, 64:128], lhsT=bones[:], rhs=tt[:, 128:192])], reads=[B_tt, B_M, Bc], writes=[pbB[bu]])
                            S.do("vector", lambda e, a=a, b=b: e.tensor_scalar(out=Mg[:, a, :], in0=Mst[:, a, :], scalar1=gamx[:, a, 1 + b:2 + b], scalar2=None, op0=ALU.mult),
                                 reads=[B_M, B_gam], writes=[B_Mg])
                            S.do("vector", lambda e, a=a, b=b, bu=bu: e.scalar_tensor_tensor(out=Mg[:, a, :], in0=pb[bu][:, 0:64], scalar=S3[:, a, b:b + 1], in1=Mg[:, a, :],
                                                                                         op0=ALU.mult, op1=ALU.add),
                                 reads=[pbB[bu], B_S3], writes=[B_Mg])
                            S.do("vector", lambda e, a=a, b=b, bu=bu: e.scalar_tensor_tensor(out=Mg[:, a, :], in0=pb[bu][:, 64:128], scalar=S2[:, a, b:b + 1], in1=Mg[:, a, :],
                                                                                         op0=ALU.mult, op1=ALU.add),
                                 reads=[pbB[bu], B_S2], writes=[B_Mg])
                        S.do("vector", lambda e: e.memset(bv[:], 0.0), writes=[B_bv])
                        for hp in range(2):
                            rows = slice(hp * 64, (hp + 1) * 64)
                            S.do("vector", lambda e, b=b, hp=hp, rows=rows: e.tensor_copy(out=bv[rows, :].rearrange("p (a h q) -> p a h q", a=4, h=2)[:, :, hp, b:b + 1],
                                                                                      in_=rT[rows, :, b:b + 1]), reads=[B_r], writes=[B_bv])
                        specs = []
                        for a in range(4):
                            for hp in range(2):
                                lhsT = bv[:, :].rearrange("p (a h q) -> p a h q", a=4, h=2)[:, a, hp, 0:NS]
                                specs.append(dict(out=pb[7][0:NS, a * 128 + hp * 64:a * 128 + (hp + 1) * 64], lhsT=lhsT, rhs=Mg[:, a, :],
                                                  start=(b == 0 and a == 0 and hp == 0), stop=(b == NS - 1 and a == 3 and hp == 1)))
                        S.mm(specs, reads=[B_bv, B_Mg], writes=[pbB[7]])
                        S.mm([("T", pb[6][0:64, a * 128:(a + 1) * 128], Mg[:, a, :], ident[:]) for a in range(4)], reads=[B_Mg, B_ident], writes=[pbB[6]])
                        S.do("vector", lambda e: e.tensor_copy(out=tt[0:64, :], in_=pb[6][0:64, :]), reads=[pbB[6]], writes=[B_tt])
                        S.dma("sync", wkv_s_d[b, 8 * hh:8 * hh + 8].rearrange("h i j -> i h j"), tt[0:64, :].rearrange("p (h j) -> p h j", h=8), reads=[B_tt])
                if DLIM < 7:
                    return
                yield
                y3 = pb[7][0:np_, :].rearrange("p (h i) -> p h i", h=8)

                def b8(c0_):
                    return stt[0:np_, c0_:c0_ + 8].unsqueeze(2).broadcast_to([np_, 8, 64])

                def t3(tile_):
                    return tile_[0:np_, :].rearrange("p (h i) -> p h i", h=8)
                S.do("vector", lambda e: e.tensor_reduce(out=stt[0:np_, 0:8], in_=y3, axis=AX.X, op=ALU.add), reads=[pbB[7]], writes=[B_st])
                S.do("scalar", lambda e: e.activation(out=ysq[0:np_, :], in_=pb[7][0:np_, :], func=AF.Square), reads=[pbB[7], B_st], writes=[B_ysq])
                S.do("vector", lambda e: e.tensor_reduce(out=stt[0:np_, 8:16], in_=t3(ysq), axis=AX.X, op=ALU.add), reads=[B_ysq], writes=[B_st])
                S.do("vector", lambda e: e.tensor_scalar(out=stt[0:np_, 16:24], in0=stt[0:np_, 0:8], scalar1=1.0 / 64, scalar2=None, op0=ALU.mult), reads=[B_st], writes=[B_st])
                S.do("vector", lambda e: e.tensor_tensor(out=stt[0:np_, 24:32], in0=stt[0:np_, 16:24], in1=stt[0:np_, 16:24], op=ALU.mult), reads=[B_st], writes=[B_st])
                S.do("vector", lambda e: e.scalar_tensor_tensor(out=stt[0:np_, 32:40], in0=stt[0:np_, 8:16], scalar=1.0 / 64, in1=stt[0:np_, 24:32], op0=ALU.mult, op1=ALU.subtract),
                     reads=[B_st], writes=[B_st])
                S.do("vector", lambda e: e.tensor_scalar(out=stt[0:np_, 32:40], in0=stt[0:np_, 32:40], scalar1=64e-5, scalar2=None, op0=ALU.add), reads=[B_st], writes=[B_st])
                S.do("scalar", lambda e: e.activation(out=stt[0:np_, 32:40], in_=stt[0:np_, 32:40], func=AF.Sqrt), reads=[B_st], writes=[B_st])
                S.do("vector", lambda e: e.reciprocal(out=stt[0:np_, 40:48], in_=stt[0:np_, 32:40]), reads=[B_st], writes=[B_st])
                S.do("vector", lambda e: e.tensor_tensor(out=t3(tt), in0=y3, in1=b8(16), op=ALU.subtract), reads=[pbB[7], B_st], writes=[B_tt])
                S.do("vector", lambda e: e.tensor_tensor(out=t3(tt), in0=t3(tt), in1=b8(40), op=ALU.mult), reads=[B_tt, B_st], writes=[B_tt])
                S.do("vector", lambda e: e.tensor_tensor(out=tt[0:np_, :], in0=tt[0:np_, :], in1=gnw[0:np_, :], op=ALU.mult), reads=[B_tt, Bc], writes=[B_tt])
                S.do("vector", lambda e: e.tensor_tensor(out=tt[0:np_, :], in0=tt[0:np_, :], in1=gnb[0:np_, :], op=ALU.add), reads=[B_tt, Bc], writes=[B_tt])
                S.do("vector", lambda e: e.tensor_tensor(out=t3(bv), in0=t3(Vtok), in1=bon[0:np_, :].unsqueeze(2).broadcast_to([np_, 8, 64]), op=ALU.mult),
                     reads=[B_Vtok, B_bon], writes=[B_bv])
                S.do("vector", lambda e: e.tensor_tensor(out=tt[0:np_, :], in0=tt[0:np_, :], in1=bv[0:np_, :], op=ALU.add), reads=[B_tt, B_bv], writes=[B_tt])
                S.do("vector", lambda e: e.tensor_tensor(out=ob[0:np_, :], in0=tt[0:np_, :], in1=gate_sb[0:np_, :], op=ALU.mult), reads=[B_tt, B_gate], writes=[B_ob])
                pbb = pb[6][:].bitcast(BF16)
                S.mm([("T", pbb[:, a * 128:a * 128 + np_], ob[0:np_, a * 128:(a + 1) * 128], identb[0:np_, 0:np_]) for a in range(4)], reads=[B_ob, B_identb], writes=[pbB[6]])
                col0 = T if isX else c * 128
                S.do("scalar", lambda e, pbb=pbb, col0=col0, np_=np_: e.copy(out=o_bT[:, a0g:a0g + 4, col0:col0 + np_],
                                                                          in_=pbb[:, 0:512].rearrange("p (a t) -> p a t", a=4)[:, :, 0:np_]),
                     reads=[pbB[6]], writes=[B_obT])
                if "rw" in debug and hh == 0 and c in (0, 1, NB):
                    for nm, tl, Bt in (("rT", rT, B_r), ("kap", S1, B_S1), ("k2", S2, B_S2), ("aT", aT, B_a), ("sg", sgT, B_sg)):
                        o_ = dbg_out(f"{nm}_{c}", [128, 4, 128])
                        S.dma("sync", o_, tl[:], reads=[Bt])
                    o_ = dbg_out(f"tt_{c}", [128, 512])
                    S.dma("sync", o_, tt[:], reads=[B_tt])
                    o_ = dbg_out(f"gate_{c}", [128, 512])
                    S.dma("sync", o_, gate_sb[:], reads=[B_gate])
            def interleave(g1, g2):
                gens = [g for g in (g1, g2) if g is not None]
                while gens:
                    for g in list(gens):
                        try:
                            next(g)
                        except StopIteration:
                            gens.remove(g)

            for c in range(NB + 1):
                interleave(rwkv_block(c, "A"), None)
                interleave(rwkv_block(c, "B"), None)
            S.barrier()

        phW = st.enter_context(ExitStack())
        Wsh = sb(phW, "Wsh", [128, 8 * 1824], BF16)
        Ws_g = Wsh[:, :].rearrange("p (k c) -> p k c", k=8)
        Wpb_v = Wsh[:, 0:8192].rearrange("p (k c) -> p k c", k=8)
        Wpa_v = Wsh[:, 8192:12288].rearrange("p (k c) -> p k c", k=4)
        B_Ws_g = Buf()

        def load_Wp():
            S.dma("gpsimd", Wpb_v, w_pb_d.rearrange("(kc p) c -> p kc c", p=128), writes=[B_Ws_g])
            S.dma("gpsimd", Wpa_v, w_pa_d.rearrange("(kc p) c -> p kc c", p=128), writes=[B_Ws_g])

        def load_Ws(hh_):
            for j, base in enumerate((0, 1024, 2048)):
                S.dma("gpsimd", Ws_g[:, :, j * 512:(j + 1) * 512], w_in_v[:, :, A_COLS + base + hh_ * 512:A_COLS + base + hh_ * 512 + 512], writes=[B_Ws_g])
            if hh_ == 0:
                S.dma("gpsimd", Ws_g[:, :, 1536:1824], w_in_v[:, :, A_COLS + 3072:A_COLS + 3360], writes=[B_Ws_g])
        for hh in range(2):
            if "stopC" in debug or PH_STOP == "B":
                break
            with ExitStack() as ph:
                rwkv_pass(hh, ph)


        def wload(dst_tile, src_ap, B_):
            S.dma("gpsimd", dst_tile, src_ap, writes=[B_])

        try:
          if "stopD" not in debug and PH_STOP not in ("B", "D"):
            S.barrier()
            phWo = ExitStack()
            Wout = sb(phWo, "Wout", [128, 8, 1024], BF16)
            B_Wout = Buf()
            with ExitStack() as ph:
                Wg = sb(ph, "Wg", [128, 8, 2048], BF16)
                Wpa, Wpb = Wpa_v, Wpb_v
                B_W = Buf()
                for q in range(2):
                    wload(Wg[:, :, q * 1024:(q + 1) * 1024], w_in_v[:, :, A_COLS + SH + q * 1024:A_COLS + SH + (q + 1) * 1024], B_W)
                wload(Wout[:], w_out_d.rearrange("(kc p) c -> p kc c", p=128), B_Wout)
                oaT2 = sb(ph, "oaT", [128, 4, T + 128], BF16)
                B_oaT2 = Buf("oaT2")
                S.dma("sync", oaT2[:], oaT_d, writes=[B_oaT2])
                mtmp = sb(ph, "mtmp", [128, 8, 512], BF16)
                B_mtmp = Buf()
                sga = [sb(ph, f"sga{i}", [128, 512], F32) for i in range(2)]
                sgb = [sb(ph, f"sgb{i}", [128, 512], F32) for i in range(2)]
                B_sga, B_sgb = [Buf(), Buf()], [Buf(), Buf()]
                B_ob_ch = [Buf() for _ in range(5)]
                for ch in range(5):
                    if E1LIM < 1 or (E1LIM < 3 and ch >= 1):
                        continue
                    n_ = 512 if ch < 4 else 128
                    hc0 = C0 + ch * 512 if ch < 4 else SC0
                    oc0 = ch * 512 if ch < 4 else T
                    for m in range(8):
                        u = (ch * 8 + m) % 2
                        bks = [4 * u + j for j in range(4)]
                        S.mm([dict(out=pb[bks[0]][:, 0:n_], lhsT=Wg[:, kc, m * 128:(m + 1) * 128], rhs=hT[:, kc, hc0:hc0 + n_], start=(kc == 0), stop=(kc == 7)) for kc in range(8)],
                             reads=[B_W, B_hT], writes=[pbB[bks[0]]])
                        S.mm([dict(out=pb[bks[1]][:, 0:n_], lhsT=Wg[:, kc, 1024 + m * 128:1024 + (m + 1) * 128], rhs=hT[:, kc, hc0:hc0 + n_], start=(kc == 0), stop=(kc == 7)) for kc in range(8)],
                             reads=[B_W, B_hT], writes=[pbB[bks[1]]])
                        S.mm([dict(out=pb[bks[2]][:, 0:n_], lhsT=Wpa[:, kc, m * 128:(m + 1) * 128], rhs=oaT2[:, kc, oc0:oc0 + n_], start=(kc == 0), stop=(kc == 3)) for kc in range(4)],
                             reads=[B_Ws_g, B_oaT2], writes=[pbB[bks[2]]])
                        S.mm([dict(out=pb[bks[3]][:, 0:n_], lhsT=Wpb[:, kc, m * 128:(m + 1) * 128], rhs=o_bT[:, kc, oc0:oc0 + n_], start=(kc == 0), stop=(kc == 7)) for kc in range(8)],
                             reads=[B_Ws_g, B_ob_ch[ch]], writes=[pbB[bks[3]]])
                        if E1LIM < 2:
                            continue
                        S.do("scalar", lambda e, u=u, b_=bks[0], n_=n_: e.activation(out=sga[u][:, 0:n_], in_=pb[b_][:, 0:n_], func=AF.Sigmoid), reads=[pbB[bks[0]]], writes=[B_sga[u]])
                        S.do("scalar", lambda e, u=u, b_=bks[1], n_=n_: e.activation(out=sgb[u][:, 0:n_], in_=pb[b_][:, 0:n_], func=AF.Sigmoid), reads=[pbB[bks[1]]], writes=[B_sgb[u]])
                        S.do("vector", lambda e, u=u, b_=bks[2], n_=n_: e.tensor_tensor(out=sga[u][:, 0:n_], in0=sga[u][:, 0:n_], in1=pb[b_][:, 0:n_], op=ALU.mult),
                             reads=[pbB[bks[2]], B_sga[u]], writes=[B_sga[u]])
                        S.do("vector", lambda e, u=u, b_=bks[3], n_=n_: e.tensor_tensor(out=sgb[u][:, 0:n_], in0=sgb[u][:, 0:n_], in1=pb[b_][:, 0:n_], op=ALU.mult),
                             reads=[pbB[bks[3]], B_sgb[u]], writes=[B_sgb[u]])
                        S.do("vector", lambda e, u=u, m=m, n_=n_: e.tensor_tensor(out=mtmp[:, m, 0:n_], in0=sga[u][:, 0:n_], in1=sgb[u][:, 0:n_], op=ALU.add),
                             reads=[B_sga[u], B_sgb[u]], writes=[B_mtmp])
                    S.do("scalar", lambda e, oc0=oc0, n_=n_: e.copy(out=o_bT[:, :, oc0:oc0 + n_], in_=mtmp[:, :, 0:n_]), reads=[B_mtmp], writes=[B_ob_ch[ch]])
                S.barrier()
            if ELIM < 2:
                raise StopIteration
            with ExitStack() as ph:
                B_W = B_Wout

                def prod_e2(i, xt, B_xt):
                    oc0 = i * 128 if i < NB else T
                    for hf in range(2):
                        bk = 2 * (i % 2) + hf
                        S.mm([dict(out=pb[bk][:, :], lhsT=o_bT[:, kc, oc0:oc0 + 128], rhs=Wout[:, kc, hf * 512:(hf + 1) * 512], start=(kc == 0), stop=(kc == 7)) for kc in range(8)],
                             reads=[B_obT, B_W], writes=[pbB[bk]])
                        S.do("vector", lambda e, bk=bk, hf=hf, xt=xt: e.tensor_tensor(out=xt[:, hf * 512:(hf + 1) * 512], in0=xt[:, hf * 512:(hf + 1) * 512], in1=pb[bk][:, :], op=ALU.add),
                             reads=[pbB[bk]], writes=[B_xt])
                    S.dma("sync", x1_d[oc0:oc0 + 128, :], xt[:], reads=[B_xt])
                blocks = [(x_d[i * 128:(i + 1) * 128, :], 128, C0 + i * 128) for i in range(NB)]
                blocks.append((xs_d, NS, SC0))
                rmsnorm_to_T(ph, blocks, norm_ffn_d, hT, B_hT, "E2", producer=prod_e2)
                S.barrier()
            phWo.close()
            phW.close()
            if ELIM < 3:
                raise StopIteration
            with ExitStack() as ph:
                actT2 = sb(ph, "actT2", [128, 14, T + 128], BF16)
                B_act = Buf()

                def act_tile(f):
                    if f < 8:
                        return o_bT[:, f, :]
                    return actT2[:, f - 8, :]
                Wd = sb(ph, "Wd", [128, 22, 1024], BF16)
                B_Wd = Buf()
                wd_v = w_down_d.rearrange("(f p) c -> p f c", p=128)
                with ExitStack() as ph3:
                    WG = [sb(ph3, f"WG{i}", [128, 8, 128], BF16) for i in range(2)]
                    WU = [sb(ph3, f"WU{i}", [128, 8, 128], BF16) for i in range(2)]
                    B_WG, B_WU = [Buf(), Buf()], [Buf(), Buf()]
                    sgl = [sb(ph3, f"sgl{i}", [128, 512], F32) for i in range(2)]
                    B_sgl = [Buf(), Buf()]
                    wg_v = w_gate_d.rearrange("(kc p) c -> p kc c", p=128)
                    wu_v = w_up_d.rearrange("(kc p) c -> p kc c", p=128)
                    for f in range(22):
                        s_ = f % 2
                        wload(WG[s_][:], wg_v[:, :, f * 128:(f + 1) * 128], B_WG[s_])
                        wload(WU[s_][:], wu_v[:, :, f * 128:(f + 1) * 128], B_WU[s_])
                        if 2 <= f < 13:
                            q = f - 2
                            wload(Wd[:, 2 * q:2 * q + 2, :], wd_v[:, 2 * q:2 * q + 2, :], B_Wd)
                        for ch in range(5):
                            n_ = 512 if ch < 4 else 128
                            hc0 = C0 + ch * 512 if ch < 4 else SC0
                            oc0 = ch * 512 if ch < 4 else T
                            u = (f * 5 + ch) % 2
                            bg, bu_ = 2 * u, 2 * u + 1
                            S.mm([dict(out=pb[bg][:, 0:n_], lhsT=WG[s_][:, kc, :], rhs=hT[:, kc, hc0:hc0 + n_], start=(kc == 0), stop=(kc == 7)) for kc in range(8)],
                                 reads=[B_WG[s_], B_hT], writes=[pbB[bg]])
                            S.mm([dict(out=pb[bu_][:, 0:n_], lhsT=WU[s_][:, kc, :], rhs=hT[:, kc, hc0:hc0 + n_], start=(kc == 0), stop=(kc == 7)) for kc in range(8)],
                                 reads=[B_WU[s_], B_hT], writes=[pbB[bu_]])
                            S.do("scalar", lambda e, u=u, bg=bg, n_=n_: e.activation(out=sgl[u][:, 0:n_], in_=pb[bg][:, 0:n_], func=AF.Silu), reads=[pbB[bg]], writes=[B_sgl[u]])
                            S.do("vector", lambda e, u=u, bu_=bu_, n_=n_, f=f, oc0=oc0: e.tensor_tensor(out=act_tile(f)[:, oc0:oc0 + n_], in0=sgl[u][:, 0:n_], in1=pb[bu_][:, 0:n_], op=ALU.mult),
                                 reads=[B_sgl[u], pbB[bu_]], writes=[B_act])
                    S.barrier()
                if ELIM < 4:
                    raise StopIteration
                B_W = B_Wd

                def prod_e4(i, xt, B_xt):
                    oc0 = i * 128 if i < NB else T
                    for hf in range(2):
                        bk = 2 * (i % 2) + hf
                        S.mm([dict(out=pb[bk][:, :], lhsT=act_tile(f)[:, oc0:oc0 + 128], rhs=Wd[:, f, hf * 512:(hf + 1) * 512], start=(f == 0), stop=(f == 21)) for f in range(22)],
                             reads=[B_act, B_W], writes=[pbB[bk]])
                        S.do("vector", lambda e, bk=bk, hf=hf, xt=xt: e.tensor_tensor(out=xt[:, hf * 512:(hf + 1) * 512], in0=xt[:, hf * 512:(hf + 1) * 512], in1=pb[bk][:, :], op=ALU.add),
                             reads=[pbB[bk]], writes=[B_xt])
                blocks = [(x1_d[i * 128:(i + 1) * 128, :], 128, 0) for i in range(NB)]
                blocks.append((x1_d[T:T + 128, :], 128, 0))
                outs_ = [(y_d[i * 128:(i + 1) * 128, :], 128) for i in range(NB)] + [(ys_d, NS)]
                rmsnorm_to_T(ph, blocks, norm_final_d, None, None, "E4", producer=prod_e4, out_rows=outs_)
                S.barrier()
        except StopIteration:
            S.barrier()
        if "obT" in debug:
            o_ = dbg_out("obT", [128, 8, T + 128])
            with ExitStack() as phd:
                tmpd = sb(phd, "dbgtmp2", [128, 8, T + 128], F32)
                Bt = Buf()
                S.do("vector", lambda e: e.tensor_copy(out=tmpd[:], in_=o_bT[:]), reads=[B_obT], writes=[Bt])
                S.dma("sync", o_, tmpd[:], reads=[Bt])
                S.barrier()
        S.final_wait()
        S.emit()
    print("instruction counts:", S.ninst, "sem incs:", S.cnt, flush=True)
    return nc, list(dbg.keys())


_CACHE = {}


def _get_nc(debug=()):
    key = tuple(debug)
    if key not in _CACHE:
        _CACHE[key] = build(debug)
    return _CACHE[key]


def kernel(_debug=(), **inputs):
    nc, dbg_names = _get_nc(_debug)
    consts = make_consts()
    f32 = lambda a: np.ascontiguousarray(np.asarray(a, dtype=np.float32))
    in_maps = []
    for c in range(8):
        m = dict(consts)
        m["x"] = f32(inputs["x_prompt"][c])
        m["xs"] = f32(inputs["x_sample"][4 * c:4 * c + 4, 0, :])
        m["w_in"] = f32(inputs["w_in"][0])
        m["norm_mix"] = f32(inputs["norm_mix"])
        m["w_proj_a"] = f32(inputs["w_proj_a"][0])
        m["w_proj_b"] = f32(inputs["w_proj_b"][0])
        m["w_out"] = f32(inputs["w_out"][0])
        m["norm_ffn"] = f32(inputs["norm_ffn"])
        m["w_gate"] = f32(inputs["w_gate"][0])
        m["w_up"] = f32(inputs["w_up"][0])
        m["w_down"] = f32(inputs["w_down"][0])
        m["norm_final"] = f32(inputs["norm_final"]).reshape(1, D)
        m["sshift"] = f32(inputs["state_shift"][0, 4 * c:4 * c + 4])
        m["swkv"] = f32(inputs["state_wkv"][0, 4 * c:4 * c + 4])
        mu = np.zeros(27 * 128, np.float32)
        mu[:SH] = np.asarray(inputs["mu_shift"], np.float32)[0]
        m["mu_c"] = np.ascontiguousarray(mu.reshape(27, 128).T)
        m["cols8"] = np.ascontiguousarray(np.stack([np.asarray(inputs[k_], np.float32).reshape(8, 128).T for k_ in ("w0", "a0", "k_k", "k_a", "r_k")], 1))
        m["w2a2"] = np.ascontiguousarray(np.concatenate([f32(inputs["w2"][0]), f32(inputs["a2"][0])], 0))
        m["g2"] = f32(inputs["g2"][0])
        m["gn_w"] = f32(inputs["gn_w"])
        m["gn_b"] = f32(inputs["gn_b"])
        for g in range(3):
            ck = inputs[f"cache_kv_g{g + 1}"][0, 4 * c:4 * c + 4]
            m[f"ck{g + 1}"] = f32(ck).reshape(NS, ck.shape[1], 1024)
        in_maps.append(m)
    res = run_bass_kernel_spmd(nc, in_maps, core_ids=list(range(8)))
    R = res.results
    if _debug:
        return R
    def cat(name, shape=None):
        a = np.concatenate([np.asarray(R[c][name]) for c in range(8)], 0)
        return a
    y_prompt = np.stack([np.asarray(R[c]["y"]) for c in range(8)], 0).astype(np.float32)
    y_sample = cat("ys").reshape(32, 1, D).astype(np.float32)
    kvp = [np.stack([np.asarray(R[c][f"kv{g + 1}_p"]) for c in range(8)], 0).reshape(1, 8, -1, 2, 8, 64).astype(np.float32) for g in range(3)]
    shp = cat("shift_p").reshape(1, 8, SH).astype(np.float32)
    wkvp = np.stack([np.asarray(R[c]["wkv_p"]) for c in range(8)], 0).reshape(1, 8, 16, 64, 64).astype(np.float32)
    kvs = [cat(f"kv{g + 1}_s").reshape(1, 32, 1, 2, 8, 64).astype(np.float32) for g in range(3)]
    shs = cat("shift_s").reshape(1, 32, SH).astype(np.float32)
    wkvs = cat("wkv_s").reshape(1, 32, 16, 64, 64).astype(np.float32)
    return (y_prompt, y_sample, kvp[0], kvp[1], kvp[2], shp, wkvp, kvs[0], kvs[1], kvs[2], shs, wkvs)
```
